# Optimizing a Trainium2 kernel written in Bass

```python
import jax, jax.numpy as jnp
from jax import lax
import numpy as np

D_MODEL = 1024
BATCH = 16
SEQ = 256
DEPTH = 1
DEC_BATCH = 2
DEC_SEQ = 1024
PAST_LEN = 256

GRID_W = 64
HEAD_DIM = 64
RW_HEADS = 8
RW_WIDTH = RW_HEADS * HEAD_DIM
NA_HEADS = 8
NA_WIDTH = NA_HEADS * HEAD_DIM
LORA_DECAY = 64
LORA_ICLR = 64
LORA_GATE = 128
NA_ROWS = 8
NA_COLS = 16
NA_QCOLS = 16
NA_KCOLS = 32
Q_BLOCK = 128
FF_HIDDEN = -(-8 * D_MODEL // (3 * 256)) * 256
RW_COLS = 3 * RW_WIDTH + 2 * LORA_DECAY + 2 * LORA_ICLR + LORA_GATE
NA_IN_COLS = 3 * NA_WIDTH
IN_COLS = RW_COLS + NA_IN_COLS + 2 * D_MODEL
RW_SPLITS = (RW_WIDTH, 2 * RW_WIDTH, 3 * RW_WIDTH,
             3 * RW_WIDTH + 2 * LORA_DECAY,
             3 * RW_WIDTH + 2 * LORA_DECAY + 2 * LORA_ICLR)
IN_SPLITS = (RW_COLS, RW_COLS + NA_IN_COLS, RW_COLS + NA_IN_COLS + D_MODEL)
RMS_EPS = 1e-6
GN_EPS = 64e-5
L2_EPS = 1e-12
NEG_INF = -1e30

kernel_name = "hybrid_rwkv7_natten_prefix_step"


def _rmsnorm(x, g):
    xf = x.astype(jnp.float32)
    y = xf * lax.rsqrt(jnp.mean(xf * xf, axis=-1, keepdims=True) + RMS_EPS)
    return (y * g.astype(jnp.float32)).astype(x.dtype)


def _heads(t):
    return t.reshape(t.shape[:-1] + (t.shape[-1] // HEAD_DIM, HEAD_DIM))


def _centred_shift(u, mu):
    prev = jnp.pad(u[:, :-1], ((0, 0), (1, 0), (0, 0)))
    nxt = jnp.pad(u[:, 1:], ((0, 0), (0, 1), (0, 0)))
    return u + mu[0] * (prev - u) + mu[1] * (nxt - u)


def _wkv_step(s, inp):
    r, w, k, v, a, b = inp
    sa = jnp.einsum("bhvk,bhk->bhv", s, a)
    s = s * w[:, :, None, :] + sa[..., None] * b[:, :, None, :] + v[..., None] * k[:, :, None, :]
    return s, jnp.einsum("bhvk,bhk->bhv", s, r)


def _wkv_scan(s0, r, w, k, v, a, b, reverse):
    xs = tuple(jnp.moveaxis(t.astype(jnp.float32), 1, 0) for t in (r, w, k, v, a, b))
    s, ys = lax.scan(_wkv_step, s0.astype(jnp.float32), xs, reverse=reverse)
    return s, jnp.moveaxis(ys, 0, 1)


def _rwkv_branch(u, s0, mu, w0, w_up, a0, a_up, g_up, k_k, k_a, r_k, ln_g, ln_b):
    u = _centred_shift(u, mu)
    r, k, v, wd, ad, gd = jnp.split(u, RW_SPLITS, axis=-1)
    bsz, t_len = r.shape[:2]
    wd = wd.reshape(bsz, t_len, 2, LORA_DECAY)
    ad = ad.reshape(bsz, t_len, 2, LORA_ICLR)
    w_soft = -jax.nn.softplus(-(w0 + jnp.einsum("btel,elc->btec", jnp.tanh(wd), w_up))) - 0.5
    decay = jnp.exp(-jnp.exp(w_soft.astype(jnp.float32)))
    iclr = jax.nn.sigmoid(a0 + jnp.einsum("btel,elc->btec", ad, a_up))
    gate = jax.nn.sigmoid(gd) @ g_up
    kk = _heads(k * k_k).astype(jnp.float32)
    kk = kk * lax.rsqrt(jnp.sum(kk * kk, axis=-1, keepdims=True) + L2_EPS)
    k_dir = k[:, :, None, :] * (1.0 + (iclr - 1.0) * k_a)
    rh, vh = _heads(r), _heads(v)
    s_f, y_f = _wkv_scan(s0[:, 0], rh, _heads(decay[:, :, 0]), _heads(k_dir[:, :, 0]), vh,
                         -kk, kk * _heads(iclr[:, :, 0]), False)
    s_b, y_b = _wkv_scan(s0[:, 1], rh, _heads(decay[:, :, 1]), _heads(k_dir[:, :, 1]), vh,
                         -kk, kk * _heads(iclr[:, :, 1]), True)
    y = y_f + y_b
    mean = jnp.mean(y, axis=-1, keepdims=True)
    var = jnp.mean(jnp.square(y - mean), axis=-1, keepdims=True)
    y = (y - mean) * lax.rsqrt(var + GN_EPS) * _heads(ln_g).astype(jnp.float32) + _heads(ln_b).astype(jnp.float32)
    k_bonus = _heads(0.5 * (k_dir[:, :, 0] + k_dir[:, :, 1]))
    bonus = jnp.sum(rh * k_bonus * r_k, axis=-1, keepdims=True) * vh
    out = (y.astype(u.dtype) + bonus).reshape(bsz, t_len, RW_WIDTH) * gate
    return out, jnp.stack([s_f, s_b], axis=1).astype(u.dtype)


def _qk_norm(t, g):
    tf = t.astype(jnp.float32)
    y = tf * lax.rsqrt(jnp.mean(tf * tf, axis=-1, keepdims=True) + RMS_EPS)
    return (y * g.astype(jnp.float32)).astype(t.dtype)


def _na_context(q, k, v):
    bsz, c_len, h, d = q.shape
    qb = jnp.moveaxis(q.reshape(bsz, c_len // Q_BLOCK, Q_BLOCK, h, d), 1, 0)

    def block(q_b):
        logits = jnp.einsum("bqhd,bkhd->bhqk", q_b, k).astype(jnp.float32) * (HEAD_DIM ** -0.5)
        p = jax.nn.softmax(logits, axis=-1).astype(v.dtype)
        return jnp.einsum("bhqk,bkhd->bqhd", p, v)

    out = lax.map(block, qb)
    return jnp.moveaxis(out, 0, 1).reshape(bsz, c_len, h * d)


def _na_latent(q, k, v, k_ctx, v_ctx, rpb):
    bsz, t_len, h, d = q.shape
    rows = t_len // GRID_W
    kr = min(NA_ROWS, rows)
    n_cb = GRID_W // NA_QCOLS
    row_start = np.clip(np.arange(rows) - kr // 2, 0, rows - kr)
    dr_idx = row_start[:, None] + np.arange(kr)[None, :] - np.arange(rows)[:, None] + NA_ROWS - 1
    q_col = np.arange(GRID_W).reshape(n_cb, NA_QCOLS)
    band_start = np.clip(q_col[:, 0] - NA_COLS // 2, 0, GRID_W - NA_KCOLS)
    k_col = band_start[:, None] + np.arange(NA_KCOLS)[None, :]
    win_start = np.clip(q_col - NA_COLS // 2, 0, GRID_W - NA_COLS)
    valid = (k_col[:, None, :] >= win_start[..., None]) & (k_col[:, None, :] < win_start[..., None] + NA_COLS)
    dc_idx = np.clip(k_col[:, None, :] - q_col[..., None], -(NA_COLS - 1), NA_COLS - 1) + NA_COLS - 1
    col_bias = rpb[:, :, dc_idx].astype(jnp.float32)
    qg = q.reshape(bsz, rows, n_cb, NA_QCOLS, h, d)
    kg = k.reshape(bsz, rows, GRID_W, h, d)
    vg = v.reshape(bsz, rows, GRID_W, h, d)
    scale = HEAD_DIM ** -0.5
    n_win = kr * NA_KCOLS

    def row_block(args):
        q_r, r0, dr = args
        k_r = lax.dynamic_slice_in_dim(kg, r0, kr, axis=1)[:, :, k_col]
        v_r = lax.dynamic_slice_in_dim(vg, r0, kr, axis=1)[:, :, k_col]
        bias = jnp.transpose(jnp.take(col_bias, dr, axis=1), (0, 2, 3, 1, 4))
        lw = jnp.einsum("bjqhd,bkjchd->bhjqkc", q_r, k_r).astype(jnp.float32) * scale + bias[None]
        lw = jnp.where(valid[:, :, None, :], lw, NEG_INF)
        lc = jnp.einsum("bjqhd,blhd->bhjql", q_r, k_ctx).astype(jnp.float32) * scale
        logits = jnp.concatenate([lw.reshape(bsz, h, n_cb, NA_QCOLS, n_win), lc], axis=-1)
        p = jax.nn.softmax(logits, axis=-1).astype(v.dtype)
        pw = p[..., :n_win].reshape(bsz, h, n_cb, NA_QCOLS, kr, NA_KCOLS)
        pc = p[..., n_win:]
        return (jnp.einsum("bhjqkc,bkjchd->bjqhd", pw, v_r)
                + jnp.einsum("bhjql,blhd->bjqhd", pc, v_ctx))

    out = lax.map(row_block, (jnp.moveaxis(qg, 1, 0), jnp.asarray(row_start, jnp.int32),
                              jnp.asarray(dr_idx, jnp.int32)))
    return jnp.moveaxis(out, 0, 1).reshape(bsz, t_len, h * d)


def _modulation(cvec, w_ada, b_ada):
    m = jax.nn.silu(cvec) @ w_ada + b_ada
    return jnp.split(m[..., None, :], 6, axis=-1)


def _trunk_layer(x, cvec, s0, ctx_k, ctx_v, p):
    sh1, sc1, g1, sh2, sc2, g2 = _modulation(cvec, p["w_ada"], p["b_ada"])
    h = _rmsnorm(x, p["norm1_g"]) * (1.0 + sc1) + sh1
    u_rw, u_na, gate_rw, gate_na = jnp.split(h @ p["w_in"], IN_SPLITS, axis=-1)
    o_rw, s_new = _rwkv_branch(u_rw, s0, p["shift_mu"], p["rw_w0"], p["rw_w_up"], p["rw_a0"],
                               p["rw_a_up"], p["rw_g_up"], p["rw_k_k"], p["rw_k_a"], p["rw_r_k"],
                               p["rw_ln_g"], p["rw_ln_b"])
    q, k, v = jnp.split(u_na, 3, axis=-1)
    q = _qk_norm(_heads(q), p["na_q_g"])
    k = _qk_norm(_heads(k), p["na_k_g"])
    v = _heads(v)
    if ctx_k is None:
        o_na = _na_context(q, k, v)
    else:
        o_na = _na_latent(q, k, v, ctx_k, ctx_v, p["na_rpb"])
    merged = (jax.nn.sigmoid(gate_rw) * (o_rw @ p["w_o_rwkv"])
              + jax.nn.sigmoid(gate_na) * (o_na @ p["w_o_na"]))
    x = x + g1 * (merged @ p["w_out"])
    h2 = _rmsnorm(x, p["norm2_g"]) * (1.0 + sc2) + sh2
    x = x + g2 * ((jax.nn.silu(h2 @ p["ffn_w1"]) * (h2 @ p["ffn_w3"])) @ p["ffn_w2"])
    return x, s_new, k, v


def setup_inputs(seed: int = 0) -> dict:
    key = jax.random.key(seed)
    ks = jax.random.split(key, 40)
    f32 = jnp.float32
    n = lambda i, shape, s: jax.random.normal(ks[i], shape, f32) * s
    return {
        "x_prompt": n(0, (BATCH, SEQ, D_MODEL), 1.0),
        "x_sample": n(1, (DEC_BATCH, DEC_SEQ, D_MODEL), 1.0),
        "state_rwkv": n(2, (DEC_BATCH, DEPTH, 2, RW_HEADS, HEAD_DIM, HEAD_DIM), 0.5),
        "cache_na_k": n(3, (DEC_BATCH, DEPTH, PAST_LEN, NA_HEADS, HEAD_DIM), 1.0),
        "cache_na_v": n(4, (DEC_BATCH, DEPTH, PAST_LEN, NA_HEADS, HEAD_DIM), 1.0),
        "c": n(5, (DEC_BATCH, D_MODEL), 1.0),
        "c_ctx": n(6, (D_MODEL,), 1.0),
        "norm1_g": 1.0 + n(7, (DEPTH, D_MODEL), 0.02),
        "norm2_g": 1.0 + n(8, (DEPTH, D_MODEL), 0.02),
        "w_ada": n(9, (DEPTH, D_MODEL, 6 * D_MODEL), 0.5 * D_MODEL ** -0.5),
        "b_ada": n(10, (DEPTH, 6 * D_MODEL), 0.02),
        "w_in": n(11, (DEPTH, D_MODEL, IN_COLS), D_MODEL ** -0.5),
        "shift_mu": jax.random.uniform(ks[12], (DEPTH, 2, RW_COLS), f32, 0.0, 0.5),
        "rw_w0": -1.0 + n(13, (DEPTH, 2, RW_WIDTH), 0.3),
        "rw_w_up": n(14, (DEPTH, 2, LORA_DECAY, RW_WIDTH), 0.5 * LORA_DECAY ** -0.5),
        "rw_a0": n(15, (DEPTH, 2, RW_WIDTH), 0.1),
        "rw_a_up": n(16, (DEPTH, 2, LORA_ICLR, RW_WIDTH), LORA_ICLR ** -0.5),
        "rw_g_up": n(17, (DEPTH, LORA_GATE, RW_WIDTH), LORA_GATE ** -0.5),
        "rw_k_k": 0.85 + n(18, (DEPTH, RW_WIDTH), 0.02),
        "rw_k_a": 1.0 + n(19, (DEPTH, RW_WIDTH), 0.02),
        "rw_r_k": n(20, (DEPTH, RW_HEADS, HEAD_DIM), 0.1),
        "rw_ln_g": 1.0 + n(21, (DEPTH, RW_WIDTH), 0.02),
        "rw_ln_b": n(22, (DEPTH, RW_WIDTH), 0.02),
        "na_q_g": 1.0 + n(23, (DEPTH, HEAD_DIM), 0.02),
        "na_k_g": 1.0 + n(24, (DEPTH, HEAD_DIM), 0.02),
        "na_rpb": n(25, (DEPTH, NA_HEADS, 2 * NA_ROWS - 1, 2 * NA_COLS - 1), 0.1),
        "w_o_rwkv": n(26, (DEPTH, RW_WIDTH, D_MODEL), RW_WIDTH ** -0.5),
        "w_o_na": n(27, (DEPTH, NA_WIDTH, D_MODEL), NA_WIDTH ** -0.5),
        "w_out": n(28, (DEPTH, D_MODEL, D_MODEL), D_MODEL ** -0.5),
        "ffn_w1": n(29, (DEPTH, D_MODEL, FF_HIDDEN), D_MODEL ** -0.5),
        "ffn_w3": n(30, (DEPTH, D_MODEL, FF_HIDDEN), D_MODEL ** -0.5),
        "ffn_w2": n(31, (DEPTH, FF_HIDDEN, D_MODEL), FF_HIDDEN ** -0.5),
    }


def reference(x_prompt, x_sample, state_rwkv, cache_na_k, cache_na_v, c, c_ctx,
              norm1_g, norm2_g, w_ada, b_ada, w_in, shift_mu, rw_w0, rw_w_up, rw_a0, rw_a_up,
              rw_g_up, rw_k_k, rw_k_a, rw_r_k, rw_ln_g, rw_ln_b, na_q_g, na_k_g, na_rpb,
              w_o_rwkv, w_o_na, w_out, ffn_w1, ffn_w3, ffn_w2):
    y_p, y_s = x_prompt, x_sample
    new_s, new_k, new_v = [], [], []
    for l in range(DEPTH):
        p = dict(norm1_g=norm1_g[l], norm2_g=norm2_g[l], w_ada=w_ada[l], b_ada=b_ada[l],
                 w_in=w_in[l], shift_mu=shift_mu[l], rw_w0=rw_w0[l], rw_w_up=rw_w_up[l],
                 rw_a0=rw_a0[l], rw_a_up=rw_a_up[l], rw_g_up=rw_g_up[l], rw_k_k=rw_k_k[l],
                 rw_k_a=rw_k_a[l], rw_r_k=rw_r_k[l], rw_ln_g=rw_ln_g[l], rw_ln_b=rw_ln_b[l],
                 na_q_g=na_q_g[l], na_k_g=na_k_g[l], na_rpb=na_rpb[l], w_o_rwkv=w_o_rwkv[l],
                 w_o_na=w_o_na[l], w_out=w_out[l], ffn_w1=ffn_w1[l], ffn_w3=ffn_w3[l],
                 ffn_w2=ffn_w2[l])
        s0 = jnp.zeros((y_p.shape[0], 2, RW_HEADS, HEAD_DIM, HEAD_DIM), y_p.dtype)
        y_p, s_l, k_l, v_l = _trunk_layer(y_p, c_ctx, s0, None, None, p)
        new_s.append(s_l)
        new_k.append(k_l)
        new_v.append(v_l)
        y_s, _, _, _ = _trunk_layer(y_s, c, state_rwkv[:, l], cache_na_k[:, l], cache_na_v[:, l], p)
    state_rwkv_new = jnp.stack(new_s, axis=1)
    cache_na_k_new = jnp.stack(new_k, axis=1)
    cache_na_v_new = jnp.stack(new_v, axis=1)
    return (y_prompt_out := y_p, y_s, state_rwkv_new, cache_na_k_new, cache_na_v_new)
```

```python
import numpy as np
from contextlib import ExitStack
import concourse.bass as bass
import concourse.mybir as mybir
from concourse.bass_utils import run_bass_kernel_spmd

F32 = mybir.dt.float32
BF16 = mybir.dt.bfloat16
ALU = mybir.AluOpType
AF = mybir.ActivationFunctionType
AX = mybir.AxisListType

D = 1024
T = 1024
NCH = 16
FF = 2816
NFT = 22
RMS_EPS = 1e-6
GN_EPS = 64e-5
L2_EPS = 1e-12
NEGM = -30000.0
LWC = -0.30326533
NDMA = 24
NSLOT = 4

PV_N1, PV_N2, PV_BADA, PV_MU0, PV_MU1 = 0, 8, 16, 64, 79
PV_KK, PV_KA, PV_RK, PV_W0, PV_A0, PV_QG, PV_KG, NPV = 94, 98, 102, 106, 114, 122, 123, 124
CB_ID, CB_ONE, CB_BO, CB_LT, CB_LE, CB_GT, CB_GE, CB_I64, NCB = 0, 128, 256, 384, 448, 512, 576, 640, 704

def _qwin(R):
    return min(max(R - 2, 0), 3)
QL = [[R for R in range(8) if _qwin(R) <= j < _qwin(R) + 5] for j in range(8)]
BLK = {}
_b = 0
for _j in range(8):
    for _R in QL[_j]:
        BLK[(_j, _R)] = _b
        _b += 1
NBLK = _b


class Sched:
    def __init__(self, nc, stack):
        self.nc = nc
        self.ce = ["pe", "act", "dve", "pool"]
        self.qe = ["sp"]
        self.sem = {e: stack.enter_context(nc.semaphore("s_" + e)) for e in self.ce + self.qe}
        self.cnt = {e: 0 for e in self.ce + self.qe}
        self.dsem = [stack.enter_context(nc.semaphore("d%d" % i)) for i in range(NDMA)]
        self.dcnt = [0] * NDMA
        self.drr = {"sp": 0, "pool": 0}
        self.ops = {e: [] for e in self.ce + self.qe}
        self.res = {}
        self.waited = {e: {} for e in self.ce + self.qe}
        self.nops = 0
        self.pending_dma = []
        self.same = {e: {} for e in self.ce}

    def _semobj(self, sk):
        return self.sem[sk[1]] if sk[0] == "eng" else self.dsem[sk[1]]

    def _key(self, x):
        if isinstance(x, (str, tuple)):
            return x
        return x.tensor.name if hasattr(x, "tensor") else x.name

    def _deps(self, reads, writes):
        raw, other = [], []
        for r in reads:
            st = self.res.get(self._key(r))
            if st and st["w"]:
                raw.append(st["w"])
        for w in writes:
            st = self.res.get(self._key(w))
            if st:
                if st["w"]:
                    other.append(st["w"])
                other.extend(st["r"].values())
        return raw, other

    def _mark(self, reads, writes, ev, rk):
        for r in reads:
            st = self.res.setdefault(self._key(r), {"w": None, "r": {}})
            st["r"][rk] = ev
        for w in writes:
            self.res[self._key(w)] = {"w": ev, "r": {}}

    def _waits(self, eng, deps):
        if isinstance(deps, tuple):
            raw, other = deps
            deps = list(raw) + [d_ for d_ in other if d_[0] != ("eng", eng)]
        best = {}
        for sk, v in deps:
            if sk == ("eng", eng) and eng == "pe":
                continue
            if self.waited[eng].get(sk, 0) >= v:
                continue
            if best.get(sk, 0) < v:
                best[sk] = v
        for sk, v in best.items():
            self.waited[eng][sk] = v
        return list(best.items())

    @staticmethod
    def _box(x):
        if isinstance(x, (str, tuple)) or not hasattr(x, "ap"):
            return None
        try:
            ap = x.ap
            p0 = x.base_partition()
            off = int(x.offset) - p0 * ap[0][0]
            lo = hi = off
            for st_, n_ in ap[1:]:
                ext = st_ * (n_ - 1)
                if ext < 0:
                    lo += ext
                else:
                    hi += ext
            return (p0, p0 + ap[0][1], lo, hi + 1)
        except Exception:
            return None

    @staticmethod
    def _ovl(a, b):
        if a is None or b is None:
            return True
        return a[0] < b[1] and b[0] < a[1] and a[2] < b[3] and b[2] < a[3]

    def _same_deps(self, eng, reads, writes):
        deps = []
        hist = self.same[eng]
        for lst, isw in ((reads, False), (writes, True)):
            for x in lst:
                h = hist.get(self._key(x))
                if not h:
                    continue
                b = self._box(x)
                for b2, cnt2, w2 in h:
                    if (isw or w2) and self._ovl(b, b2):
                        deps.append((("eng", eng), cnt2))
        return deps

    def _same_record(self, eng, reads, writes, cnt):
        hist = self.same[eng]
        for lst, isw in ((reads, False), (writes, True)):
            for x in lst:
                h = hist.setdefault(self._key(x), [])
                h.append((self._box(x), cnt, isw))
                if len(h) > 64:
                    del h[0]

    def _psum_excl(self, r, w):
        r2, w2 = [], list(w)
        for x in r:
            k = self._key(x)
            if isinstance(k, str) and (k.startswith("pf") or k.startswith("pb")):
                w2.append(x)
            else:
                r2.append(x)
        return r2, w2

    def op(self, eng, fn, r=(), w=()):
        r0, w0 = list(r), list(w)
        r, w = self._psum_excl(r, w)
        raw_, oth_ = self._deps(r, w)
        own = ("eng", eng)
        deps = [d_ for d_ in list(raw_) + list(oth_) if d_[0] != own]
        if eng != "pe":
            deps += self._same_deps(eng, r0, w0)
        waits = self._waits(eng, deps)
        self.cnt[eng] += 1
        if eng != "pe":
            self._same_record(eng, r0, w0, self.cnt[eng])
        ev = (("eng", eng), self.cnt[eng])
        self.ops[eng].append((fn, waits, (ev[0], 1)))
        self._mark(r, w, ev, eng)
        self.nops += 1

    def dma(self, q, out, in_, r=None, w=None, track=True):
        r = [in_] if r is None else r
        w = [out] if w is None else w
        half_ = NDMA // 2
        base = 0 if q == "sp" else half_
        i = base + self.drr[q] % half_
        self.drr[q] += 1
        raw_, oth_ = self._deps(r, w)
        deps = list(raw_) + list(oth_)
        if self.dcnt[i] > 0:
            deps.append((("dma", i), self.dcnt[i]))
        waits = self._waits(q, deps)
        self.dcnt[i] += 16
        ev = (("dma", i), self.dcnt[i])

        def fn(e, out=out, in_=in_):
            return e.dma_start(out=out, in_=in_)

        self.ops[q].append((fn, waits, (ev[0], 16)))
        self._mark(r, w, ev, ("dma", i))
        if track:
            self.pending_dma.append(ev)
        self.nops += 1
        return ev

    def barrier(self):
        waits = self._waits("sp", list(self.pending_dma))
        self.pending_dma = []
        self.cnt["sp"] += 1
        spv = self.cnt["sp"]
        self.ops["sp"].append(("seminc", waits, None))
        tgt = [(("eng", e), self.cnt[e]) for e in self.ce if self.cnt[e] > 0] + [(("eng", "sp"), spv)]
        for e in self.ce + self.qe:
            ws = self._waits(e, [t for t in tgt if t[0] != ("eng", e)])
            self.ops[e].append((None, ws, None))

    def flush(self):
        nc = self.nc
        ops = self.ops
        self.ops = {e: [] for e in self.ce + self.qe}
        with nc.Block() as block:
            def emit(engname):
                def body(e):
                    for fn, waits, inc in ops[engname]:
                        for sk, v in waits:
                            e.wait_ge(self._semobj(sk), v)
                        if fn is None:
                            continue
                        if fn == "seminc":
                            e.sem_inc(self.sem["sp"], 1)
                            continue
                        ins = fn(e)
                        if inc is not None:
                            ins.then_inc(self._semobj(inc[0]), inc[1])
                return body
            block.tensor(emit("pe"))
            block.scalar(emit("act"))
            block.vector(emit("dve"))
            block.gpsimd(emit("pool"))
            block.sync(emit("sp"))

    def mm(self, out, lhsT, rhs, start=True, stop=True):
        self.op("pe", lambda e: e.matmul(out, lhsT, rhs, start=start, stop=stop), r=[lhsT, rhs], w=[out])

    def tr(self, out, in_, ident):
        self.op("pe", lambda e: e.transpose(out, in_, ident), r=[in_, ident], w=[out])

    def act(self, out, in_, func, bias=None, scale=None):
        kw = {}
        rr = [in_]
        if bias is not None:
            kw["bias"] = bias
            if not isinstance(bias, (int, float)):
                rr.append(bias)
        if scale is not None:
            kw["scale"] = scale
            if not isinstance(scale, (int, float)):
                rr.append(scale)
        self.op("act", lambda e: e.activation(out, in_, func, **kw), r=rr, w=[out])

    def tt(self, eng, out, in0, in1, op):
        self.op(eng, lambda e: e.tensor_tensor(out, in0, in1, op), r=[in0, in1], w=[out])

    def ts(self, eng, out, in0, s1, op0, s2=None, op1=None):
        rr = [in0] + [s for s in (s1, s2) if s is not None and not isinstance(s, (int, float))]
        kw = {}
        if op1 is not None:
            kw["op1"] = op1
        self.op(eng, lambda e: e.tensor_scalar(out, in0, s1, s2, op0, **kw), r=rr, w=[out])

    def stt(self, out, in0, scalar, in1, op0, op1):
        rr = [in0, in1] + ([scalar] if not isinstance(scalar, (int, float)) else [])
        self.op("dve", lambda e: e.scalar_tensor_tensor(out, in0, scalar, in1, op0, op1), r=rr, w=[out])

    def copy(self, eng, out, in_):
        if eng == "act":
            self.op("act", lambda e: e.activation(out, in_, AF.Identity), r=[in_], w=[out])
        else:
            self.op(eng, lambda e: e.tensor_copy(out, in_), r=[in_], w=[out])

    def memset(self, eng, out, val):
        self.op(eng, lambda e: e.memset(out, val), r=[], w=[out])

    def reduce(self, out, in_, op, axis=AX.X):
        self.op("dve", lambda e: e.tensor_reduce(out, in_, axis, op), r=[in_], w=[out])

    def recip(self, out, in_):
        self.op("dve", lambda e: e.reciprocal(out, in_), r=[in_], w=[out])

    def scan(self, out, d0, d1, init, op0, op1):
        self.op("dve", lambda e: e.tensor_tensor_scan(out, d0, d1, init, op0, op1), r=[d0, d1], w=[out])


def _prod(xs):
    p = 1
    for x in xs:
        p *= x
    return p


def build_nc(dbg=(), stop=None):
    nc = bass.Bass("TRN2", target_bir_lowering=False)

    def din(name, shape):
        return nc.dram_tensor(name, list(shape), F32, kind="ExternalInput").ap()

    def dout(name, shape):
        return nc.dram_tensor(name, list(shape), F32, kind="ExternalOutput").ap()

    I = {}
    for name, shape in [("xT", (128, 8, T)), ("cv", (128, 8)), ("fl", (128, 4)), ("s0T", (128, 4, 2, 64)),
                        ("ckT", (128, 4, 256)), ("cvv", (128, 2, 512)), ("tbl", (8, 128, NBLK, 128)),
                        ("cstb", (128, NCB)), ("smask", (128, T)), ("pv", (128, NPV)), ("lnp", (128, 2, 4, 64)),
                        ("wada", (128, 8, 6 * D)), ("win", (128, 8, 5504)), ("wup", (128, 512)),
                        ("aup", (128, 512)), ("gup", (128, 512)), ("wor", (128, 4, D)), ("won", (128, 4, D)),
                        ("wout", (128, 8, D)), ("w1", (128, 8, FF)), ("w3", (128, 8, FF)), ("w2", (128, NFT, D))]:
        I[name] = din(name, shape)
    O = {}
    for name, shape in [("yT", (128, 8, T)), ("oS", (128, 4, 2, 4, 64)), ("okT", (128, 4, T)), ("ov", (128, 8, 512))]:
        O[name] = dout(name, shape)
    dbg_out = {}

    with ExitStack() as top:
        S = Sched(nc, top)

        def sbt(stack, name, shape, dt):
            return stack.enter_context(nc.sbuf_tensor("sb_" + name, list(shape), dt))

        slots = [sbt(top, "ws%d" % i, (128, 4096), BF16) for i in range(NSLOT)]
        cb = sbt(top, "cb", (128, NCB), BF16)
        pv = sbt(top, "pvt", (128, NPV), F32)
        fl = sbt(top, "flt", (128, 4), F32)
        dv = sbt(top, "dv", (128, 104), F32)
        mod = sbt(top, "mod", (128, 48), F32)
        hT = sbt(top, "hT", (128, 8, T), BF16)
        oT_na = sbt(top, "oTna", (128, 4, T), BF16)
        oT_rw = sbt(top, "oTrw", (128, 4, T), BF16)
        cvb = sbt(top, "cvb", (128, 8), BF16)
        NPF = 6
        PF = [top.enter_context(nc.psum_tensor("pf%d" % i, [128, 512], F32)) for i in range(NPF)]
        PBs = [top.enter_context(nc.psum_tensor("pb%d" % i, [128, 1024], BF16)) for i in range(2)]
        PB = PBs[0]
        pfi = [0]

        live = set()

        def pf():
            for _ in range(8):
                t = PF[pfi[0] % NPF]
                pfi[0] += 1
                if t.name not in live:
                    return t
            raise RuntimeError("no free psum bank")

        wsi = [0]

        wq = []

        def _wload(src, shape, track):
            slot = slots[wsi[0] % NSLOT]
            wsi[0] += 1
            n = _prod(shape[1:])
            v = slot[:, 0:n]
            if len(shape) == 3:
                v = v.rearrange("p (a b) -> p a b", a=shape[1], b=shape[2])
            S.dma("pool", v, src, track=track)
            return v

        def prefetch(src, shape):
            wq.append((repr(src), tuple(shape), _wload(src, shape, False)))

        def wload(src, shape):
            if wq:
                k, sh, v = wq.pop(0)
                assert k == repr(src) and sh == tuple(shape), (k, repr(src))
                return v
            return _wload(src, shape, True)

        def dbgdump(name, ap, shape, dt=F32):
            if name not in dbg:
                return
            d = nc.dram_tensor("dbg_" + name, list(shape), dt, kind="ExternalOutput").ap()
            dbg_out[name] = d
            S.dma("sp", d, ap)

        def chk(tag):
            if stop == tag:
                S.barrier()
                S.flush()
                return True
            return False

        ident = cb[:, CB_ID:CB_ID + 128]
        ones_b = cb[:, CB_ONE:CB_ONE + 128]
        bones = cb[:, CB_BO:CB_BO + 128]
        mLT = cb[:, CB_LT:CB_LT + 64]
        mLE = cb[:, CB_LE:CB_LE + 64]
        mGT = cb[:, CB_GT:CB_GT + 64]
        mGE = cb[:, CB_GE:CB_GE + 64]
        eye64 = cb[:, CB_I64:CB_I64 + 64]

        def bc8(m):
            return m.unsqueeze(1).to_broadcast([128, 8, 64])

        def pcol(i):
            return pv[:, i:i + 1]

        DV_C0, DV_NF0, DV_NF1, DV_HW0, DV_HA0, DV_OMKA, DV_RK05, DV_QG8 = 0, 15, 30, 45, 53, 61, 65, 69
        DV_GS1, DV_GS2, DV_G1H = 70, 78, 86
        DV_HKA, DV_OMHKA = 94, 98

        def dcol(i):
            return dv[:, i:i + 1]

        S.dma("pool", cb[:], I["cstb"][:])
        S.dma("sp", pv[:], I["pv"][:])
        S.dma("sp", fl[:], I["fl"][:])
        S.tt("dve", dv[:, DV_C0:DV_C0 + 15], pv[:, PV_MU0:PV_MU0 + 15], pv[:, PV_MU1:PV_MU1 + 15], ALU.add)
        S.ts("dve", dv[:, DV_C0:DV_C0 + 15], dv[:, DV_C0:DV_C0 + 15], -1.0, ALU.mult, 1.0, ALU.add)
        S.ts("dve", dv[:, DV_NF0:DV_NF0 + 15], pv[:, PV_MU0:PV_MU0 + 15], fl[:, 0:1], ALU.mult, -1.0, ALU.mult)
        S.ts("dve", dv[:, DV_NF1:DV_NF1 + 15], pv[:, PV_MU1:PV_MU1 + 15], fl[:, 0:1], ALU.mult, -1.0, ALU.mult)
        S.ts("dve", dv[:, DV_HW0:DV_HW0 + 8], pv[:, PV_W0:PV_W0 + 8], 0.5, ALU.mult)
        S.ts("dve", dv[:, DV_HA0:DV_HA0 + 8], pv[:, PV_A0:PV_A0 + 8], 0.5, ALU.mult)
        S.ts("dve", dv[:, DV_OMKA:DV_OMKA + 4], pv[:, PV_KA:PV_KA + 4], -1.0, ALU.mult, 1.0, ALU.add)
        S.ts("dve", dv[:, DV_HKA:DV_HKA + 4], pv[:, PV_KA:PV_KA + 4], 0.5, ALU.mult)
        S.ts("dve", dv[:, DV_OMHKA:DV_OMHKA + 4], pv[:, PV_KA:PV_KA + 4], -0.5, ALU.mult, 1.0, ALU.add)
        S.ts("dve", dv[:, DV_RK05:DV_RK05 + 4], pv[:, PV_RK:PV_RK + 4], 0.5, ALU.mult)
        S.ts("dve", dv[:, DV_QG8:DV_QG8 + 1], pv[:, PV_QG:PV_QG + 1], 0.125, ALU.mult)

        def mod_group(g):
            pm = pf()
            w = wload(I["wada"][:, :, g * 512:(g + 1) * 512], (128, 8, 512))
            for j4 in range(4):
                for kc in range(8):
                    S.mm(pm[:, j4:j4 + 1], w[:, kc, j4 * 128:(j4 + 1) * 128], cvb[:, kc:kc + 1],
                         start=(kc == 0), stop=(kc == 7))
            S.tt("dve", mod[:, g * 4:g * 4 + 4], pm[:, 0:4], pv[:, PV_BADA + g * 4:PV_BADA + g * 4 + 4], ALU.add)

        def modulation(j0, j1):
            for g in range(j0 // 4, j1 // 4):
                mod_group(g)

        def rmsnorm_mod(xT, gs0, sh0, tmpA, sqb, rstd):
            pa, pb_ = pf(), pf()
            for kc in range(8):
                S.act(sqb[:, kc % 2, :], xT[:, kc, :], AF.Square)
                for th, pp in enumerate((pa, pb_)):
                    S.mm(pp[:], ones_b, sqb[:, kc % 2, th * 512:(th + 1) * 512], start=(kc == 0), stop=(kc == 7))
            for th, pp in enumerate((pa, pb_)):
                S.act(rstd[:, th * 512:(th + 1) * 512], pp[:], AF.Ln, scale=1.0 / D, bias=RMS_EPS)
            S.act(rstd[:], rstd[:], AF.Exp, scale=-0.5)
            for kc in range(8):
                S.tt("dve", tmpA[:, kc % 2, :], xT[:, kc, :], rstd[:], ALU.mult)
                S.act(hT[:, kc, :], tmpA[:, kc % 2, :], AF.Identity, scale=dv[:, gs0 + kc:gs0 + kc + 1],
                      bias=mod[:, sh0 + kc:sh0 + kc + 1])

        with ExitStack() as ph:
            xT = sbt(ph, "xTa", (128, 8, T), F32)
            cvf = sbt(ph, "cvf", (128, 8), F32)
            tmpA = sbt(ph, "tmpA", (128, 2, T), F32)
            sqb = sbt(ph, "sqb", (128, 2, T), BF16)
            rstd = sbt(ph, "rstd", (128, T), F32)
            S.dma("sp", cvf[:], I["cv"][:])
            S.dma("sp", xT[:], I["xT"][:])
            S.act(cvb[:], cvf[:], AF.Silu)
            modulation(0, 16)
            S.stt(dv[:, DV_GS1:DV_GS1 + 8], mod[:, 8:16], 1.0, pv[:, PV_N1:PV_N1 + 8], ALU.add, ALU.mult)
            rmsnorm_mod(xT, DV_GS1, 0, tmpA, sqb, rstd)
            dbgdump("hT", hT[:], (128, 8, T), BF16)
            prefetch(I["win"][:, :, 1920:1920 + 512], (128, 8, 512))
            prefetch(I["win"][:, :, 2432:2432 + 512], (128, 8, 512))
            S.barrier()
            S.flush()
            if stop == "A":
                return nc, dbg_out

        WC_LORA, WC_RD, WC_NQ, WC_NK, WC_NV, WC_GRW, WC_GNA = 0, 384, 1920, 2432, 2944, 3456, 4480

        with ExitStack() as ph:
            qT = sbt(ph, "qT", (128, 4, T), BF16)
            kT = sbt(ph, "kT", (128, 4, T + 256), BF16)
            vaug = sbt(ph, "vaug", (128, 10, 8, 66), BF16)
            ona = sbt(ph, "ona", (128, 8, 512), BF16)
            U = [sbt(ph, "naU%d" % i, (128, T), F32) for i in range(2)]
            sq = [sbt(ph, "nasq%d" % i, (128, T), BF16) for i in range(2)]
            rin = [sbt(ph, "narin%d" % i, (128, T), F32) for i in range(2)]
            kn = [sbt(ph, "nakn%d" % i, (128, T), F32) for i in range(2)]
            vf = [sbt(ph, "navf%d" % i, (128, 512), F32) for i in range(2)]
            tb = [sbt(ph, "natb%d" % i, (128, NBLK, 128), BF16) for i in range(2)]
            PT = [sbt(ph, "naPT%d" % i, (128, NBLK + 16, 128), BF16) for i in range(2)]
            lg = [sbt(ph, "nalg%d" % i, (128, 512), F32) for i in range(3)]
            rden = sbt(ph, "rden", (128, 8), F32)
            if chk("NA00"):
                return nc, dbg_out
            S.dma("pool", kT[:, :, T:T + 256], I["ckT"][:])
            if chk("NA01"):
                return nc, dbg_out
            cvst = sbt(ph, "cvst", (128, 2, 512), BF16)
            S.dma("pool", cvst[:], I["cvv"][:])
            for a_ in range(2):
                S.copy("dve", vaug[:, 8 + a_, :, 0:64], cvst[:, a_, :].rearrange("p (h d) -> p h d", d=64))
            S.memset("dve", vaug[:, :, :, 64:65], 1.0)
            if chk("NA0"):
                return nc, dbg_out
            for which, wc in ((0, WC_NQ), (1, WC_NK)):
                w = wload(I["win"][:, :, wc:wc + 512], (128, 8, 512))
                for rd in range(4):
                    b = rd % 2
                    pa, pb_ = pf(), pf()
                    for th, pp in enumerate((pa, pb_)):
                        for kc in range(8):
                            S.mm(pp[:], w[:, kc, rd * 128:(rd + 1) * 128], hT[:, kc, th * 512:(th + 1) * 512],
                                 start=(kc == 0), stop=(kc == 7))
                    for th, pp in enumerate((pa, pb_)):
                        S.act(sq[b][:, th * 512:(th + 1) * 512], pp[:], AF.Square)
                        S.copy("dve", U[b][:, th * 512:(th + 1) * 512], pp[:])
                    pc, pd = pf(), pf()
                    for th, pp in enumerate((pc, pd)):
                        S.mm(pp[:], bones, sq[b][:, th * 512:(th + 1) * 512])
                    for th, pp in enumerate((pc, pd)):
                        S.act(rin[b][:, th * 512:(th + 1) * 512], pp[:], AF.Ln, scale=1.0 / 64, bias=RMS_EPS)
                    S.act(rin[b][:], rin[b][:], AF.Exp, scale=-0.5)
                    if which == 0:
                        S.stt(qT[:, rd, :], U[b][:], dcol(DV_QG8), rin[b][:], ALU.mult, ALU.mult)
                    else:
                        S.stt(kn[b][:], U[b][:], pcol(PV_KG), rin[b][:], ALU.mult, ALU.mult)
                        S.copy("act", kT[:, rd, 0:T], kn[b][:])
                        S.dma("sp", O["okT"][:, rd, :], kn[b][:])
            if chk("NA1"):
                return nc, dbg_out
            w = wload(I["win"][:, :, WC_NV:WC_NV + 512], (128, 8, 512))
            for tt_ in range(8):
                pp = pf()
                for kc in range(8):
                    S.mm(pp[:], hT[:, kc, tt_ * 128:(tt_ + 1) * 128], w[:, kc, :], start=(kc == 0), stop=(kc == 7))
                S.copy("act", vf[tt_ % 2][:], pp[:])
                S.copy("dve", vaug[:, tt_, :, 0:64], pp[:].rearrange("p (h d) -> p h d", d=64))
                S.dma("sp", O["ov"][:, tt_, :], vf[tt_ % 2][:])
            dbgdump("qT", qT[:], (128, 4, T), BF16)
            dbgdump("kT", kT[:], (128, 4, T + 256), BF16)
            if chk("NA2"):
                return nc, dbg_out
            li = 0
            for h in range(8):
                rd, h2 = h // 2, h % 2
                pr = slice(64 * h2, 64 * h2 + 64)
                tbh = tb[h % 2]
                PTh = PT[h % 2]
                S.dma("pool", tbh[:], I["tbl"][h])
                for j in range(8):
                    Rs = QL[j]
                    q0, nq = Rs[0] * 128, len(Rs) * 128
                    b0 = BLK[(j, Rs[0])]
                    for off in range(0, nq, 512):
                        n = min(512, nq - off)
                        pp = pf()
                        S.mm(pp[:, 0:n], kT[pr, rd, j * 128:(j + 1) * 128], qT[pr, rd, q0 + off:q0 + off + n])
                        bb = b0 + off // 128
                        l = lg[li % 3]
                        li += 1
                        S.tt("dve", l[:, 0:n], pp[:, 0:n],
                             tbh[:, bb:bb + n // 128, :].rearrange("p a b -> p (a b)"), ALU.add)
                        S.act(PTh[:, bb:bb + n // 128, :].rearrange("p a b -> p (a b)"), l[:, 0:n], AF.Exp)
                for jc in range(2):
                    for off in range(0, T, 512):
                        pp = pf()
                        S.mm(pp[:], kT[pr, rd, T + jc * 128:T + (jc + 1) * 128], qT[pr, rd, off:off + 512])
                        bb = NBLK + jc * 8 + off // 128
                        S.act(PTh[:, bb:bb + 4, :].rearrange("p a b -> p (a b)"), pp[:], AF.Exp, bias=fl[:, 1:2])
                for half in range(2):
                    po = pf()
                    for r4 in range(4):
                        R = half * 4 + r4
                        w0 = _qwin(R)
                        keys = [(jj, BLK[(jj, R)]) for jj in range(w0, w0 + 5)] + \
                               [(8 + jc, NBLK + jc * 8 + R) for jc in range(2)]
                        for n_, (vt, bb) in enumerate(keys):
                            S.mm(po[:, r4 * 65:(r4 + 1) * 65], PTh[:, bb, :], vaug[:, vt, h, 0:65],
                                 start=(n_ == 0), stop=(n_ == len(keys) - 1))
                    pov = po[:, 0:260].rearrange("p (r c) -> p r c", c=65)
                    S.recip(rden[:, half * 4:(half + 1) * 4], pov[:, :, 64])
                    S.tt("dve", ona[:, half * 4:(half + 1) * 4, h * 64:(h + 1) * 64], pov[:, :, 0:64],
                         rden[:, half * 4:(half + 1) * 4].unsqueeze(2).to_broadcast([128, 4, 64]), ALU.mult)
            if chk("NA3"):
                return nc, dbg_out
            for ct in range(4):
                for R in range(8):
                    S.tr(PB[:, R * 128:(R + 1) * 128], ona[:, R, ct * 128:(ct + 1) * 128], ident)
                S.copy("act", oT_na[:, ct, :], PB[:])
            dbgdump("oTna", oT_na[:], (128, 4, T), BF16)
            prefetch(I["win"][:, :, WC_LORA:WC_LORA + 384], (128, 8, 384))
            prefetch(I["win"][:, :, WC_RD:WC_RD + 384], (128, 8, 384))
            S.barrier()
            S.flush()
            if stop == "NA":
                return nc, dbg_out

        with ExitStack() as ph:
            twd = sbt(ph, "twd", (128, T), BF16)
            adb = sbt(ph, "adb", (128, T), BF16)
            sgd = sbt(ph, "sgd", (128, T), BF16)
            wupb = sbt(ph, "wupb", (128, 512), BF16)
            aupb = sbt(ph, "aupb", (128, 512), BF16)
            gupb = sbt(ph, "gupb", (128, 512), BF16)
            lnp = [sbt(ph, "lnpt%d" % i, (128, 2, 64), F32) for i in range(2)]
            smask = sbt(ph, "smaskt", (128, 512), F32)
            s0r = [sbt(ph, "s0r%d" % i, (128, 2, 64), F32) for i in range(2)]
            bon = [sbt(ph, "bon%d" % i, (128, 16), F32) for i in range(2)]
            Uraw = sbt(ph, "Uraw", (128, T), F32)
            Rp = sbt(ph, "Rp", (128, T), F32)
            Kp = sbt(ph, "Kp", (128, T), F32)
            VKD = sbt(ph, "VKD", (128, T), F32)
            Vp = VKD
            KD = VKD[:, 0:512]
            Dx = VKD[:, 512:1024]
            Vb = [sbt(ph, "Vb0", (128, T), BF16)] * 2
            Vtm = [sbt(ph, "Vtm%d" % i, (128, 16, 64), BF16) for i in range(2)]
            gfm = [sbt(ph, "gfm%d" % i, (128, T), BF16) for i in range(2)]
            kkn = sbt(ph, "kkn", (128, 512), F32)
            sqk = sbt(ph, "sqk", (128, 512), BF16)
            LW = sbt(ph, "LW", (128, 512), F32)
            C = sbt(ph, "Ccs", (128, 512), F32)
            IC = sbt(ph, "IC", (128, 512), F32)
            E0 = sbt(ph, "E0", (128, 512), F32)
            PBh = [sbt(ph, "PBh%d" % d, (128, 512), BF16) for d in range(2)]
            AR = [[sbt(ph, "AR%d%d" % (h, d), (128, 8, 2, 64), BF16) for d in range(2)] for h in range(2)]
            BK = [[sbt(ph, "BK%d%d" % (h, d), (128, 8, 2, 64), BF16) for d in range(2)] for h in range(2)]
            AA = [[sbt(ph, "AA%d%d" % (h, d), (128, 8, 2, 64), BF16) for d in range(2)] for h in range(2)]
            AB = [[sbt(ph, "AB%d%d" % (h, d), (128, 8, 2, 64), BF16) for d in range(2)] for h in range(2)]
            KTM = [[sbt(ph, "KTM%d%d" % (h, d), (128, 8, 64), BF16) for d in range(2)] for h in range(2)]
            PN = [[sbt(ph, "PN%d%d" % (h, d), (128, 8, 2, 64), BF16) for d in range(2)] for h in range(2)]
            NT = [[sbt(ph, "NT%d%d" % (h, d), (128, 8, 64), BF16) for d in range(2)] for h in range(2)]
            AV = [[sbt(ph, "AV%d%d" % (h, d), (128, 8, 2, 64), BF16) for d in range(2)] for h in range(2)]
            Rh = sbt(ph, "Rh", (128, 2, 16, 64), BF16)
            G0 = sbt(ph, "G0", (128, 2, 16, 64), BF16)
            Avv = sbt(ph, "Avv", (128, 2, 16, 64), BF16)
            H0 = sbt(ph, "H0", (128, 2, 16, 64), BF16)
            Sall = [[sbt(ph, "S%d_%d" % (d, i), (128, 64), BF16) for i in range(17)] for d in range(2)]
            SF = sbt(ph, "SF", (128, 2, 4, 64), F32)
            wend = sbt(ph, "wend", (128, 2, 16), F32)
            wendc = sbt(ph, "wendc", (128, 2, 16), F32)
            wk = sbt(ph, "wk", (128, 2, 16), F32)
            Yf = sbt(ph, "Yf", (128, 16, 64), F32)
            Yq = sbt(ph, "Yq", (128, 16, 64), BF16)
            otm = sbt(ph, "otm", (128, 16, 64), BF16)
            st4 = sbt(ph, "st4", (128, 6, 16), F32)

            S.dma("pool", wupb[:], I["wup"][:])
            S.dma("pool", aupb[:], I["aup"][:])
            S.dma("pool", gupb[:], I["gup"][:])
            S.dma("sp", smask[:], I["smask"][:, 0:512])

            def inproj_shift(w, ti, q, dest):
                pa, pb_ = pf(), pf()
                for th, pp in enumerate((pa, pb_)):
                    for kc in range(8):
                        S.mm(pp[:], w[:, kc, ti * 128:(ti + 1) * 128], hT[:, kc, th * 512:(th + 1) * 512],
                             start=(kc == 0), stop=(kc == 7))
                for th, pp in enumerate((pa, pb_)):
                    S.copy("act", Uraw[:, th * 512:(th + 1) * 512], pp[:])
                    S.ts("dve", dest[:, th * 512:(th + 1) * 512], pp[:], dcol(DV_C0 + q), ALU.mult)
                S.stt(dest[:, 1:T], Uraw[:, 0:T - 1], pcol(PV_MU0 + q), dest[:, 1:T], ALU.mult, ALU.add)
                S.stt(dest[:, 0:T - 1], Uraw[:, 1:T], pcol(PV_MU1 + q), dest[:, 0:T - 1], ALU.mult, ALU.add)
                S.stt(dest[:, 256:T:256], Uraw[:, 255:T - 1:256], dcol(DV_NF0 + q), dest[:, 256:T:256],
                      ALU.mult, ALU.add)
                S.stt(dest[:, 255:T - 1:256], Uraw[:, 256:T:256], dcol(DV_NF1 + q), dest[:, 255:T - 1:256],
                      ALU.mult, ALU.add)

            w = wload(I["win"][:, :, WC_LORA:WC_LORA + 384], (128, 8, 384))
            inproj_shift(w, 0, 0, Rp)
            S.act(twd[:], Rp[:], AF.Tanh)
            inproj_shift(w, 1, 1, Kp)
            S.copy("act", adb[:], Kp[:])
            inproj_shift(w, 2, 2, Vp)
            S.act(Yf[:].rearrange("p c v -> p (c v)"), Vp[:], AF.Tanh, scale=0.5)
            S.ts("dve", sgd[:], Yf[:].rearrange("p c v -> p (c v)"), 0.5, ALU.mult, 0.5, ALU.add)

            def v3(ap):
                return ap.rearrange("p (c t) -> p c t", t=64)

            mstrict = (mLT, mGT)
            mincl = (mLE, mGE)
            mstrictT = (mGT, mLT)

            def bc4(m):
                return m.unsqueeze(1).to_broadcast([128, 4, 64])

            def pv4(p):
                return p[:].rearrange("p (c x) -> p c x", x=128)

            def f128(ap):
                return ap.rearrange("p a b -> p (a b)")

            def pfl():
                t = pf()
                live.add(t.name)
                return t

            def rel(banks):
                for t in banks:
                    live.discard(t.name)

            def run(*gens):
                gens = list(gens)
                while gens:
                    for g_ in list(gens):
                        try:
                            next(g_)
                        except StopIteration:
                            gens.remove(g_)

            def head(rd):
                p = rd % 2
                S.dma("sp", lnp[p][:], I["lnp"][:, :, rd, :])
                S.dma("sp", s0r[p][:], I["s0T"][:, rd, :, :])
                w = wload(I["win"][:, :, WC_RD + rd * 384:WC_RD + (rd + 1) * 384], (128, 8, 384))
                for ti, dest in enumerate((Rp, Kp, Vp)):
                    inproj_shift(w, ti, 3 + rd * 3 + ti, dest)
                    yield
                mod_group(4 + 2 * rd)
                mod_group(5 + 2 * rd)
                yield
                S.copy("act", Vb[p][:], Vp[:])
                for c in range(16):
                    for h2 in range(2):
                        pr = slice(64 * h2, 64 * h2 + 64)
                        S.tr(PB[pr, c * 64:(c + 1) * 64], Vb[p][pr, c * 64:(c + 1) * 64], ident[pr, pr])
                S.copy("act", Vtm[p][:].rearrange("p c v -> p (c v)"), PB[:])
                yield
                for th in range(2):
                    pp = pf()
                    S.mm(pp[:], gupb[:, rd * 128:(rd + 1) * 128], sgd[:, th * 512:(th + 1) * 512])
                    S.copy("act", gfm[p][:, th * 512:(th + 1) * 512], pp[:])
                yield

            pbi = [0]

            def prep(rd, half):
                hs = slice(half * 512, half * 512 + 512)
                S.act(sqk[:], Kp[:, hs], AF.Square, scale=pcol(PV_KK + rd))
                pp = pf()
                S.mm(pp[:], bones, sqk[:])
                S.act(E0[:], pp[:], AF.Ln, bias=L2_EPS)
                S.act(E0[:], E0[:], AF.Exp, scale=-0.5)
                S.stt(kkn[:], Kp[:, hs], pcol(PV_KK + rd), E0[:], ALU.mult, ALU.mult)
                yield
                for d in range(2):
                    ARd, BKd, AAd, ABd, KTMd = AR[half][d], BK[half][d], AA[half][d], AB[half][d], KTM[half][d]
                    pd = slice(64 * d, 64 * d + 64)
                    pp = pf()
                    S.mm(pp[:], wupb[pd, rd * 128:(rd + 1) * 128], twd[pd, hs])
                    S.act(LW[:], pp[:], AF.Tanh, scale=0.5, bias=dcol(DV_HW0 + d * 4 + rd))
                    S.act(LW[:], LW[:], AF.Identity, scale=LWC, bias=LWC)
                    S.scan(C[:], smask[:], LW[:], 0.0, ALU.mult, ALU.add)
                    pp = pf()
                    S.mm(pp[:], aupb[pd, rd * 128:(rd + 1) * 128], adb[pd, hs])
                    S.act(IC[:], pp[:], AF.Tanh, scale=0.5, bias=dcol(DV_HA0 + d * 4 + rd))
                    yield
                    S.act(Dx[:], IC[:], AF.Identity, scale=dcol(DV_HKA + rd), bias=dcol(DV_OMHKA + rd))
                    S.act(IC[:], IC[:], AF.Identity, scale=0.5, bias=0.5)
                    S.tt("dve", KD[:], Kp[:, hs], Dx[:], ALU.mult)
                    S.stt(PBh[d][:], Rp[:, hs], dcol(DV_RK05 + rd), KD[:], ALU.mult, ALU.mult)
                    S.tt("dve", IC[:], kkn[:], IC[:], ALU.mult)
                    yield
                    tot = v3(C[:])[:, :, 63]
                    if d == 0:
                        S.tt("dve", Dx[:], C[:], LW[:], ALU.subtract)
                        logW = C
                    else:
                        S.tt("dve", v3(Dx[:]), tot.unsqueeze(2).to_broadcast([128, 8, 64]), v3(C[:]), ALU.subtract)
                        S.tt("dve", LW[:], Dx[:], LW[:], ALU.add)
                        logW = LW
                    S.act(wend[:, d, half * 8:(half + 1) * 8], tot, AF.Exp)
                    S.act(E0[:], logW[:], AF.Exp)
                    S.tt("dve", ARd[:, :, 1, :], v3(Rp[:, hs]), v3(E0[:]), ALU.mult)
                    S.act(E0[:], Dx[:], AF.Exp)
                    S.stt(ARd[:, :, 0, :], v3(kkn[:]), -1.0, v3(E0[:]), ALU.mult, ALU.mult)
                    yield
                    S.act(E0[:], logW[:], AF.Exp, scale=-1.0)
                    S.tt("dve", BKd[:, :, 0, :], v3(IC[:]), v3(E0[:]), ALU.mult)
                    S.tt("dve", BKd[:, :, 1, :], v3(KD[:]), v3(E0[:]), ALU.mult)
                    yield
                    for src, dst, eng in ((ARd[:, :, 0, :], AAd[:, :, 0, :], "act"),
                                          (BKd[:, :, 0, :], ABd[:, :, 1, :], "act"),
                                          (BKd[:, :, 1, :], KTMd[:], "act")):
                        PBx = PBs[pbi[0] % 2]
                        pbi[0] += 1
                        for c in range(8):
                            for h2 in range(2):
                                pr = slice(64 * h2, 64 * h2 + 64)
                                S.tr(PBx[pr, c * 64:(c + 1) * 64], src[pr, c, :], ident[pr, pr])
                        S.copy(eng, dst, v3(PBx[:, 0:512]))
                        yield
                pbn = pf()
                for c in range(8):
                    for h2 in range(2):
                        pr = slice(64 * h2, 64 * h2 + 64)
                        for d in range(2):
                            S.mm(pbn[pr, c:c + 1], PBh[d][pr, c * 64:(c + 1) * 64], ones_b[pr, 0:1],
                                 start=(d == 0), stop=(d == 1))
                S.copy("act", bon[rd % 2][:, half * 8:(half + 1) * 8], pbn[:, 0:8])
                yield

            def chunk_pipe(d, half):
                ms, mi, mT = mstrict[d], mincl[d], mstrictT[d]
                ARd, BKd, AAd, ABd, KTMd = AR[half][d], BK[half][d], AA[half][d], AB[half][d], KTM[half][d]
                PNs, NTs, AVs = PN[half][d], NT[half][d], AV[half][d]
                g8 = slice(half * 8, half * 8 + 8)

                def stg(lhs, rhs, wide):
                    banks = [pfl(), pfl()] if wide else [pfl()]
                    for c in range(8):
                        for h2 in range(2):
                            pr = slice(64 * h2, 64 * h2 + 64)
                            if wide:
                                o = banks[c // 4][pr, (c % 4) * 128:(c % 4 + 1) * 128]
                            else:
                                o = banks[0][pr, c * 64:(c + 1) * 64]
                            S.mm(o, lhs(pr, c), rhs(pr, c))
                    return banks

                o = stg(lambda pr, c: BKd[pr, c, 0, :], lambda pr, c: f128(ARd[pr, c, :, :]), True)
                yield "mm"
                for b in range(2):
                    bs = slice(b * 4, b * 4 + 4)
                    S.tt("dve", PNs[:, bs, 1, :], pv4(o[b])[:, :, 0:64], bc4(ms), ALU.mult)
                    S.tt("dve", ABd[:, bs, 0, :], pv4(o[b])[:, :, 64:128], bc4(mi), ALU.mult)
                rel(o)
                S.tt("pool", PNs[:, :, 0, :], PNs[:, :, 1, :], bc8(eye64), ALU.add)
                yield "ev"
                o = stg(lambda pr, c: BKd[pr, c, 1, :], lambda pr, c: ARd[pr, c, 1, :], False)
                yield "mm"
                S.tt("dve", Avv[:, d, g8, :], v3(o[0][:]), bc8(mi), ALU.mult)
                rel(o)
                yield "ev"
                o = stg(lambda pr, c: ARd[pr, c, 0, :], lambda pr, c: f128(BKd[pr, c, :, :]), True)
                yield "mm"
                for b in range(2):
                    bs = slice(b * 4, b * 4 + 4)
                    S.tt("dve", NTs[:, bs, :], pv4(o[b])[:, :, 0:64], bc4(mT), ALU.mult)
                    S.tt("dve", AAd[:, bs, 1, :], pv4(o[b])[:, :, 64:128], bc4(mT), ALU.mult)
                rel(o)
                yield "ev"
                oa = stg(lambda pr, c: NTs[pr, c, :], lambda pr, c: PNs[pr, c, 1, :], False)
                ob = stg(lambda pr, c: PNs[pr, c, 1, :], lambda pr, c: NTs[pr, c, :], False)
                yield "mm"
                S.copy("act", PNs[:, :, 1, :], v3(oa[0][:]))
                S.copy("act", NTs[:], v3(ob[0][:]))
                rel(oa + ob)
                yield "ev"
                for step in range(4):
                    oa = stg(lambda pr, c: NTs[pr, c, :], lambda pr, c: f128(PNs[pr, c, :, :]), True)
                    ob = stg(lambda pr, c: PNs[pr, c, 1, :], lambda pr, c: NTs[pr, c, :], False)
                    yield "mm"
                    S.copy("act", NTs[:], v3(ob[0][:]))
                    S.tt("dve", PNs[:, 0:4, 0, :], pv4(oa[0])[:, :, 0:64], PNs[:, 0:4, 0, :], ALU.add)
                    S.copy("act", PNs[:, 0:4, 1, :], pv4(oa[0])[:, :, 64:128])
                    S.tt("dve", PNs[:, 4:8, 0, :], pv4(oa[1])[:, :, 0:64], PNs[:, 4:8, 0, :], ALU.add)
                    S.copy("act", PNs[:, 4:8, 1, :], pv4(oa[1])[:, :, 64:128])
                    rel(oa + ob)
                    yield "ev"
                oa = stg(lambda pr, c: NTs[pr, c, :], lambda pr, c: PNs[pr, c, 0, :], False)
                yield "mm"
                S.tt("dve", PNs[:, :, 0, :], v3(oa[0][:]), PNs[:, :, 0, :], ALU.add)
                rel(oa)
                yield "ev"
                o = stg(lambda pr, c: PNs[pr, c, 0, :], lambda pr, c: f128(AAd[pr, c, :, :]), True)
                yield "mm"
                for b in range(2):
                    bs = slice(b * 4, b * 4 + 4)
                    S.copy("act", AVs[:, bs, :, :].rearrange("p c a b -> p c (a b)"), pv4(o[b]))
                rel(o)
                yield "ev"
                o = stg(lambda pr, c: AVs[pr, c, 0, :], lambda pr, c: f128(ABd[pr, c, :, :]), True)
                yield "mm"
                for b in range(2):
                    bs = slice(b * 4, b * 4 + 4)
                    gs_ = slice(half * 8 + b * 4, half * 8 + b * 4 + 4)
                    S.tt("dve", Rh[:, d, gs_, :], pv4(o[b])[:, :, 0:64], ARd[:, bs, 1, :], ALU.add)
                    S.tt("dve", G0[:, d, gs_, :], pv4(o[b])[:, :, 64:128], bc4(eye64), ALU.add)
                rel(o)
                yield "ev"
                o = stg(lambda pr, c: AVs[pr, c, 1, :], lambda pr, c: f128(ABd[pr, c, :, :]), True)
                yield "mm"
                for b in range(2):
                    bs = slice(b * 4, b * 4 + 4)
                    gs_ = slice(half * 8 + b * 4, half * 8 + b * 4 + 4)
                    S.tt("dve", Avv[:, d, gs_, :], pv4(o[b])[:, :, 0:64], Avv[:, d, gs_, :], ALU.add)
                    S.tt("dve", H0[:, d, gs_, :], pv4(o[b])[:, :, 64:128], KTMd[:, bs, :], ALU.add)
                rel(o)

            def run_pipes(gens):
                idle = list(range(len(gens)))
                holding = []
                while idle or holding:
                    while len(holding) < 2 and idle:
                        i = idle.pop(0)
                        tag = next(gens[i])
                        assert tag == "mm"
                        holding.append(i)
                    i = holding.pop(0)
                    try:
                        tag = next(gens[i])
                        assert tag == "ev"
                        idle.append(i)
                    except StopIteration:
                        pass

            def tail(rd):
                p = rd % 2
                Vt = Vtm[p]
                S.copy("dve", wendc[:], wend[:])
                S.copy("dve", wk[:], wend[:])
                S.ts("dve", wk[:, 0, 3:12:4], wendc[:, 0, 3:12:4], fl[:, 2:3], ALU.mult)
                S.ts("dve", wk[:, 1, 4:13:4], wendc[:, 1, 4:13:4], fl[:, 2:3], ALU.mult)
                S.copy("dve", Sall[0][0][:], s0r[p][:, 0, :])
                S.copy("dve", Sall[1][16][:], s0r[p][:, 1, :])
                yield
                pys = (pfl(), pfl())

                def ymm(c):
                    for h2 in range(2):
                        pr = slice(64 * h2, 64 * h2 + 64)
                        o = pys[c // 8][pr, (c % 8) * 64:(c % 8 + 1) * 64]
                        S.mm(o, Avv[pr, 0, c, :], Vt[pr, c, :], start=True, stop=False)
                        S.mm(o, Avv[pr, 1, c, :], Vt[pr, c, :], start=False, stop=False)
                        S.mm(o, Rh[pr, 0, c, :], Sall[0][c][pr, :], start=False, stop=False)
                        S.mm(o, Rh[pr, 1, c, :], Sall[1][c + 1][pr, :], start=False, stop=True)

                for step in range(16):
                    cf, cbk = step, 15 - step
                    pcs = (pfl(), pfl())
                    for d, c, sin in ((0, cf, cf), (1, cbk, cbk + 1)):
                        for h2 in range(2):
                            pr = slice(64 * h2, 64 * h2 + 64)
                            S.mm(pcs[d][pr, 0:64], H0[pr, d, c, :], Vt[pr, c, :], start=True, stop=False)
                            S.mm(pcs[d][pr, 0:64], G0[pr, d, c, :], Sall[d][sin][pr, :], start=False, stop=True)
                    if step >= 8:
                        ymm(15 - step)
                        ymm(step)
                    yield
                    S.act(Sall[0][cf + 1][:], pcs[0][:, 0:64], AF.Identity, scale=wk[:, 0, cf:cf + 1])
                    S.ts("dve", Sall[1][cbk][:], pcs[1][:, 0:64], wk[:, 1, cbk:cbk + 1], ALU.mult)
                    if cf % 4 == 3:
                        S.act(SF[:, 0, cf // 4, :], pcs[0][:, 0:64], AF.Identity, scale=wendc[:, 0, cf:cf + 1])
                        S.ts("dve", SF[:, 1, cbk // 4, :], pcs[1][:, 0:64], wendc[:, 1, cbk:cbk + 1], ALU.mult)
                    rel(list(pcs))
                S.dma("sp", O["oS"][:, rd, :, :, :], SF[:])
                for b in range(2):
                    S.copy("act", Yf[:, b * 8:(b + 1) * 8, :], v3(pys[b][:]))
                rel(list(pys))
                yield
                S.reduce(st4[:, 0, :], Yf[:], ALU.add)
                S.tt("dve", Yq[:], Yf[:], Yf[:], ALU.mult)
                S.reduce(st4[:, 1, :], Yq[:], ALU.add)
                yield
                S.ts("dve", st4[:, 0, :], st4[:, 0, :], 1.0 / 64, ALU.mult)
                S.tt("dve", st4[:, 2, :], st4[:, 0, :], st4[:, 0, :], ALU.mult)
                S.stt(st4[:, 3, :], st4[:, 1, :], 1.0 / 64, st4[:, 2, :], ALU.mult, ALU.subtract)
                S.act(st4[:, 3, :], st4[:, 3, :], AF.Ln, bias=GN_EPS)
                S.act(st4[:, 3, :], st4[:, 3, :], AF.Exp, scale=-0.5)
                b16 = lambda ap: ap.unsqueeze(2).to_broadcast([128, 16, 64])
                g16 = lambda ap: ap.unsqueeze(1).to_broadcast([128, 16, 64])
                S.tt("dve", Yq[:], Yf[:], b16(st4[:, 0, :]), ALU.subtract)
                yield
                S.tt("dve", Yq[:], Yq[:], b16(st4[:, 3, :]), ALU.mult)
                S.tt("dve", Yq[:], Yq[:], g16(lnp[p][:, 0, :]), ALU.mult)
                yield
                S.tt("dve", Yq[:], Yq[:], g16(lnp[p][:, 1, :]), ALU.add)
                S.tt("dve", Yf[:], Vt[:], b16(bon[p][:]), ALU.mult)
                yield
                S.tt("dve", otm[:], Yq[:], Yf[:], ALU.add)
                for c in range(16):
                    for h2 in range(2):
                        pr = slice(64 * h2, 64 * h2 + 64)
                        S.tr(PBs[1][pr, c * 64:(c + 1) * 64], otm[pr, c, :], ident[pr, pr])
                S.tt("dve", oT_rw[:, rd, :], PBs[1][:], gfm[p][:], ALU.mult)
                yield

            def nxt(r):
                yield from head(r)
                yield from prep(r, 0)
                yield from prep(r, 1)

            run(nxt(0))
            for rd in range(4):
                run_pipes([chunk_pipe(0, 0), chunk_pipe(1, 0), chunk_pipe(0, 1), chunk_pipe(1, 1)])
                if rd < 3:
                    t_, n_ = tail(rd), nxt(rd + 1)
                    alive = True
                    while alive:
                        alive = False
                        for g_, k_ in ((t_, 1), (n_, 3)):
                            for _ in range(k_):
                                try:
                                    next(g_)
                                    alive = True
                                except StopIteration:
                                    break
                else:
                    run(tail(rd))
            dbgdump("oTrw", oT_rw[:], (128, 4, T), BF16)
            prefetch(I["wor"][:], (128, 4, D))
            prefetch(I["won"][:], (128, 4, D))
            prefetch(I["win"][:, :, WC_GRW:WC_GRW + 512], (128, 8, 512))
            prefetch(I["win"][:, :, WC_GNA:WC_GNA + 512], (128, 8, 512))
            S.barrier()
            S.flush()
            if stop == "RW":
                return nc, dbg_out

        with ExitStack() as phx, ExitStack() as ph:
            xT = sbt(phx, "xTm", (128, 8, T), F32)
            mrg = sbt(ph, "mrg", (128, 8, T), BF16)
            tg = [sbt(ph, "tg%d" % i, (128, 512), F32) for i in range(4)]
            t1 = [sbt(ph, "t1%d" % i, (128, 512), F32) for i in range(2)]
            tmpA = sbt(ph, "tmpB", (128, 2, T), F32)
            sqb = sbt(ph, "sqb2", (128, 2, T), BF16)
            rstd = sbt(ph, "rstd2", (128, T), F32)
            S.dma("sp", xT[:], I["xT"][:])
            S.ts("dve", dv[:, DV_G1H:DV_G1H + 8], mod[:, 16:24], 0.5, ALU.mult)
            S.stt(dv[:, DV_GS2:DV_GS2 + 8], mod[:, 32:40], 1.0, pv[:, PV_N2:PV_N2 + 8], ALU.add, ALU.mult)
            ti = 0
            for g in range(2):
                wor = wload(I["wor"][:], (128, 4, D))
                won = wload(I["won"][:], (128, 4, D))
                wgr = wload(I["win"][:, :, WC_GRW + g * 512:WC_GRW + (g + 1) * 512], (128, 8, 512))
                wgn = wload(I["win"][:, :, WC_GNA + g * 512:WC_GNA + (g + 1) * 512], (128, 8, 512))
                for j4 in range(4):
                    j = g * 4 + j4
                    for th in range(2):
                        ts_ = slice(th * 512, th * 512 + 512)
                        pgr, pgn, pa, pb_ = pf(), pf(), pf(), pf()
                        for kc in range(8):
                            S.mm(pgr[:], wgr[:, kc, j4 * 128:(j4 + 1) * 128], hT[:, kc, ts_], start=(kc == 0), stop=(kc == 7))
                        for kc in range(8):
                            S.mm(pgn[:], wgn[:, kc, j4 * 128:(j4 + 1) * 128], hT[:, kc, ts_], start=(kc == 0), stop=(kc == 7))
                        for kc in range(4):
                            S.mm(pa[:], wor[:, kc, j * 128:(j + 1) * 128], oT_rw[:, kc, ts_], start=(kc == 0), stop=(kc == 3))
                        for kc in range(4):
                            S.mm(pb_[:], won[:, kc, j * 128:(j + 1) * 128], oT_na[:, kc, ts_], start=(kc == 0), stop=(kc == 3))
                        ta, tb_ = tg[ti % 4], tg[(ti + 1) % 4]
                        tt1 = t1[(ti // 2) % 2]
                        ti += 2
                        S.act(ta[:], pgr[:], AF.Tanh, scale=0.5)
                        S.act(tb_[:], pgn[:], AF.Tanh, scale=0.5)
                        S.stt(tt1[:], ta[:], 1.0, pa[:], ALU.add, ALU.mult)
                        S.stt(ta[:], tb_[:], 1.0, pb_[:], ALU.add, ALU.mult)
                        S.tt("dve", mrg[:, j, ts_], tt1[:], ta[:], ALU.add)
            dbgdump("mrg", mrg[:], (128, 8, T), BF16)
            for g in range(2):
                wo = wload(I["wout"][:, :, g * 512:(g + 1) * 512], (128, 8, 512))
                for j4 in range(4):
                    j = g * 4 + j4
                    for th in range(2):
                        ts_ = slice(th * 512, th * 512 + 512)
                        pp = pf()
                        for kc in range(8):
                            S.mm(pp[:], wo[:, kc, j4 * 128:(j4 + 1) * 128], mrg[:, kc, ts_], start=(kc == 0), stop=(kc == 7))
                        S.stt(xT[:, j, ts_], pp[:], dcol(DV_G1H + j), xT[:, j, ts_], ALU.mult, ALU.add)
            dbgdump("x1", xT[:], (128, 8, T))
            prefetch(I["w1"][:, :, 0:512], (128, 8, 512))
            prefetch(I["w3"][:, :, 0:512], (128, 8, 512))
            rmsnorm_mod(xT, DV_GS2, 24, tmpA, sqb, rstd)
            S.barrier()
            S.flush()
            ph.close()

            with ExitStack() as ph2:
                actb = sbt(ph2, "actb", (128, NFT, T), BF16)
                sl = [sbt(ph2, "sl%d" % i, (128, 512), F32) for i in range(3)]
                yo = [sbt(ph2, "yo%d" % i, (128, 512), F32) for i in range(2)]
                ti = 0
                for g in range(6):
                    nt = min(4, NFT - g * 4)
                    wa = wload(I["w1"][:, :, g * 512:g * 512 + nt * 128], (128, 8, nt * 128))
                    wb = wload(I["w3"][:, :, g * 512:g * 512 + nt * 128], (128, 8, nt * 128))
                    for f4 in range(nt):
                        f = g * 4 + f4
                        for th in range(2):
                            ts_ = slice(th * 512, th * 512 + 512)
                            pa, pb_ = pf(), pf()
                            for kc in range(8):
                                S.mm(pa[:], wa[:, kc, f4 * 128:(f4 + 1) * 128], hT[:, kc, ts_], start=(kc == 0), stop=(kc == 7))
                            for kc in range(8):
                                S.mm(pb_[:], wb[:, kc, f4 * 128:(f4 + 1) * 128], hT[:, kc, ts_], start=(kc == 0), stop=(kc == 7))
                            s_ = sl[ti % 3]
                            ti += 1
                            S.act(s_[:], pa[:], AF.Silu)
                            S.tt("dve", actb[:, f, ts_], s_[:], pb_[:], ALU.mult)
                oi = 0
                for j in range(8):
                    w2 = wload(I["w2"][:, :, j * 128:(j + 1) * 128], (128, NFT, 128))
                    for th in range(2):
                        ts_ = slice(th * 512, th * 512 + 512)
                        pp = pf()
                        for f in range(NFT):
                            S.mm(pp[:], w2[:, f, :], actb[:, f, ts_], start=(f == 0), stop=(f == NFT - 1))
                        y_ = yo[oi % 2]
                        oi += 1
                        S.stt(y_[:], pp[:], mod[:, 40 + j:41 + j], xT[:, j, ts_], ALU.mult, ALU.add)
                        S.dma("sp", O["yT"][:, j, ts_], y_[:])
                S.barrier()
                S.flush()
    return nc, dbg_out


def _fm(w):
    K, N = w.shape
    return np.ascontiguousarray(w.reshape(K // 128, 128, N).transpose(1, 0, 2))


def _consts():
    cst = np.zeros((128, NCB), np.float32)
    p = np.arange(128)
    cst[:, CB_ID:CB_ID + 128] = np.eye(128, dtype=np.float32)
    cst[:, CB_ONE:CB_ONE + 128] = 1.0
    cst[:, CB_BO:CB_BO + 128] = (p[:, None] // 64 == p[None, :] // 64)
    i = (p % 64)[:, None]
    f = np.arange(64)[None, :]
    cst[:, CB_LT:CB_LT + 64] = i < f
    cst[:, CB_LE:CB_LE + 64] = i <= f
    cst[:, CB_GT:CB_GT + 64] = i > f
    cst[:, CB_GE:CB_GE + 64] = i >= f
    cst[:, CB_I64:CB_I64 + 64] = i == f
    sm = np.ones((128, T), np.float32)
    sm[:, ::64] = 0.0
    return cst, sm


def _bias_tables(na_rpb):
    GW, ROWS, NR, NCOL = 64, 16, 8, 16
    t = np.arange(T)
    row, col = t // GW, t % GW
    row_start = np.clip(np.arange(ROWS) - NR // 2, 0, ROWS - NR)
    win_start = np.clip(np.arange(GW) - NCOL // 2, 0, GW - NCOL)
    qr, qc = row[:, None], col[:, None]
    kr, kc = row[None, :], col[None, :]
    valid = (kr >= row_start[qr]) & (kr < row_start[qr] + NR) & (kc >= win_start[qc]) & (kc < win_start[qc] + NCOL)
    dr = np.clip(kr - qr + NR - 1, 0, 2 * NR - 2)
    dc = np.clip(kc - qc, -(NCOL - 1), NCOL - 1) + NCOL - 1
    Bs = np.where(valid[None], na_rpb[:, dr, dc], np.float32(NEGM)).astype(np.float32)
    same = (t[:, None] // 256) == (t[None, :] // 256)
    Bp = np.where(same, np.float32(0.0), np.float32(NEGM)).astype(np.float32)[None]

    def tile(B):
        out = np.empty((B.shape[0], 128, NBLK, 128), np.float32)
        for (j, R), b in BLK.items():
            out[:, :, b, :] = B[:, R * 128:(R + 1) * 128, j * 128:(j + 1) * 128].transpose(0, 2, 1)
        return out
    ts_ = tile(Bs)
    tp = np.ascontiguousarray(np.broadcast_to(tile(Bp), (8, 128, NBLK, 128)))
    return tp, ts_


_CACHE = {}


def _get_nc():
    if "nc" not in _CACHE:
        _CACHE["nc"] = build_nc()[0]
    return _CACHE["nc"]


def _prep(inp):
    f = lambda a: np.ascontiguousarray(np.asarray(a, dtype=np.float32))
    w_in = f(inp["w_in"])[0]
    perm = list(range(1536, 1920))
    for rd in range(4):
        for ti in range(3):
            perm += list(range(ti * 512 + rd * 128, ti * 512 + rd * 128 + 128))
    perm += list(range(1920, 5504))
    perm = np.array(perm)
    otile = [12, 13, 14] + [ti * 4 + rd for rd in range(4) for ti in range(3)]
    mu = f(inp["shift_mu"])[0]
    pv = np.zeros((128, NPV), np.float32)
    col = lambda v: np.ascontiguousarray(v.reshape(-1, 128).T)
    pv[:, PV_N1:PV_N1 + 8] = col(f(inp["norm1_g"])[0])
    pv[:, PV_N2:PV_N2 + 8] = col(f(inp["norm2_g"])[0])
    pv[:, PV_BADA:PV_BADA + 48] = col(f(inp["b_ada"])[0])
    pv[:, PV_MU0:PV_MU0 + 15] = col(mu[0])[:, otile]
    pv[:, PV_MU1:PV_MU1 + 15] = col(mu[1])[:, otile]
    pv[:, PV_KK:PV_KK + 4] = col(f(inp["rw_k_k"])[0])
    pv[:, PV_KA:PV_KA + 4] = col(f(inp["rw_k_a"])[0])
    pv[:, PV_RK:PV_RK + 4] = col(f(inp["rw_r_k"])[0].reshape(-1))
    pv[:, PV_W0:PV_W0 + 4] = col(f(inp["rw_w0"])[0, 0])
    pv[:, PV_W0 + 4:PV_W0 + 8] = col(f(inp["rw_w0"])[0, 1])
    pv[:, PV_A0:PV_A0 + 4] = col(f(inp["rw_a0"])[0, 0])
    pv[:, PV_A0 + 4:PV_A0 + 8] = col(f(inp["rw_a0"])[0, 1])
    pv[:, PV_QG] = np.tile(f(inp["na_q_g"])[0], 2)
    pv[:, PV_KG] = np.tile(f(inp["na_k_g"])[0], 2)
    lnp = np.stack([f(inp["rw_ln_g"])[0].reshape(4, 2, 64), f(inp["rw_ln_b"])[0].reshape(4, 2, 64)])
    lnp = np.ascontiguousarray(np.repeat(lnp.transpose(2, 0, 1, 3), 64, axis=0))
    cst, sm = _consts()
    tp, ts_ = _bias_tables(f(inp["na_rpb"])[0])
    common = {
        "cstb": cst, "smask": sm, "pv": pv, "lnp": lnp,
        "wada": _fm(f(inp["w_ada"])[0]), "win": _fm(w_in[:, perm]),
        "wup": f(inp["rw_w_up"])[0].reshape(128, 512), "aup": f(inp["rw_a_up"])[0].reshape(128, 512),
        "gup": f(inp["rw_g_up"])[0], "wor": _fm(f(inp["w_o_rwkv"])[0]), "won": _fm(f(inp["w_o_na"])[0]),
        "wout": _fm(f(inp["w_out"])[0]), "w1": _fm(f(inp["ffn_w1"])[0]), "w3": _fm(f(inp["ffn_w3"])[0]),
        "w2": _fm(f(inp["ffn_w2"])[0]),
    }
    xp, xs = f(inp["x_prompt"]), f(inp["x_sample"])
    st, ck, cv_ = f(inp["state_rwkv"]), f(inp["cache_na_k"]), f(inp["cache_na_v"])
    c, cctx = f(inp["c"]), f(inp["c_ctx"])
    maps = []
    for u in range(8):
        m = dict(common)
        if u < 4:
            X = xp[4 * u:4 * u + 4].reshape(T, D)
            cvec = cctx
            m["fl"] = np.tile(np.array([1.0, NEGM, 0.0, 0.0], np.float32), (128, 1))
            m["s0T"] = np.zeros((128, 4, 2, 64), np.float32)
            m["ckT"] = np.zeros((128, 4, 256), np.float32)
            m["cvv"] = np.zeros((128, 2, 512), np.float32)
            m["tbl"] = tp
        else:
            b = (u - 4) % 2
            X = xs[b]
            cvec = c[b]
            m["fl"] = np.tile(np.array([0.0, 0.0, 1.0, 0.0], np.float32), (128, 1))
            s = st[b, 0].reshape(2, 4, 2, 64, 64)
            m["s0T"] = np.ascontiguousarray(s.transpose(2, 4, 1, 0, 3).reshape(128, 4, 2, 64))
            kk_ = ck[b, 0].reshape(256, 4, 2, 64)
            m["ckT"] = np.ascontiguousarray(kk_.transpose(2, 3, 1, 0).reshape(128, 4, 256))
            m["cvv"] = np.ascontiguousarray(cv_[b, 0].reshape(2, 128, 512).transpose(1, 0, 2))
            m["tbl"] = ts_
        m["xT"] = _fm(np.ascontiguousarray(X.T))
        m["cv"] = np.ascontiguousarray(cvec.reshape(8, 128).T)
        maps.append(m)
    return maps


def _unfm(a):
    return a.transpose(1, 0, 2).reshape(-1, a.shape[2])


def kernel(**inp):
    maps = _prep(inp)
    nc = _get_nc()
    res = run_bass_kernel_spmd(nc, maps, core_ids=list(range(8)))
    R = res.results
    y_p = np.empty((16, 256, D), np.float32)
    y_s = np.empty((2, 1024, D), np.float32)
    s_new = np.empty((16, 1, 2, 8, 64, 64), np.float32)
    k_new = np.empty((16, 1, 256, 8, 64), np.float32)
    v_new = np.empty((16, 1, 256, 8, 64), np.float32)
    for u in range(6):
        Y = _unfm(np.asarray(R[u]["yT"])).T
        if u < 4:
            y_p[4 * u:4 * u + 4] = Y.reshape(4, 256, D)
            oS = np.asarray(R[u]["oS"]).reshape(2, 64, 4, 2, 4, 64)
            s_new[4 * u:4 * u + 4, 0] = oS.transpose(4, 3, 2, 0, 5, 1).reshape(4, 2, 8, 64, 64)
            okT = np.asarray(R[u]["okT"]).reshape(2, 64, 4, T)
            k_new[4 * u:4 * u + 4, 0] = okT.transpose(3, 2, 0, 1).reshape(4, 256, 8, 64)
            ov = np.asarray(R[u]["ov"]).transpose(1, 0, 2).reshape(T, 512)
            v_new[4 * u:4 * u + 4, 0] = ov.reshape(4, 256, 8, 64)
        else:
            y_s[u - 4] = Y
    return y_p, y_s, s_new, k_new, v_new
```

```python
import numpy as np
from contextlib import ExitStack
import concourse.bass as bass
import concourse.mybir as mybir
from concourse.bass_utils import run_bass_kernel_spmd

F32 = mybir.dt.float32
BF16 = mybir.dt.bfloat16
ALU = mybir.AluOpType
AF = mybir.ActivationFunctionType
AX = mybir.AxisListType

D = 1024
T = 1024
NCH = 16
FF = 2816
NFT = 22
RMS_EPS = 1e-6
GN_EPS = 64e-5
L2_EPS = 1e-12
NEGM = -30000.0
LWC = -0.30326533
NDMA = 24
NSLOT = 4

PV_N1, PV_N2, PV_BADA, PV_MU0, PV_MU1 = 0, 8, 16, 64, 79
PV_KK, PV_KA, PV_RK, PV_W0, PV_A0, PV_QG, PV_KG, NPV = 94, 98, 102, 106, 114, 122, 123, 124
CB_ID, CB_ONE, CB_BO, CB_LT, CB_LE, CB_GT, CB_GE, CB_I64, NCB = 0, 128, 256, 384, 448, 512, 576, 640, 704

def _qwin(R):
    return min(max(R - 2, 0), 3)
QL = [[R for R in range(8) if _qwin(R) <= j < _qwin(R) + 5] for j in range(8)]
BLK = {}
_b = 0
for _j in range(8):
    for _R in QL[_j]:
        BLK[(_j, _R)] = _b
        _b += 1
NBLK = _b


class Sched:
    def __init__(self, nc, stack):
        self.nc = nc
        self.ce = ["pe", "act", "dve", "pool"]
        self.qe = ["sp"]
        self.sem = {e: stack.enter_context(nc.semaphore("s_" + e)) for e in self.ce + self.qe}
        self.cnt = {e: 0 for e in self.ce + self.qe}
        self.dsem = [stack.enter_context(nc.semaphore("d%d" % i)) for i in range(NDMA)]
        self.dcnt = [0] * NDMA
        self.drr = {"sp": 0, "pool": 0}
        self.ops = {e: [] for e in self.ce + self.qe}
        self.res = {}
        self.waited = {e: {} for e in self.ce + self.qe}
        self.nops = 0
        self.pending_dma = []
        self.same = {e: {} for e in self.ce}

    def _semobj(self, sk):
        return self.sem[sk[1]] if sk[0] == "eng" else self.dsem[sk[1]]

    def _key(self, x):
        if isinstance(x, (str, tuple)):
            return x
        return x.tensor.name if hasattr(x, "tensor") else x.name

    def _deps(self, reads, writes):
        raw, other = [], []
        for r in reads:
            st = self.res.get(self._key(r))
            if st and st["w"]:
                raw.append(st["w"])
        for w in writes:
            st = self.res.get(self._key(w))
            if st:
                if st["w"]:
                    other.append(st["w"])
                other.extend(st["r"].values())
        return raw, other

    def _mark(self, reads, writes, ev, rk):
        for r in reads:
            st = self.res.setdefault(self._key(r), {"w": None, "r": {}})
            st["r"][rk] = ev
        for w in writes:
            self.res[self._key(w)] = {"w": ev, "r": {}}

    def _waits(self, eng, deps):
        if isinstance(deps, tuple):
            raw, other = deps
            deps = list(raw) + [d_ for d_ in other if d_[0] != ("eng", eng)]
        best = {}
        for sk, v in deps:
            if sk == ("eng", eng) and eng == "pe":
                continue
            if self.waited[eng].get(sk, 0) >= v:
                continue
            if best.get(sk, 0) < v:
                best[sk] = v
        for sk, v in best.items():
            self.waited[eng][sk] = v
        return list(best.items())

    @staticmethod
    def _box(x):
        if isinstance(x, (str, tuple)) or not hasattr(x, "ap"):
            return None
        try:
            ap = x.ap
            p0 = x.base_partition()
            off = int(x.offset) - p0 * ap[0][0]
            lo = hi = off
            for st_, n_ in ap[1:]:
                ext = st_ * (n_ - 1)
                if ext < 0:
                    lo += ext
                else:
                    hi += ext
            return (p0, p0 + ap[0][1], lo, hi + 1)
        except Exception:
            return None

    @staticmethod
    def _ovl(a, b):
        if a is None or b is None:
            return True
        return a[0] < b[1] and b[0] < a[1] and a[2] < b[3] and b[2] < a[3]

    def _same_deps(self, eng, reads, writes):
        deps = []
        hist = self.same[eng]
        for lst, isw in ((reads, False), (writes, True)):
            for x in lst:
                h = hist.get(self._key(x))
                if not h:
                    continue
                b = self._box(x)
                for b2, cnt2, w2 in h:
                    if (isw or w2) and self._ovl(b, b2):
                        deps.append((("eng", eng), cnt2))
        return deps

    def _same_record(self, eng, reads, writes, cnt):
        hist = self.same[eng]
        for lst, isw in ((reads, False), (writes, True)):
            for x in lst:
                h = hist.setdefault(self._key(x), [])
                h.append((self._box(x), cnt, isw))
                if len(h) > 64:
                    del h[0]

    def _psum_excl(self, r, w):
        r2, w2 = [], list(w)
        for x in r:
            k = self._key(x)
            if isinstance(k, str) and (k.startswith("pf") or k.startswith("pb")):
                w2.append(x)
            else:
                r2.append(x)
        return r2, w2

    def op(self, eng, fn, r=(), w=()):
        r0, w0 = list(r), list(w)
        r, w = self._psum_excl(r, w)
        raw_, oth_ = self._deps(r, w)
        own = ("eng", eng)
        deps = [d_ for d_ in list(raw_) + list(oth_) if d_[0] != own]
        if eng != "pe":
            deps += self._same_deps(eng, r0, w0)
        waits = self._waits(eng, deps)
        self.cnt[eng] += 1
        if eng != "pe":
            self._same_record(eng, r0, w0, self.cnt[eng])
        ev = (("eng", eng), self.cnt[eng])
        self.ops[eng].append((fn, waits, (ev[0], 1)))
        self._mark(r, w, ev, eng)
        self.nops += 1

    def dma(self, q, out, in_, r=None, w=None, track=True):
        r = [in_] if r is None else r
        w = [out] if w is None else w
        half_ = NDMA // 2
        base = 0 if q == "sp" else half_
        i = base + self.drr[q] % half_
        self.drr[q] += 1
        raw_, oth_ = self._deps(r, w)
        deps = list(raw_) + list(oth_)
        if self.dcnt[i] > 0:
            deps.append((("dma", i), self.dcnt[i]))
        waits = self._waits(q, deps)
        self.dcnt[i] += 16
        ev = (("dma", i), self.dcnt[i])

        def fn(e, out=out, in_=in_):
            return e.dma_start(out=out, in_=in_)

        self.ops[q].append((fn, waits, (ev[0], 16)))
        self._mark(r, w, ev, ("dma", i))
        if track:
            self.pending_dma.append(ev)
        self.nops += 1
        return ev

    def barrier(self):
        waits = self._waits("sp", list(self.pending_dma))
        self.pending_dma = []
        self.cnt["sp"] += 1
        spv = self.cnt["sp"]
        self.ops["sp"].append(("seminc", waits, None))
        tgt = [(("eng", e), self.cnt[e]) for e in self.ce if self.cnt[e] > 0] + [(("eng", "sp"), spv)]
        for e in self.ce + self.qe:
            ws = self._waits(e, [t for t in tgt if t[0] != ("eng", e)])
            self.ops[e].append((None, ws, None))

    def flush(self):
        nc = self.nc
        ops = self.ops
        self.ops = {e: [] for e in self.ce + self.qe}
        with nc.Block() as block:
            def emit(engname):
                def body(e):
                    for fn, waits, inc in ops[engname]:
                        for sk, v in waits:
                            e.wait_ge(self._semobj(sk), v)
                        if fn is None:
                            continue
                        if fn == "seminc":
                            e.sem_inc(self.sem["sp"], 1)
                            continue
                        ins = fn(e)
                        if inc is not None:
                            ins.then_inc(self._semobj(inc[0]), inc[1])
                return body
            block.tensor(emit("pe"))
            block.scalar(emit("act"))
            block.vector(emit("dve"))
            block.gpsimd(emit("pool"))
            block.sync(emit("sp"))

    def mm(self, out, lhsT, rhs, start=True, stop=True):
        self.op("pe", lambda e: e.matmul(out, lhsT, rhs, start=start, stop=stop), r=[lhsT, rhs], w=[out])

    def tr(self, out, in_, ident):
        self.op("pe", lambda e: e.transpose(out, in_, ident), r=[in_, ident], w=[out])

    def act(self, out, in_, func, bias=None, scale=None):
        kw = {}
        rr = [in_]
        if bias is not None:
            kw["bias"] = bias
            if not isinstance(bias, (int, float)):
                rr.append(bias)
        if scale is not None:
            kw["scale"] = scale
            if not isinstance(scale, (int, float)):
                rr.append(scale)
        self.op("act", lambda e: e.activation(out, in_, func, **kw), r=rr, w=[out])

    def tt(self, eng, out, in0, in1, op):
        self.op(eng, lambda e: e.tensor_tensor(out, in0, in1, op), r=[in0, in1], w=[out])

    def ts(self, eng, out, in0, s1, op0, s2=None, op1=None):
        rr = [in0] + [s for s in (s1, s2) if s is not None and not isinstance(s, (int, float))]
        kw = {}
        if op1 is not None:
            kw["op1"] = op1
        self.op(eng, lambda e: e.tensor_scalar(out, in0, s1, s2, op0, **kw), r=rr, w=[out])

    def stt(self, out, in0, scalar, in1, op0, op1):
        rr = [in0, in1] + ([scalar] if not isinstance(scalar, (int, float)) else [])
        self.op("dve", lambda e: e.scalar_tensor_tensor(out, in0, scalar, in1, op0, op1), r=rr, w=[out])

    def copy(self, eng, out, in_):
        if eng == "act":
            self.op("act", lambda e: e.activation(out, in_, AF.Identity), r=[in_], w=[out])
        else:
            self.op(eng, lambda e: e.tensor_copy(out, in_), r=[in_], w=[out])

    def memset(self, eng, out, val):
        self.op(eng, lambda e: e.memset(out, val), r=[], w=[out])

    def reduce(self, out, in_, op, axis=AX.X):
        self.op("dve", lambda e: e.tensor_reduce(out, in_, axis, op), r=[in_], w=[out])

    def recip(self, out, in_):
        self.op("dve", lambda e: e.reciprocal(out, in_), r=[in_], w=[out])

    def scan(self, out, d0, d1, init, op0, op1):
        self.op("dve", lambda e: e.tensor_tensor_scan(out, d0, d1, init, op0, op1), r=[d0, d1], w=[out])


def _prod(xs):
    p = 1
    for x in xs:
        p *= x
    return p


def build_nc(dbg=(), stop=None):
    nc = bass.Bass("TRN2", target_bir_lowering=False)

    def din(name, shape):
        return nc.dram_tensor(name, list(shape), F32, kind="ExternalInput").ap()

    def dout(name, shape):
        return nc.dram_tensor(name, list(shape), F32, kind="ExternalOutput").ap()

    I = {}
    for name, shape in [("xT", (128, 8, T)), ("cv", (128, 8)), ("fl", (128, 4)), ("s0T", (128, 4, 2, 64)),
                        ("ckT", (128, 4, 256)), ("cvv", (128, 2, 512)), ("tbl", (8, 128, NBLK, 128)),
                        ("cstb", (128, NCB)), ("smask", (128, T)), ("pv", (128, NPV)), ("lnp", (128, 2, 4, 64)),
                        ("wada", (128, 8, 6 * D)), ("win", (128, 8, 5504)), ("wup", (128, 512)),
                        ("aup", (128, 512)), ("gup", (128, 512)), ("wor", (128, 4, D)), ("won", (128, 4, D)),
                        ("wout", (128, 8, D)), ("w1", (128, 8, FF)), ("w3", (128, 8, FF)), ("w2", (128, NFT, D))]:
        I[name] = din(name, shape)
    O = {}
    for name, shape in [("yT", (128, 8, T)), ("oS", (128, 4, 2, 4, 64)), ("okT", (128, 4, T)), ("ov", (128, 8, 512))]:
        O[name] = dout(name, shape)
    dbg_out = {}

    with ExitStack() as top:
        S = Sched(nc, top)

        def sbt(stack, name, shape, dt):
            return stack.enter_context(nc.sbuf_tensor("sb_" + name, list(shape), dt))

        slots = [sbt(top, "ws%d" % i, (128, 4096), BF16) for i in range(NSLOT)]
        cb = sbt(top, "cb", (128, NCB), BF16)
        pv = sbt(top, "pvt", (128, NPV), F32)
        fl = sbt(top, "flt", (128, 4), F32)
        dv = sbt(top, "dv", (128, 104), F32)
        mod = sbt(top, "mod", (128, 48), F32)
        hT = sbt(top, "hT", (128, 8, T), BF16)
        oT_na = sbt(top, "oTna", (128, 4, T), BF16)
        oT_rw = sbt(top, "oTrw", (128, 4, T), BF16)
        cvb = sbt(top, "cvb", (128, 8), BF16)
        NPF = 6
        PF = [top.enter_context(nc.psum_tensor("pf%d" % i, [128, 512], F32)) for i in range(NPF)]
        PBs = [top.enter_context(nc.psum_tensor("pb%d" % i, [128, 1024], BF16)) for i in range(2)]
        PB = PBs[0]
        pfi = [0]

        live = set()

        def pf():
            for _ in range(8):
                t = PF[pfi[0] % NPF]
                pfi[0] += 1
                if t.name not in live:
                    return t
            raise RuntimeError("no free psum bank")

        wsi = [0]

        wq = []

        def _wload(src, shape, track):
            slot = slots[wsi[0] % NSLOT]
            wsi[0] += 1
            n = _prod(shape[1:])
            v = slot[:, 0:n]
            if len(shape) == 3:
                v = v.rearrange("p (a b) -> p a b", a=shape[1], b=shape[2])
            S.dma("pool", v, src, track=track)
            return v

        def prefetch(src, shape):
            wq.append((repr(src), tuple(shape), _wload(src, shape, False)))

        def wload(src, shape):
            if wq:
                k, sh, v = wq.pop(0)
                assert k == repr(src) and sh == tuple(shape), (k, repr(src))
                return v
            return _wload(src, shape, True)

        def dbgdump(name, ap, shape, dt=F32):
            if name not in dbg:
                return
            d = nc.dram_tensor("dbg_" + name, list(shape), dt, kind="ExternalOutput").ap()
            dbg_out[name] = d
            S.dma("sp", d, ap)

        def chk(tag):
            if stop == tag:
                S.barrier()
                S.flush()
                return True
            return False

        ident = cb[:, CB_ID:CB_ID + 128]
        ones_b = cb[:, CB_ONE:CB_ONE + 128]
        bones = cb[:, CB_BO:CB_BO + 128]
        mLT = cb[:, CB_LT:CB_LT + 64]
        mLE = cb[:, CB_LE:CB_LE + 64]
        mGT = cb[:, CB_GT:CB_GT + 64]
        mGE = cb[:, CB_GE:CB_GE + 64]
        eye64 = cb[:, CB_I64:CB_I64 + 64]

        def bc8(m):
            return m.unsqueeze(1).to_broadcast([128, 8, 64])

        def pcol(i):
            return pv[:, i:i + 1]

        DV_C0, DV_NF0, DV_NF1, DV_HW0, DV_HA0, DV_OMKA, DV_RK05, DV_QG8 = 0, 15, 30, 45, 53, 61, 65, 69
        DV_GS1, DV_GS2, DV_G1H = 70, 78, 86
        DV_HKA, DV_OMHKA = 94, 98

        def dcol(i):
            return dv[:, i:i + 1]

        S.dma("pool", cb[:], I["cstb"][:])
        S.dma("sp", pv[:], I["pv"][:])
        S.dma("sp", fl[:], I["fl"][:])
        S.tt("dve", dv[:, DV_C0:DV_C0 + 15], pv[:, PV_MU0:PV_MU0 + 15], pv[:, PV_MU1:PV_MU1 + 15], ALU.add)
        S.ts("dve", dv[:, DV_C0:DV_C0 + 15], dv[:, DV_C0:DV_C0 + 15], -1.0, ALU.mult, 1.0, ALU.add)
        S.ts("dve", dv[:, DV_NF0:DV_NF0 + 15], pv[:, PV_MU0:PV_MU0 + 15], fl[:, 0:1], ALU.mult, -1.0, ALU.mult)
        S.ts("dve", dv[:, DV_NF1:DV_NF1 + 15], pv[:, PV_MU1:PV_MU1 + 15], fl[:, 0:1], ALU.mult, -1.0, ALU.mult)
        S.ts("dve", dv[:, DV_HW0:DV_HW0 + 8], pv[:, PV_W0:PV_W0 + 8], 0.5, ALU.mult)
        S.ts("dve", dv[:, DV_HA0:DV_HA0 + 8], pv[:, PV_A0:PV_A0 + 8], 0.5, ALU.mult)
        S.ts("dve", dv[:, DV_OMKA:DV_OMKA + 4], pv[:, PV_KA:PV_KA + 4], -1.0, ALU.mult, 1.0, ALU.add)
        S.ts("dve", dv[:, DV_HKA:DV_HKA + 4], pv[:, PV_KA:PV_KA + 4], 0.5, ALU.mult)
        S.ts("dve", dv[:, DV_OMHKA:DV_OMHKA + 4], pv[:, PV_KA:PV_KA + 4], -0.5, ALU.mult, 1.0, ALU.add)
        S.ts("dve", dv[:, DV_RK05:DV_RK05 + 4], pv[:, PV_RK:PV_RK + 4], 0.5, ALU.mult)
        S.ts("dve", dv[:, DV_QG8:DV_QG8 + 1], pv[:, PV_QG:PV_QG + 1], 0.125, ALU.mult)

        def mod_group(g):
            pm = pf()
            w = wload(I["wada"][:, :, g * 512:(g + 1) * 512], (128, 8, 512))
            for j4 in range(4):
                for kc in range(8):
                    S.mm(pm[:, j4:j4 + 1], w[:, kc, j4 * 128:(j4 + 1) * 128], cvb[:, kc:kc + 1],
                         start=(kc == 0), stop=(kc == 7))
            S.tt("dve", mod[:, g * 4:g * 4 + 4], pm[:, 0:4], pv[:, PV_BADA + g * 4:PV_BADA + g * 4 + 4], ALU.add)

        def modulation(j0, j1):
            for g in range(j0 // 4, j1 // 4):
                mod_group(g)

        def rmsnorm_mod(xT, gs0, sh0, tmpA, sqb, rstd):
            pa, pb_ = pf(), pf()
            for kc in range(8):
                S.act(sqb[:, kc % 2, :], xT[:, kc, :], AF.Square)
                for th, pp in enumerate((pa, pb_)):
                    S.mm(pp[:], ones_b, sqb[:, kc % 2, th * 512:(th + 1) * 512], start=(kc == 0), stop=(kc == 7))
            for th, pp in enumerate((pa, pb_)):
                S.act(rstd[:, th * 512:(th + 1) * 512], pp[:], AF.Ln, scale=1.0 / D, bias=RMS_EPS)
            S.act(rstd[:], rstd[:], AF.Exp, scale=-0.5)
            for kc in range(8):
                S.tt("dve", tmpA[:, kc % 2, :], xT[:, kc, :], rstd[:], ALU.mult)
                S.act(hT[:, kc, :], tmpA[:, kc % 2, :], AF.Identity, scale=dv[:, gs0 + kc:gs0 + kc + 1],
                      bias=mod[:, sh0 + kc:sh0 + kc + 1])

        with ExitStack() as ph:
            xT = sbt(ph, "xTa", (128, 8, T), F32)
            cvf = sbt(ph, "cvf", (128, 8), F32)
            tmpA = sbt(ph, "tmpA", (128, 2, T), F32)
            sqb = sbt(ph, "sqb", (128, 2, T), BF16)
            rstd = sbt(ph, "rstd", (128, T), F32)
            S.dma("sp", cvf[:], I["cv"][:])
            S.dma("sp", xT[:], I["xT"][:])
            S.act(cvb[:], cvf[:], AF.Silu)
            modulation(0, 16)
            S.stt(dv[:, DV_GS1:DV_GS1 + 8], mod[:, 8:16], 1.0, pv[:, PV_N1:PV_N1 + 8], ALU.add, ALU.mult)
            rmsnorm_mod(xT, DV_GS1, 0, tmpA, sqb, rstd)
            dbgdump("hT", hT[:], (128, 8, T), BF16)
            prefetch(I["win"][:, :, 1920:1920 + 512], (128, 8, 512))
            prefetch(I["win"][:, :, 2432:2432 + 512], (128, 8, 512))
            S.barrier()
            S.flush()
            if stop == "A":
                return nc, dbg_out

        WC_LORA, WC_RD, WC_NQ, WC_NK, WC_NV, WC_GRW, WC_GNA = 0, 384, 1920, 2432, 2944, 3456, 4480

        with ExitStack() as ph:
            qT = sbt(ph, "qT", (128, 4, T), BF16)
            kT = sbt(ph, "kT", (128, 4, T + 256), BF16)
            vaug = sbt(ph, "vaug", (128, 10, 8, 66), BF16)
            ona = sbt(ph, "ona", (128, 8, 512), BF16)
            U = [sbt(ph, "naU%d" % i, (128, T), F32) for i in range(2)]
            sq = [sbt(ph, "nasq%d" % i, (128, T), BF16) for i in range(2)]
            rin = [sbt(ph, "narin%d" % i, (128, T), F32) for i in range(2)]
            kn = [sbt(ph, "nakn%d" % i, (128, T), F32) for i in range(2)]
            vf = [sbt(ph, "navf%d" % i, (128, 512), F32) for i in range(2)]
            tb = [sbt(ph, "natb%d" % i, (128, NBLK, 128), BF16) for i in range(2)]
            PT = [sbt(ph, "naPT%d" % i, (128, NBLK + 16, 128), BF16) for i in range(2)]
            lg = [sbt(ph, "nalg%d" % i, (128, 512), F32) for i in range(3)]
            rden = sbt(ph, "rden", (128, 8), F32)
            if chk("NA00"):
                return nc, dbg_out
            S.dma("pool", kT[:, :, T:T + 256], I["ckT"][:])
            if chk("NA01"):
                return nc, dbg_out
            cvst = sbt(ph, "cvst", (128, 2, 512), BF16)
            S.dma("pool", cvst[:], I["cvv"][:])
            for a_ in range(2):
                S.copy("dve", vaug[:, 8 + a_, :, 0:64], cvst[:, a_, :].rearrange("p (h d) -> p h d", d=64))
            S.memset("dve", vaug[:, :, :, 64:65], 1.0)
            if chk("NA0"):
                return nc, dbg_out
            for which, wc in ((0, WC_NQ), (1, WC_NK)):
                w = wload(I["win"][:, :, wc:wc + 512], (128, 8, 512))
                for rd in range(4):
                    b = rd % 2
                    pa, pb_ = pf(), pf()
                    for th, pp in enumerate((pa, pb_)):
                        for kc in range(8):
                            S.mm(pp[:], w[:, kc, rd * 128:(rd + 1) * 128], hT[:, kc, th * 512:(th + 1) * 512],
                                 start=(kc == 0), stop=(kc == 7))
                    for th, pp in enumerate((pa, pb_)):
                        S.act(sq[b][:, th * 512:(th + 1) * 512], pp[:], AF.Square)
                        S.copy("dve", U[b][:, th * 512:(th + 1) * 512], pp[:])
                    pc, pd = pf(), pf()
                    for th, pp in enumerate((pc, pd)):
                        S.mm(pp[:], bones, sq[b][:, th * 512:(th + 1) * 512])
                    for th, pp in enumerate((pc, pd)):
                        S.act(rin[b][:, th * 512:(th + 1) * 512], pp[:], AF.Ln, scale=1.0 / 64, bias=RMS_EPS)
                    S.act(rin[b][:], rin[b][:], AF.Exp, scale=-0.5)
                    if which == 0:
                        S.stt(qT[:, rd, :], U[b][:], dcol(DV_QG8), rin[b][:], ALU.mult, ALU.mult)
                    else:
                        S.stt(kn[b][:], U[b][:], pcol(PV_KG), rin[b][:], ALU.mult, ALU.mult)
                        S.copy("act", kT[:, rd, 0:T], kn[b][:])
                        S.dma("sp", O["okT"][:, rd, :], kn[b][:])
            if chk("NA1"):
                return nc, dbg_out
            w = wload(I["win"][:, :, WC_NV:WC_NV + 512], (128, 8, 512))
            for tt_ in range(8):
                pp = pf()
                for kc in range(8):
                    S.mm(pp[:], hT[:, kc, tt_ * 128:(tt_ + 1) * 128], w[:, kc, :], start=(kc == 0), stop=(kc == 7))
                S.copy("act", vf[tt_ % 2][:], pp[:])
                S.copy("dve", vaug[:, tt_, :, 0:64], pp[:].rearrange("p (h d) -> p h d", d=64))
                S.dma("sp", O["ov"][:, tt_, :], vf[tt_ % 2][:])
            dbgdump("qT", qT[:], (128, 4, T), BF16)
            dbgdump("kT", kT[:], (128, 4, T + 256), BF16)
            if chk("NA2"):
                return nc, dbg_out
            li = 0
            for h in range(8):
                rd, h2 = h // 2, h % 2
                pr = slice(64 * h2, 64 * h2 + 64)
                tbh = tb[h % 2]
                PTh = PT[h % 2]
                S.dma("pool", tbh[:], I["tbl"][h])
                for j in range(8):
                    Rs = QL[j]
                    q0, nq = Rs[0] * 128, len(Rs) * 128
                    b0 = BLK[(j, Rs[0])]
                    for off in range(0, nq, 512):
                        n = min(512, nq - off)
                        pp = pf()
                        S.mm(pp[:, 0:n], kT[pr, rd, j * 128:(j + 1) * 128], qT[pr, rd, q0 + off:q0 + off + n])
                        bb = b0 + off // 128
                        l = lg[li % 3]
                        li += 1
                        S.tt("dve", l[:, 0:n], pp[:, 0:n],
                             tbh[:, bb:bb + n // 128, :].rearrange("p a b -> p (a b)"), ALU.add)
                        S.act(PTh[:, bb:bb + n // 128, :].rearrange("p a b -> p (a b)"), l[:, 0:n], AF.Exp)
                for jc in range(2):
                    for off in range(0, T, 512):
                        pp = pf()
                        S.mm(pp[:], kT[pr, rd, T + jc * 128:T + (jc + 1) * 128], qT[pr, rd, off:off + 512])
                        bb = NBLK + jc * 8 + off // 128
                        S.act(PTh[:, bb:bb + 4, :].rearrange("p a b -> p (a b)"), pp[:], AF.Exp, bias=fl[:, 1:2])
                for half in range(2):
                    po = pf()
                    for r4 in range(4):
                        R = half * 4 + r4
                        w0 = _qwin(R)
                        keys = [(jj, BLK[(jj, R)]) for jj in range(w0, w0 + 5)] + \
                               [(8 + jc, NBLK + jc * 8 + R) for jc in range(2)]
                        for n_, (vt, bb) in enumerate(keys):
                            S.mm(po[:, r4 * 65:(r4 + 1) * 65], PTh[:, bb, :], vaug[:, vt, h, 0:65],
                                 start=(n_ == 0), stop=(n_ == len(keys) - 1))
                    pov = po[:, 0:260].rearrange("p (r c) -> p r c", c=65)
                    S.recip(rden[:, half * 4:(half + 1) * 4], pov[:, :, 64])
                    S.tt("dve", ona[:, half * 4:(half + 1) * 4, h * 64:(h + 1) * 64], pov[:, :, 0:64],
                         rden[:, half * 4:(half + 1) * 4].unsqueeze(2).to_broadcast([128, 4, 64]), ALU.mult)
            if chk("NA3"):
                return nc, dbg_out
            for ct in range(4):
                for R in range(8):
                    S.tr(PB[:, R * 128:(R + 1) * 128], ona[:, R, ct * 128:(ct + 1) * 128], ident)
                S.copy("act", oT_na[:, ct, :], PB[:])
            dbgdump("oTna", oT_na[:], (128, 4, T), BF16)
            prefetch(I["win"][:, :, WC_LORA:WC_LORA + 384], (128, 8, 384))
            prefetch(I["win"][:, :, WC_RD:WC_RD + 384], (128, 8, 384))
            S.barrier()
            S.flush()
            if stop == "NA":
                return nc, dbg_out

        with ExitStack() as ph:
            twd = sbt(ph, "twd", (128, T), BF16)
            adb = sbt(ph, "adb", (128, T), BF16)
            sgd = sbt(ph, "sgd", (128, T), BF16)
            wupb = sbt(ph, "wupb", (128, 512), BF16)
            aupb = sbt(ph, "aupb", (128, 512), BF16)
            gupb = sbt(ph, "gupb", (128, 512), BF16)
            lnp = [sbt(ph, "lnpt%d" % i, (128, 2, 64), F32) for i in range(2)]
            smask = sbt(ph, "smaskt", (128, 512), F32)
            s0r = [sbt(ph, "s0r%d" % i, (128, 2, 64), F32) for i in range(2)]
            bon = [sbt(ph, "bon%d" % i, (128, 16), F32) for i in range(2)]
            Uraw = sbt(ph, "Uraw", (128, T), F32)
            Rp = sbt(ph, "Rp", (128, T), F32)
            Kp = sbt(ph, "Kp", (128, T), F32)
            VKD = sbt(ph, "VKD", (128, T), F32)
            Vp = VKD
            KD = VKD[:, 0:512]
            Dx = VKD[:, 512:1024]
            Vb = [sbt(ph, "Vb0", (128, T), BF16)] * 2
            Vtm = [sbt(ph, "Vtm%d" % i, (128, 16, 64), BF16) for i in range(2)]
            gfm = [sbt(ph, "gfm%d" % i, (128, T), BF16) for i in range(2)]
            kkn = sbt(ph, "kkn", (128, 512), F32)
            sqk = sbt(ph, "sqk", (128, 512), BF16)
            LW = sbt(ph, "LW", (128, 512), F32)
            C = sbt(ph, "Ccs", (128, 512), F32)
            IC = sbt(ph, "IC", (128, 512), F32)
            E0 = sbt(ph, "E0", (128, 512), F32)
            PBh = [sbt(ph, "PBh%d" % d, (128, 512), BF16) for d in range(2)]
            AR = [[sbt(ph, "AR%d%d" % (h, d), (128, 8, 2, 64), BF16) for d in range(2)] for h in range(2)]
            BK = [[sbt(ph, "BK%d%d" % (h, d), (128, 8, 2, 64), BF16) for d in range(2)] for h in range(2)]
            AA = [[sbt(ph, "AA%d%d" % (h, d), (128, 8, 2, 64), BF16) for d in range(2)] for h in range(2)]
            AB = [[sbt(ph, "AB%d%d" % (h, d), (128, 8, 2, 64), BF16) for d in range(2)] for h in range(2)]
            KTM = [[sbt(ph, "KTM%d%d" % (h, d), (128, 8, 64), BF16) for d in range(2)] for h in range(2)]
            PN = [[sbt(ph, "PN%d%d" % (h, d), (128, 8, 2, 64), BF16) for d in range(2)] for h in range(2)]
            NT = [[sbt(ph, "NT%d%d" % (h, d), (128, 8, 64), BF16) for d in range(2)] for h in range(2)]
            AV = [[sbt(ph, "AV%d%d" % (h, d), (128, 8, 2, 64), BF16) for d in range(2)] for h in range(2)]
            Rh = sbt(ph, "Rh", (128, 2, 16, 64), BF16)
            G0 = sbt(ph, "G0", (128, 2, 16, 64), BF16)
            Avv = sbt(ph, "Avv", (128, 2, 16, 64), BF16)
            H0 = sbt(ph, "H0", (128, 2, 16, 64), BF16)
            Sall = [[sbt(ph, "S%d_%d" % (d, i), (128, 64), BF16) for i in range(17)] for d in range(2)]
            SF = sbt(ph, "SF", (128, 2, 4, 64), F32)
            wend = sbt(ph, "wend", (128, 2, 16), F32)
            wendc = sbt(ph, "wendc", (128, 2, 16), F32)
            wk = sbt(ph, "wk", (128, 2, 16), F32)
            Yf = sbt(ph, "Yf", (128, 16, 64), F32)
            Yq = sbt(ph, "Yq", (128, 16, 64), BF16)
            otm = sbt(ph, "otm", (128, 16, 64), BF16)
            st4 = sbt(ph, "st4", (128, 6, 16), F32)

            S.dma("pool", wupb[:], I["wup"][:])
            S.dma("pool", aupb[:], I["aup"][:])
            S.dma("pool", gupb[:], I["gup"][:])
            S.dma("sp", smask[:], I["smask"][:, 0:512])

            def inproj_shift(w, ti, q, dest):
                pa, pb_ = pf(), pf()
                for th, pp in enumerate((pa, pb_)):
                    for kc in range(8):
                        S.mm(pp[:], w[:, kc, ti * 128:(ti + 1) * 128], hT[:, kc, th * 512:(th + 1) * 512],
                             start=(kc == 0), stop=(kc == 7))
                for th, pp in enumerate((pa, pb_)):
                    S.copy("act", Uraw[:, th * 512:(th + 1) * 512], pp[:])
                    S.ts("dve", dest[:, th * 512:(th + 1) * 512], pp[:], dcol(DV_C0 + q), ALU.mult)
                S.stt(dest[:, 1:T], Uraw[:, 0:T - 1], pcol(PV_MU0 + q), dest[:, 1:T], ALU.mult, ALU.add)
                S.stt(dest[:, 0:T - 1], Uraw[:, 1:T], pcol(PV_MU1 + q), dest[:, 0:T - 1], ALU.mult, ALU.add)
                S.stt(dest[:, 256:T:256], Uraw[:, 255:T - 1:256], dcol(DV_NF0 + q), dest[:, 256:T:256],
                      ALU.mult, ALU.add)
                S.stt(dest[:, 255:T - 1:256], Uraw[:, 256:T:256], dcol(DV_NF1 + q), dest[:, 255:T - 1:256],
                      ALU.mult, ALU.add)

            w = wload(I["win"][:, :, WC_LORA:WC_LORA + 384], (128, 8, 384))
            inproj_shift(w, 0, 0, Rp)
            S.act(twd[:], Rp[:], AF.Tanh)
            inproj_shift(w, 1, 1, Kp)
            S.copy("act", adb[:], Kp[:])
            inproj_shift(w, 2, 2, Vp)
            S.act(Yf[:].rearrange("p c v -> p (c v)"), Vp[:], AF.Tanh, scale=0.5)
            S.ts("dve", sgd[:], Yf[:].rearrange("p c v -> p (c v)"), 0.5, ALU.mult, 0.5, ALU.add)

            def v3(ap):
                return ap.rearrange("p (c t) -> p c t", t=64)

            mstrict = (mLT, mGT)
            mincl = (mLE, mGE)
            mstrictT = (mGT, mLT)

            def bc4(m):
                return m.unsqueeze(1).to_broadcast([128, 4, 64])

            def pv4(p):
                return p[:].rearrange("p (c x) -> p c x", x=128)

            def f128(ap):
                return ap.rearrange("p a b -> p (a b)")

            def pfl():
                t = pf()
                live.add(t.name)
                return t

            def rel(banks):
                for t in banks:
                    live.discard(t.name)

            def run(*gens):
                gens = list(gens)
                while gens:
                    for g_ in list(gens):
                        try:
                            next(g_)
                        except StopIteration:
                            gens.remove(g_)

            def head(rd):
                p = rd % 2
                S.dma("sp", lnp[p][:], I["lnp"][:, :, rd, :])
                S.dma("sp", s0r[p][:], I["s0T"][:, rd, :, :])
                w = wload(I["win"][:, :, WC_RD + rd * 384:WC_RD + (rd + 1) * 384], (128, 8, 384))
                for ti, dest in enumerate((Rp, Kp, Vp)):
                    inproj_shift(w, ti, 3 + rd * 3 + ti, dest)
                    yield
                mod_group(4 + 2 * rd)
                mod_group(5 + 2 * rd)
                yield
                S.copy("act", Vb[p][:], Vp[:])
                for c in range(16):
                    for h2 in range(2):
                        pr = slice(64 * h2, 64 * h2 + 64)
                        S.tr(PB[pr, c * 64:(c + 1) * 64], Vb[p][pr, c * 64:(c + 1) * 64], ident[pr, pr])
                S.copy("act", Vtm[p][:].rearrange("p c v -> p (c v)"), PB[:])
                yield
                for th in range(2):
                    pp = pf()
                    S.mm(pp[:], gupb[:, rd * 128:(rd + 1) * 128], sgd[:, th * 512:(th + 1) * 512])
                    S.copy("act", gfm[p][:, th * 512:(th + 1) * 512], pp[:])
                yield

            pbi = [0]

            def prep(rd, half):
                hs = slice(half * 512, half * 512 + 512)
                S.act(sqk[:], Kp[:, hs], AF.Square, scale=pcol(PV_KK + rd))
                pp = pf()
                S.mm(pp[:], bones, sqk[:])
                S.act(E0[:], pp[:], AF.Ln, bias=L2_EPS)
                S.act(E0[:], E0[:], AF.Exp, scale=-0.5)
                S.stt(kkn[:], Kp[:, hs], pcol(PV_KK + rd), E0[:], ALU.mult, ALU.mult)
                yield
                for d in range(2):
                    ARd, BKd, AAd, ABd, KTMd = AR[half][d], BK[half][d], AA[half][d], AB[half][d], KTM[half][d]
                    pd = slice(64 * d, 64 * d + 64)
                    pp = pf()
                    S.mm(pp[:], wupb[pd, rd * 128:(rd + 1) * 128], twd[pd, hs])
                    S.act(LW[:], pp[:], AF.Tanh, scale=0.5, bias=dcol(DV_HW0 + d * 4 + rd))
                    S.act(LW[:], LW[:], AF.Identity, scale=LWC, bias=LWC)
                    S.scan(C[:], smask[:], LW[:], 0.0, ALU.mult, ALU.add)
                    pp = pf()
                    S.mm(pp[:], aupb[pd, rd * 128:(rd + 1) * 128], adb[pd, hs])
                    S.act(IC[:], pp[:], AF.Tanh, scale=0.5, bias=dcol(DV_HA0 + d * 4 + rd))
                    yield
                    S.act(Dx[:], IC[:], AF.Identity, scale=dcol(DV_HKA + rd), bias=dcol(DV_OMHKA + rd))
                    S.act(IC[:], IC[:], AF.Identity, scale=0.5, bias=0.5)
                    S.tt("dve", KD[:], Kp[:, hs], Dx[:], ALU.mult)
                    S.stt(PBh[d][:], Rp[:, hs], dcol(DV_RK05 + rd), KD[:], ALU.mult, ALU.mult)
                    S.tt("dve", IC[:], kkn[:], IC[:], ALU.mult)
                    yield
                    tot = v3(C[:])[:, :, 63]
                    if d == 0:
                        S.tt("dve", Dx[:], C[:], LW[:], ALU.subtract)
                        logW = C
                    else:
                        S.tt("dve", v3(Dx[:]), tot.unsqueeze(2).to_broadcast([128, 8, 64]), v3(C[:]), ALU.subtract)
                        S.tt("dve", LW[:], Dx[:], LW[:], ALU.add)
                        logW = LW
                    S.act(wend[:, d, half * 8:(half + 1) * 8], tot, AF.Exp)
                    S.act(E0[:], logW[:], AF.Exp)
                    S.tt("dve", ARd[:, :, 1, :], v3(Rp[:, hs]), v3(E0[:]), ALU.mult)
                    S.act(E0[:], Dx[:], AF.Exp)
                    S.stt(ARd[:, :, 0, :], v3(kkn[:]), -1.0, v3(E0[:]), ALU.mult, ALU.mult)
                    yield
                    S.act(E0[:], logW[:], AF.Exp, scale=-1.0)
                    S.tt("dve", BKd[:, :, 0, :], v3(IC[:]), v3(E0[:]), ALU.mult)
                    S.tt("dve", BKd[:, :, 1, :], v3(KD[:]), v3(E0[:]), ALU.mult)
                    yield
                    for src, dst, eng in ((ARd[:, :, 0, :], AAd[:, :, 0, :], "act"),
                                          (BKd[:, :, 0, :], ABd[:, :, 1, :], "act"),
                                          (BKd[:, :, 1, :], KTMd[:], "act")):
                        PBx = PBs[pbi[0] % 2]
                        pbi[0] += 1
                        for c in range(8):
                            for h2 in range(2):
                                pr = slice(64 * h2, 64 * h2 + 64)
                                S.tr(PBx[pr, c * 64:(c + 1) * 64], src[pr, c, :], ident[pr, pr])
                        S.copy(eng, dst, v3(PBx[:, 0:512]))
                        yield
                pbn = pf()
                for c in range(8):
                    for h2 in range(2):
                        pr = slice(64 * h2, 64 * h2 + 64)
                        for d in range(2):
                            S.mm(pbn[pr, c:c + 1], PBh[d][pr, c * 64:(c + 1) * 64], ones_b[pr, 0:1],
                                 start=(d == 0), stop=(d == 1))
                S.copy("act", bon[rd % 2][:, half * 8:(half + 1) * 8], pbn[:, 0:8])
                yield

            def chunk_pipe(d, half):
                ms, mi, mT = mstrict[d], mincl[d], mstrictT[d]
                ARd, BKd, AAd, ABd, KTMd = AR[half][d], BK[half][d], AA[half][d], AB[half][d], KTM[half][d]
                PNs, NTs, AVs = PN[half][d], NT[half][d], AV[half][d]
                g8 = slice(half * 8, half * 8 + 8)

                def stg(lhs, rhs, wide):
                    banks = [pfl(), pfl()] if wide else [pfl()]
                    for c in range(8):
                        for h2 in range(2):
                            pr = slice(64 * h2, 64 * h2 + 64)
                            if wide:
                                o = banks[c // 4][pr, (c % 4) * 128:(c % 4 + 1) * 128]
                            else:
                                o = banks[0][pr, c * 64:(c + 1) * 64]
                            S.mm(o, lhs(pr, c), rhs(pr, c))
                    return banks

                o = stg(lambda pr, c: BKd[pr, c, 0, :], lambda pr, c: f128(ARd[pr, c, :, :]), True)
                yield "mm"
                for b in range(2):
                    bs = slice(b * 4, b * 4 + 4)
                    S.tt("dve", PNs[:, bs, 1, :], pv4(o[b])[:, :, 0:64], bc4(ms), ALU.mult)
                    S.tt("dve", ABd[:, bs, 0, :], pv4(o[b])[:, :, 64:128], bc4(mi), ALU.mult)
                rel(o)
                S.tt("pool", PNs[:, :, 0, :], PNs[:, :, 1, :], bc8(eye64), ALU.add)
                yield "ev"
                o = stg(lambda pr, c: BKd[pr, c, 1, :], lambda pr, c: ARd[pr, c, 1, :], False)
                yield "mm"
                S.tt("dve", Avv[:, d, g8, :], v3(o[0][:]), bc8(mi), ALU.mult)
                rel(o)
                yield "ev"
                o = stg(lambda pr, c: ARd[pr, c, 0, :], lambda pr, c: f128(BKd[pr, c, :, :]), True)
                yield "mm"
                for b in range(2):
                    bs = slice(b * 4, b * 4 + 4)
                    S.tt("dve", NTs[:, bs, :], pv4(o[b])[:, :, 0:64], bc4(mT), ALU.mult)
                    S.tt("dve", AAd[:, bs, 1, :], pv4(o[b])[:, :, 64:128], bc4(mT), ALU.mult)
                rel(o)
                yield "ev"
                oa = stg(lambda pr, c: NTs[pr, c, :], lambda pr, c: PNs[pr, c, 1, :], False)
                ob = stg(lambda pr, c: PNs[pr, c, 1, :], lambda pr, c: NTs[pr, c, :], False)
                yield "mm"
                S.copy("act", PNs[:, :, 1, :], v3(oa[0][:]))
                S.copy("act", NTs[:], v3(ob[0][:]))
                rel(oa + ob)
                yield "ev"
                for step in range(4):
                    oa = stg(lambda pr, c: NTs[pr, c, :], lambda pr, c: f128(PNs[pr, c, :, :]), True)
                    ob = stg(lambda pr, c: PNs[pr, c, 1, :], lambda pr, c: NTs[pr, c, :], False)
                    yield "mm"
                    S.copy("act", NTs[:], v3(ob[0][:]))
                    S.tt("dve", PNs[:, 0:4, 0, :], pv4(oa[0])[:, :, 0:64], PNs[:, 0:4, 0, :], ALU.add)
                    S.copy("act", PNs[:, 0:4, 1, :], pv4(oa[0])[:, :, 64:128])
                    S.tt("dve", PNs[:, 4:8, 0, :], pv4(oa[1])[:, :, 0:64], PNs[:, 4:8, 0, :], ALU.add)
                    S.copy("act", PNs[:, 4:8, 1, :], pv4(oa[1])[:, :, 64:128])
                    rel(oa + ob)
                    yield "ev"
                oa = stg(lambda pr, c: NTs[pr, c, :], lambda pr, c: PNs[pr, c, 0, :], False)
                yield "mm"
                S.tt("dve", PNs[:, :, 0, :], v3(oa[0][:]), PNs[:, :, 0, :], ALU.add)
                rel(oa)
                yield "ev"
                o = stg(lambda pr, c: PNs[pr, c, 0, :], lambda pr, c: f128(AAd[pr, c, :, :]), True)
                yield "mm"
                for b in range(2):
                    bs = slice(b * 4, b * 4 + 4)
                    S.copy("act", AVs[:, bs, :, :].rearrange("p c a b -> p c (a b)"), pv4(o[b]))
                rel(o)
                yield "ev"
                o = stg(lambda pr, c: AVs[pr, c, 0, :], lambda pr, c: f128(ABd[pr, c, :, :]), True)
                yield "mm"
                for b in range(2):
                    bs = slice(b * 4, b * 4 + 4)
                    gs_ = slice(half * 8 + b * 4, half * 8 + b * 4 + 4)
                    S.tt("dve", Rh[:, d, gs_, :], pv4(o[b])[:, :, 0:64], ARd[:, bs, 1, :], ALU.add)
                    S.tt("dve", G0[:, d, gs_, :], pv4(o[b])[:, :, 64:128], bc4(eye64), ALU.add)
                rel(o)
                yield "ev"
                o = stg(lambda pr, c: AVs[pr, c, 1, :], lambda pr, c: f128(ABd[pr, c, :, :]), True)
                yield "mm"
                for b in range(2):
                    bs = slice(b * 4, b * 4 + 4)
                    gs_ = slice(half * 8 + b * 4, half * 8 + b * 4 + 4)
                    S.tt("dve", Avv[:, d, gs_, :], pv4(o[b])[:, :, 0:64], Avv[:, d, gs_, :], ALU.add)
                    S.tt("dve", H0[:, d, gs_, :], pv4(o[b])[:, :, 64:128], KTMd[:, bs, :], ALU.add)
                rel(o)

            def run_pipes(gens):
                idle = list(range(len(gens)))
                holding = []
                while idle or holding:
                    while len(holding) < 2 and idle:
                        i = idle.pop(0)
                        tag = next(gens[i])
                        assert tag == "mm"
                        holding.append(i)
                    i = holding.pop(0)
                    try:
                        tag = next(gens[i])
                        assert tag == "ev"
                        idle.append(i)
                    except StopIteration:
                        pass

            def tail(rd):
                p = rd % 2
                Vt = Vtm[p]
                S.copy("dve", wendc[:], wend[:])
                S.copy("dve", wk[:], wend[:])
                S.ts("dve", wk[:, 0, 3:12:4], wendc[:, 0, 3:12:4], fl[:, 2:3], ALU.mult)
                S.ts("dve", wk[:, 1, 4:13:4], wendc[:, 1, 4:13:4], fl[:, 2:3], ALU.mult)
                S.copy("dve", Sall[0][0][:], s0r[p][:, 0, :])
                S.copy("dve", Sall[1][16][:], s0r[p][:, 1, :])
                yield
                pys = (pfl(), pfl())

                def ymm(c):
                    for h2 in range(2):
                        pr = slice(64 * h2, 64 * h2 + 64)
                        o = pys[c // 8][pr, (c % 8) * 64:(c % 8 + 1) * 64]
                        S.mm(o, Avv[pr, 0, c, :], Vt[pr, c, :], start=True, stop=False)
                        S.mm(o, Avv[pr, 1, c, :], Vt[pr, c, :], start=False, stop=False)
                        S.mm(o, Rh[pr, 0, c, :], Sall[0][c][pr, :], start=False, stop=False)
                        S.mm(o, Rh[pr, 1, c, :], Sall[1][c + 1][pr, :], start=False, stop=True)

                for step in range(16):
                    cf, cbk = step, 15 - step
                    pcs = (pfl(), pfl())
                    for d, c, sin in ((0, cf, cf), (1, cbk, cbk + 1)):
                        for h2 in range(2):
                            pr = slice(64 * h2, 64 * h2 + 64)
                            S.mm(pcs[d][pr, 0:64], H0[pr, d, c, :], Vt[pr, c, :], start=True, stop=False)
                            S.mm(pcs[d][pr, 0:64], G0[pr, d, c, :], Sall[d][sin][pr, :], start=False, stop=True)
                    if step >= 8:
                        ymm(15 - step)
                        ymm(step)
                    S.act(Sall[0][cf + 1][:], pcs[0][:, 0:64], AF.Identity, scale=wk[:, 0, cf:cf + 1])
                    S.ts("dve", Sall[1][cbk][:], pcs[1][:, 0:64], wk[:, 1, cbk:cbk + 1], ALU.mult)
                    if cf % 4 == 3:
                        S.act(SF[:, 0, cf // 4, :], pcs[0][:, 0:64], AF.Identity, scale=wendc[:, 0, cf:cf + 1])
                        S.ts("dve", SF[:, 1, cbk // 4, :], pcs[1][:, 0:64], wendc[:, 1, cbk:cbk + 1], ALU.mult)
                    rel(list(pcs))
                    yield
                S.dma("sp", O["oS"][:, rd, :, :, :], SF[:])
                for b in range(2):
                    S.copy("act", Yf[:, b * 8:(b + 1) * 8, :], v3(pys[b][:]))
                rel(list(pys))
                yield
                S.reduce(st4[:, 0, :], Yf[:], ALU.add)
                S.tt("dve", Yq[:], Yf[:], Yf[:], ALU.mult)
                S.reduce(st4[:, 1, :], Yq[:], ALU.add)
                yield
                S.ts("dve", st4[:, 0, :], st4[:, 0, :], 1.0 / 64, ALU.mult)
                S.tt("dve", st4[:, 2, :], st4[:, 0, :], st4[:, 0, :], ALU.mult)
                S.stt(st4[:, 3, :], st4[:, 1, :], 1.0 / 64, st4[:, 2, :], ALU.mult, ALU.subtract)
                S.act(st4[:, 3, :], st4[:, 3, :], AF.Ln, bias=GN_EPS)
                S.act(st4[:, 3, :], st4[:, 3, :], AF.Exp, scale=-0.5)
                b16 = lambda ap: ap.unsqueeze(2).to_broadcast([128, 16, 64])
                g16 = lambda ap: ap.unsqueeze(1).to_broadcast([128, 16, 64])
                S.tt("dve", Yq[:], Yf[:], b16(st4[:, 0, :]), ALU.subtract)
                yield
                S.tt("dve", Yq[:], Yq[:], b16(st4[:, 3, :]), ALU.mult)
                S.tt("dve", Yq[:], Yq[:], g16(lnp[p][:, 0, :]), ALU.mult)
                yield
                S.tt("dve", Yq[:], Yq[:], g16(lnp[p][:, 1, :]), ALU.add)
                S.tt("dve", Yf[:], Vt[:], b16(bon[p][:]), ALU.mult)
                yield
                S.tt("dve", otm[:], Yq[:], Yf[:], ALU.add)
                for c in range(16):
                    for h2 in range(2):
                        pr = slice(64 * h2, 64 * h2 + 64)
                        S.tr(PBs[1][pr, c * 64:(c + 1) * 64], otm[pr, c, :], ident[pr, pr])
                S.tt("dve", oT_rw[:, rd, :], PBs[1][:], gfm[p][:], ALU.mult)
                yield

            def nxt(r):
                yield from head(r)
                yield from prep(r, 0)
                yield from prep(r, 1)

            run(nxt(0))
            for rd in range(4):
                run_pipes([chunk_pipe(0, 0), chunk_pipe(1, 0), chunk_pipe(0, 1), chunk_pipe(1, 1)])
                if rd < 3:
                    t_, n_ = tail(rd), nxt(rd + 1)
                    alive = True
                    while alive:
                        alive = False
                        for g_, k_ in ((t_, 1), (n_, 4)):
                            for _ in range(k_):
                                try:
                                    next(g_)
                                    alive = True
                                except StopIteration:
                                    break
                else:
                    run(tail(rd))
            dbgdump("oTrw", oT_rw[:], (128, 4, T), BF16)
            prefetch(I["wor"][:], (128, 4, D))
            prefetch(I["won"][:], (128, 4, D))
            prefetch(I["win"][:, :, WC_GRW:WC_GRW + 512], (128, 8, 512))
            prefetch(I["win"][:, :, WC_GNA:WC_GNA + 512], (128, 8, 512))
            S.barrier()
            S.flush()
            if stop == "RW":
                return nc, dbg_out

        with ExitStack() as phx, ExitStack() as ph:
            xT = sbt(phx, "xTm", (128, 8, T), F32)
            mrg = sbt(ph, "mrg", (128, 8, T), BF16)
            tg = [sbt(ph, "tg%d" % i, (128, 512), F32) for i in range(4)]
            t1 = [sbt(ph, "t1%d" % i, (128, 512), F32) for i in range(2)]
            tmpA = sbt(ph, "tmpB", (128, 2, T), F32)
            sqb = sbt(ph, "sqb2", (128, 2, T), BF16)
            rstd = sbt(ph, "rstd2", (128, T), F32)
            S.dma("sp", xT[:], I["xT"][:])
            S.ts("dve", dv[:, DV_G1H:DV_G1H + 8], mod[:, 16:24], 0.5, ALU.mult)
            S.stt(dv[:, DV_GS2:DV_GS2 + 8], mod[:, 32:40], 1.0, pv[:, PV_N2:PV_N2 + 8], ALU.add, ALU.mult)
            ti = 0
            for g in range(2):
                wor = wload(I["wor"][:], (128, 4, D))
                won = wload(I["won"][:], (128, 4, D))
                wgr = wload(I["win"][:, :, WC_GRW + g * 512:WC_GRW + (g + 1) * 512], (128, 8, 512))
                wgn = wload(I["win"][:, :, WC_GNA + g * 512:WC_GNA + (g + 1) * 512], (128, 8, 512))
                for j4 in range(4):
                    j = g * 4 + j4
                    for th in range(2):
                        ts_ = slice(th * 512, th * 512 + 512)
                        pgr, pgn, pa, pb_ = pf(), pf(), pf(), pf()
                        for kc in range(8):
                            S.mm(pgr[:], wgr[:, kc, j4 * 128:(j4 + 1) * 128], hT[:, kc, ts_], start=(kc == 0), stop=(kc == 7))
                        for kc in range(8):
                            S.mm(pgn[:], wgn[:, kc, j4 * 128:(j4 + 1) * 128], hT[:, kc, ts_], start=(kc == 0), stop=(kc == 7))
                        for kc in range(4):
                            S.mm(pa[:], wor[:, kc, j * 128:(j + 1) * 128], oT_rw[:, kc, ts_], start=(kc == 0), stop=(kc == 3))
                        for kc in range(4):
                            S.mm(pb_[:], won[:, kc, j * 128:(j + 1) * 128], oT_na[:, kc, ts_], start=(kc == 0), stop=(kc == 3))
                        ta, tb_ = tg[ti % 4], tg[(ti + 1) % 4]
                        tt1 = t1[(ti // 2) % 2]
                        ti += 2
                        S.act(ta[:], pgr[:], AF.Tanh, scale=0.5)
                        S.act(tb_[:], pgn[:], AF.Tanh, scale=0.5)
                        S.stt(tt1[:], ta[:], 1.0, pa[:], ALU.add, ALU.mult)
                        S.stt(ta[:], tb_[:], 1.0, pb_[:], ALU.add, ALU.mult)
                        S.tt("dve", mrg[:, j, ts_], tt1[:], ta[:], ALU.add)
            dbgdump("mrg", mrg[:], (128, 8, T), BF16)
            for g in range(2):
                wo = wload(I["wout"][:, :, g * 512:(g + 1) * 512], (128, 8, 512))
                for j4 in range(4):
                    j = g * 4 + j4
                    for th in range(2):
                        ts_ = slice(th * 512, th * 512 + 512)
                        pp = pf()
                        for kc in range(8):
                            S.mm(pp[:], wo[:, kc, j4 * 128:(j4 + 1) * 128], mrg[:, kc, ts_], start=(kc == 0), stop=(kc == 7))
                        S.stt(xT[:, j, ts_], pp[:], dcol(DV_G1H + j), xT[:, j, ts_], ALU.mult, ALU.add)
            dbgdump("x1", xT[:], (128, 8, T))
            prefetch(I["w1"][:, :, 0:512], (128, 8, 512))
            prefetch(I["w3"][:, :, 0:512], (128, 8, 512))
            rmsnorm_mod(xT, DV_GS2, 24, tmpA, sqb, rstd)
            S.barrier()
            S.flush()
            ph.close()

            with ExitStack() as ph2:
                actb = sbt(ph2, "actb", (128, NFT, T), BF16)
                sl = [sbt(ph2, "sl%d" % i, (128, 512), F32) for i in range(3)]
                yo = [sbt(ph2, "yo%d" % i, (128, 512), F32) for i in range(2)]
                ti = 0
                for g in range(6):
                    nt = min(4, NFT - g * 4)
                    wa = wload(I["w1"][:, :, g * 512:g * 512 + nt * 128], (128, 8, nt * 128))
                    wb = wload(I["w3"][:, :, g * 512:g * 512 + nt * 128], (128, 8, nt * 128))
                    for f4 in range(nt):
                        f = g * 4 + f4
                        for th in range(2):
                            ts_ = slice(th * 512, th * 512 + 512)
                            pa, pb_ = pf(), pf()
                            for kc in range(8):
                                S.mm(pa[:], wa[:, kc, f4 * 128:(f4 + 1) * 128], hT[:, kc, ts_], start=(kc == 0), stop=(kc == 7))
                            for kc in range(8):
                                S.mm(pb_[:], wb[:, kc, f4 * 128:(f4 + 1) * 128], hT[:, kc, ts_], start=(kc == 0), stop=(kc == 7))
                            s_ = sl[ti % 3]
                            ti += 1
                            S.act(s_[:], pa[:], AF.Silu)
                            S.tt("dve", actb[:, f, ts_], s_[:], pb_[:], ALU.mult)
                oi = 0
                for j in range(8):
                    w2 = wload(I["w2"][:, :, j * 128:(j + 1) * 128], (128, NFT, 128))
                    for th in range(2):
                        ts_ = slice(th * 512, th * 512 + 512)
                        pp = pf()
                        for f in range(NFT):
                            S.mm(pp[:], w2[:, f, :], actb[:, f, ts_], start=(f == 0), stop=(f == NFT - 1))
                        y_ = yo[oi % 2]
                        oi += 1
                        S.stt(y_[:], pp[:], mod[:, 40 + j:41 + j], xT[:, j, ts_], ALU.mult, ALU.add)
                        S.dma("sp", O["yT"][:, j, ts_], y_[:])
                S.barrier()
                S.flush()
    return nc, dbg_out


def _fm(w):
    K, N = w.shape
    return np.ascontiguousarray(w.reshape(K // 128, 128, N).transpose(1, 0, 2))


def _consts():
    cst = np.zeros((128, NCB), np.float32)
    p = np.arange(128)
    cst[:, CB_ID:CB_ID + 128] = np.eye(128, dtype=np.float32)
    cst[:, CB_ONE:CB_ONE + 128] = 1.0
    cst[:, CB_BO:CB_BO + 128] = (p[:, None] // 64 == p[None, :] // 64)
    i = (p % 64)[:, None]
    f = np.arange(64)[None, :]
    cst[:, CB_LT:CB_LT + 64] = i < f
    cst[:, CB_LE:CB_LE + 64] = i <= f
    cst[:, CB_GT:CB_GT + 64] = i > f
    cst[:, CB_GE:CB_GE + 64] = i >= f
    cst[:, CB_I64:CB_I64 + 64] = i == f
    sm = np.ones((128, T), np.float32)
    sm[:, ::64] = 0.0
    return cst, sm


def _bias_tables(na_rpb):
    GW, ROWS, NR, NCOL = 64, 16, 8, 16
    t = np.arange(T)
    row, col = t // GW, t % GW
    row_start = np.clip(np.arange(ROWS) - NR // 2, 0, ROWS - NR)
    win_start = np.clip(np.arange(GW) - NCOL // 2, 0, GW - NCOL)
    qr, qc = row[:, None], col[:, None]
    kr, kc = row[None, :], col[None, :]
    valid = (kr >= row_start[qr]) & (kr < row_start[qr] + NR) & (kc >= win_start[qc]) & (kc < win_start[qc] + NCOL)
    dr = np.clip(kr - qr + NR - 1, 0, 2 * NR - 2)
    dc = np.clip(kc - qc, -(NCOL - 1), NCOL - 1) + NCOL - 1
    Bs = np.where(valid[None], na_rpb[:, dr, dc], np.float32(NEGM)).astype(np.float32)
    same = (t[:, None] // 256) == (t[None, :] // 256)
    Bp = np.where(same, np.float32(0.0), np.float32(NEGM)).astype(np.float32)[None]

    def tile(B):
        out = np.empty((B.shape[0], 128, NBLK, 128), np.float32)
        for (j, R), b in BLK.items():
            out[:, :, b, :] = B[:, R * 128:(R + 1) * 128, j * 128:(j + 1) * 128].transpose(0, 2, 1)
        return out
    ts_ = tile(Bs)
    tp = np.ascontiguousarray(np.broadcast_to(tile(Bp), (8, 128, NBLK, 128)))
    return tp, ts_


_CACHE = {}


def _get_nc():
    if "nc" not in _CACHE:
        _CACHE["nc"] = build_nc()[0]
    return _CACHE["nc"]


def _prep(inp):
    f = lambda a: np.ascontiguousarray(np.asarray(a, dtype=np.float32))
    w_in = f(inp["w_in"])[0]
    perm = list(range(1536, 1920))
    for rd in range(4):
        for ti in range(3):
            perm += list(range(ti * 512 + rd * 128, ti * 512 + rd * 128 + 128))
    perm += list(range(1920, 5504))
    perm = np.array(perm)
    otile = [12, 13, 14] + [ti * 4 + rd for rd in range(4) for ti in range(3)]
    mu = f(inp["shift_mu"])[0]
    pv = np.zeros((128, NPV), np.float32)
    col = lambda v: np.ascontiguousarray(v.reshape(-1, 128).T)
    pv[:, PV_N1:PV_N1 + 8] = col(f(inp["norm1_g"])[0])
    pv[:, PV_N2:PV_N2 + 8] = col(f(inp["norm2_g"])[0])
    pv[:, PV_BADA:PV_BADA + 48] = col(f(inp["b_ada"])[0])
    pv[:, PV_MU0:PV_MU0 + 15] = col(mu[0])[:, otile]
    pv[:, PV_MU1:PV_MU1 + 15] = col(mu[1])[:, otile]
    pv[:, PV_KK:PV_KK + 4] = col(f(inp["rw_k_k"])[0])
    pv[:, PV_KA:PV_KA + 4] = col(f(inp["rw_k_a"])[0])
    pv[:, PV_RK:PV_RK + 4] = col(f(inp["rw_r_k"])[0].reshape(-1))
    pv[:, PV_W0:PV_W0 + 4] = col(f(inp["rw_w0"])[0, 0])
    pv[:, PV_W0 + 4:PV_W0 + 8] = col(f(inp["rw_w0"])[0, 1])
    pv[:, PV_A0:PV_A0 + 4] = col(f(inp["rw_a0"])[0, 0])
    pv[:, PV_A0 + 4:PV_A0 + 8] = col(f(inp["rw_a0"])[0, 1])
    pv[:, PV_QG] = np.tile(f(inp["na_q_g"])[0], 2)
    pv[:, PV_KG] = np.tile(f(inp["na_k_g"])[0], 2)
    lnp = np.stack([f(inp["rw_ln_g"])[0].reshape(4, 2, 64), f(inp["rw_ln_b"])[0].reshape(4, 2, 64)])
    lnp = np.ascontiguousarray(np.repeat(lnp.transpose(2, 0, 1, 3), 64, axis=0))
    cst, sm = _consts()
    tp, ts_ = _bias_tables(f(inp["na_rpb"])[0])
    common = {
        "cstb": cst, "smask": sm, "pv": pv, "lnp": lnp,
        "wada": _fm(f(inp["w_ada"])[0]), "win": _fm(w_in[:, perm]),
        "wup": f(inp["rw_w_up"])[0].reshape(128, 512), "aup": f(inp["rw_a_up"])[0].reshape(128, 512),
        "gup": f(inp["rw_g_up"])[0], "wor": _fm(f(inp["w_o_rwkv"])[0]), "won": _fm(f(inp["w_o_na"])[0]),
        "wout": _fm(f(inp["w_out"])[0]), "w1": _fm(f(inp["ffn_w1"])[0]), "w3": _fm(f(inp["ffn_w3"])[0]),
        "w2": _fm(f(inp["ffn_w2"])[0]),
    }
    xp, xs = f(inp["x_prompt"]), f(inp["x_sample"])
    st, ck, cv_ = f(inp["state_rwkv"]), f(inp["cache_na_k"]), f(inp["cache_na_v"])
    c, cctx = f(inp["c"]), f(inp["c_ctx"])
    maps = []
    for u in range(8):
        m = dict(common)
        if u < 4:
            X = xp[4 * u:4 * u + 4].reshape(T, D)
            cvec = cctx
            m["fl"] = np.tile(np.array([1.0, NEGM, 0.0, 0.0], np.float32), (128, 1))
            m["s0T"] = np.zeros((128, 4, 2, 64), np.float32)
            m["ckT"] = np.zeros((128, 4, 256), np.float32)
            m["cvv"] = np.zeros((128, 2, 512), np.float32)
            m["tbl"] = tp
        else:
            b = (u - 4) % 2
            X = xs[b]
            cvec = c[b]
            m["fl"] = np.tile(np.array([0.0, 0.0, 1.0, 0.0], np.float32), (128, 1))
            s = st[b, 0].reshape(2, 4, 2, 64, 64)
            m["s0T"] = np.ascontiguousarray(s.transpose(2, 4, 1, 0, 3).reshape(128, 4, 2, 64))
            kk_ = ck[b, 0].reshape(256, 4, 2, 64)
            m["ckT"] = np.ascontiguousarray(kk_.transpose(2, 3, 1, 0).reshape(128, 4, 256))
            m["cvv"] = np.ascontiguousarray(cv_[b, 0].reshape(2, 128, 512).transpose(1, 0, 2))
            m["tbl"] = ts_
        m["xT"] = _fm(np.ascontiguousarray(X.T))
        m["cv"] = np.ascontiguousarray(cvec.reshape(8, 128).T)
        maps.append(m)
    return maps


def _unfm(a):
    return a.transpose(1, 0, 2).reshape(-1, a.shape[2])


def kernel(**inp):
    maps = _prep(inp)
    nc = _get_nc()
    res = run_bass_kernel_spmd(nc, maps, core_ids=list(range(8)))
    R = res.results
    y_p = np.empty((16, 256, D), np.float32)
    y_s = np.empty((2, 1024, D), np.float32)
    s_new = np.empty((16, 1, 2, 8, 64, 64), np.float32)
    k_new = np.empty((16, 1, 256, 8, 64), np.float32)
    v_new = np.empty((16, 1, 256, 8, 64), np.float32)
    for u in range(6):
        Y = _unfm(np.asarray(R[u]["yT"])).T
        if u < 4:
            y_p[4 * u:4 * u + 4] = Y.reshape(4, 256, D)
            oS = np.asarray(R[u]["oS"]).reshape(2, 64, 4, 2, 4, 64)
            s_new[4 * u:4 * u + 4, 0] = oS.transpose(4, 3, 2, 0, 5, 1).reshape(4, 2, 8, 64, 64)
            okT = np.asarray(R[u]["okT"]).reshape(2, 64, 4, T)
            k_new[4 * u:4 * u + 4, 0] = okT.transpose(3, 2, 0, 1).reshape(4, 256, 8, 64)
            ov = np.asarray(R[u]["ov"]).transpose(1, 0, 2).reshape(T, 512)
            v_new[4 * u:4 * u + 4, 0] = ov.reshape(4, 256, 8, 64)
        else:
            y_s[u - 4] = Y
    return y_p, y_s, s_new, k_new, v_new
```

```python
import numpy as np
from contextlib import ExitStack
import concourse.bass as bass
import concourse.mybir as mybir
from concourse.bass_utils import run_bass_kernel_spmd

F32 = mybir.dt.float32
BF16 = mybir.dt.bfloat16
ALU = mybir.AluOpType
AF = mybir.ActivationFunctionType
AX = mybir.AxisListType

D = 1024
T = 1024
NCH = 16
FF = 2816
NFT = 22
RMS_EPS = 1e-6
GN_EPS = 64e-5
L2_EPS = 1e-12
NEGM = -30000.0
LWC = -0.30326533
NDMA = 24
NSLOT = 4

PV_N1, PV_N2, PV_BADA, PV_MU0, PV_MU1 = 0, 8, 16, 64, 79
PV_KK, PV_KA, PV_RK, PV_W0, PV_A0, PV_QG, PV_KG, NPV = 94, 98, 102, 106, 114, 122, 123, 124
CB_ID, CB_ONE, CB_BO, CB_LT, CB_LE, CB_GT, CB_GE, CB_I64, NCB = 0, 128, 256, 384, 448, 512, 576, 640, 704

def _qwin(R):
    return min(max(R - 2, 0), 3)
QL = [[R for R in range(8) if _qwin(R) <= j < _qwin(R) + 5] for j in range(8)]
BLK = {}
_b = 0
for _j in range(8):
    for _R in QL[_j]:
        BLK[(_j, _R)] = _b
        _b += 1
NBLK = _b


class Sched:
    def __init__(self, nc, stack):
        self.nc = nc
        self.ce = ["pe", "act", "dve", "pool"]
        self.qe = ["sp"]
        self.sem = {e: stack.enter_context(nc.semaphore("s_" + e)) for e in self.ce + self.qe}
        self.cnt = {e: 0 for e in self.ce + self.qe}
        self.dsem = [stack.enter_context(nc.semaphore("d%d" % i)) for i in range(NDMA)]
        self.dcnt = [0] * NDMA
        self.drr = {"sp": 0, "pool": 0}
        self.ops = {e: [] for e in self.ce + self.qe}
        self.res = {}
        self.waited = {e: {} for e in self.ce + self.qe}
        self.nops = 0
        self.pending_dma = []
        self.same = {e: {} for e in self.ce}

    def _semobj(self, sk):
        return self.sem[sk[1]] if sk[0] == "eng" else self.dsem[sk[1]]

    def _key(self, x):
        if isinstance(x, (str, tuple)):
            return x
        return x.tensor.name if hasattr(x, "tensor") else x.name

    def _deps(self, reads, writes):
        raw, other = [], []
        for r in reads:
            st = self.res.get(self._key(r))
            if st and st["w"]:
                raw.append(st["w"])
        for w in writes:
            st = self.res.get(self._key(w))
            if st:
                if st["w"]:
                    other.append(st["w"])
                other.extend(st["r"].values())
        return raw, other

    def _mark(self, reads, writes, ev, rk):
        for r in reads:
            st = self.res.setdefault(self._key(r), {"w": None, "r": {}})
            st["r"][rk] = ev
        for w in writes:
            self.res[self._key(w)] = {"w": ev, "r": {}}

    def _waits(self, eng, deps):
        if isinstance(deps, tuple):
            raw, other = deps
            deps = list(raw) + [d_ for d_ in other if d_[0] != ("eng", eng)]
        best = {}
        for sk, v in deps:
            if sk == ("eng", eng) and eng == "pe":
                continue
            if self.waited[eng].get(sk, 0) >= v:
                continue
            if best.get(sk, 0) < v:
                best[sk] = v
        for sk, v in best.items():
            self.waited[eng][sk] = v
        return list(best.items())

    @staticmethod
    def _box(x):
        if isinstance(x, (str, tuple)) or not hasattr(x, "ap"):
            return None
        try:
            ap = x.ap
            p0 = x.base_partition()
            off = int(x.offset) - p0 * ap[0][0]
            lo = hi = off
            for st_, n_ in ap[1:]:
                ext = st_ * (n_ - 1)
                if ext < 0:
                    lo += ext
                else:
                    hi += ext
            return (p0, p0 + ap[0][1], lo, hi + 1)
        except Exception:
            return None

    @staticmethod
    def _ovl(a, b):
        if a is None or b is None:
            return True
        return a[0] < b[1] and b[0] < a[1] and a[2] < b[3] and b[2] < a[3]

    def _same_deps(self, eng, reads, writes):
        deps = []
        hist = self.same[eng]
        for lst, isw in ((reads, False), (writes, True)):
            for x in lst:
                h = hist.get(self._key(x))
                if not h:
                    continue
                b = self._box(x)
                for b2, cnt2, w2 in h:
                    if (isw or w2) and self._ovl(b, b2):
                        deps.append((("eng", eng), cnt2))
        return deps

    def _same_record(self, eng, reads, writes, cnt):
        hist = self.same[eng]
        for lst, isw in ((reads, False), (writes, True)):
            for x in lst:
                h = hist.setdefault(self._key(x), [])
                h.append((self._box(x), cnt, isw))
                if len(h) > 64:
                    del h[0]

    def _psum_excl(self, r, w):
        r2, w2 = [], list(w)
        for x in r:
            k = self._key(x)
            if isinstance(k, str) and (k.startswith("pf") or k.startswith("pb")):
                w2.append(x)
            else:
                r2.append(x)
        return r2, w2

    def op(self, eng, fn, r=(), w=()):
        r0, w0 = list(r), list(w)
        r, w = self._psum_excl(r, w)
        raw_, oth_ = self._deps(r, w)
        own = ("eng", eng)
        deps = [d_ for d_ in list(raw_) + list(oth_) if d_[0] != own]
        if eng != "pe":
            deps += self._same_deps(eng, r0, w0)
        waits = self._waits(eng, deps)
        self.cnt[eng] += 1
        if eng != "pe":
            self._same_record(eng, r0, w0, self.cnt[eng])
        ev = (("eng", eng), self.cnt[eng])
        self.ops[eng].append((fn, waits, (ev[0], 1)))
        self._mark(r, w, ev, eng)
        self.nops += 1

    def dma(self, q, out, in_, r=None, w=None, track=True):
        r = [in_] if r is None else r
        w = [out] if w is None else w
        half_ = NDMA // 2
        base = 0 if q == "sp" else half_
        i = base + self.drr[q] % half_
        self.drr[q] += 1
        raw_, oth_ = self._deps(r, w)
        deps = list(raw_) + list(oth_)
        if self.dcnt[i] > 0:
            deps.append((("dma", i), self.dcnt[i]))
        waits = self._waits(q, deps)
        self.dcnt[i] += 16
        ev = (("dma", i), self.dcnt[i])

        def fn(e, out=out, in_=in_):
            return e.dma_start(out=out, in_=in_)

        self.ops[q].append((fn, waits, (ev[0], 16)))
        self._mark(r, w, ev, ("dma", i))
        if track:
            self.pending_dma.append(ev)
        self.nops += 1
        return ev

    def barrier(self):
        waits = self._waits("sp", list(self.pending_dma))
        self.pending_dma = []
        self.cnt["sp"] += 1
        spv = self.cnt["sp"]
        self.ops["sp"].append(("seminc", waits, None))
        tgt = [(("eng", e), self.cnt[e]) for e in self.ce if self.cnt[e] > 0] + [(("eng", "sp"), spv)]
        for e in self.ce + self.qe:
            ws = self._waits(e, [t for t in tgt if t[0] != ("eng", e)])
            self.ops[e].append((None, ws, None))

    def flush(self):
        nc = self.nc
        ops = self.ops
        self.ops = {e: [] for e in self.ce + self.qe}
        with nc.Block() as block:
            def emit(engname):
                def body(e):
                    for fn, waits, inc in ops[engname]:
                        for sk, v in waits:
                            e.wait_ge(self._semobj(sk), v)
                        if fn is None:
                            continue
                        if fn == "seminc":
                            e.sem_inc(self.sem["sp"], 1)
                            continue
                        ins = fn(e)
                        if inc is not None:
                            ins.then_inc(self._semobj(inc[0]), inc[1])
                return body
            block.tensor(emit("pe"))
            block.scalar(emit("act"))
            block.vector(emit("dve"))
            block.gpsimd(emit("pool"))
            block.sync(emit("sp"))

    def mm(self, out, lhsT, rhs, start=True, stop=True):
        self.op("pe", lambda e: e.matmul(out, lhsT, rhs, start=start, stop=stop), r=[lhsT, rhs], w=[out])

    def tr(self, out, in_, ident):
        self.op("pe", lambda e: e.transpose(out, in_, ident), r=[in_, ident], w=[out])

    def act(self, out, in_, func, bias=None, scale=None):
        kw = {}
        rr = [in_]
        if bias is not None:
            kw["bias"] = bias
            if not isinstance(bias, (int, float)):
                rr.append(bias)
        if scale is not None:
            kw["scale"] = scale
            if not isinstance(scale, (int, float)):
                rr.append(scale)
        self.op("act", lambda e: e.activation(out, in_, func, **kw), r=rr, w=[out])

    def tt(self, eng, out, in0, in1, op):
        self.op(eng, lambda e: e.tensor_tensor(out, in0, in1, op), r=[in0, in1], w=[out])

    def ts(self, eng, out, in0, s1, op0, s2=None, op1=None):
        rr = [in0] + [s for s in (s1, s2) if s is not None and not isinstance(s, (int, float))]
        kw = {}
        if op1 is not None:
            kw["op1"] = op1
        self.op(eng, lambda e: e.tensor_scalar(out, in0, s1, s2, op0, **kw), r=rr, w=[out])

    def stt(self, out, in0, scalar, in1, op0, op1):
        rr = [in0, in1] + ([scalar] if not isinstance(scalar, (int, float)) else [])
        self.op("dve", lambda e: e.scalar_tensor_tensor(out, in0, scalar, in1, op0, op1), r=rr, w=[out])

    def copy(self, eng, out, in_):
        if eng == "act":
            self.op("act", lambda e: e.activation(out, in_, AF.Identity), r=[in_], w=[out])
        else:
            self.op(eng, lambda e: e.tensor_copy(out, in_), r=[in_], w=[out])

    def memset(self, eng, out, val):
        self.op(eng, lambda e: e.memset(out, val), r=[], w=[out])

    def reduce(self, out, in_, op, axis=AX.X):
        self.op("dve", lambda e: e.tensor_reduce(out, in_, axis, op), r=[in_], w=[out])

    def recip(self, out, in_):
        self.op("dve", lambda e: e.reciprocal(out, in_), r=[in_], w=[out])

    def scan(self, out, d0, d1, init, op0, op1):
        self.op("dve", lambda e: e.tensor_tensor_scan(out, d0, d1, init, op0, op1), r=[d0, d1], w=[out])


def _prod(xs):
    p = 1
    for x in xs:
        p *= x
    return p


def build_nc(dbg=(), stop=None):
    nc = bass.Bass("TRN2", target_bir_lowering=False)

    def din(name, shape):
        return nc.dram_tensor(name, list(shape), F32, kind="ExternalInput").ap()

    def dout(name, shape):
        return nc.dram_tensor(name, list(shape), F32, kind="ExternalOutput").ap()

    I = {}
    for name, shape in [("xT", (128, 8, T)), ("cv", (128, 8)), ("fl", (128, 4)), ("s0T", (128, 4, 2, 64)),
                        ("ckT", (128, 4, 256)), ("cvv", (128, 2, 512)), ("tbl", (8, 128, NBLK, 128)),
                        ("cstb", (128, NCB)), ("smask", (128, T)), ("pv", (128, NPV)), ("lnp", (128, 2, 4, 64)),
                        ("wada", (128, 8, 6 * D)), ("win", (128, 8, 5504)), ("wup", (128, 512)),
                        ("aup", (128, 512)), ("gup", (128, 512)), ("wor", (128, 4, D)), ("won", (128, 4, D)),
                        ("wout", (128, 8, D)), ("w1", (128, 8, FF)), ("w3", (128, 8, FF)), ("w2", (128, NFT, D))]:
        I[name] = din(name, shape)
    O = {}
    for name, shape in [("yT", (128, 8, T)), ("oS", (128, 4, 2, 4, 64)), ("okT", (128, 4, T)), ("ov", (128, 8, 512))]:
        O[name] = dout(name, shape)
    dbg_out = {}

    with ExitStack() as top:
        S = Sched(nc, top)

        def sbt(stack, name, shape, dt):
            return stack.enter_context(nc.sbuf_tensor("sb_" + name, list(shape), dt))

        slots = [sbt(top, "ws%d" % i, (128, 4096), BF16) for i in range(NSLOT)]
        cb = sbt(top, "cb", (128, NCB), BF16)
        pv = sbt(top, "pvt", (128, NPV), F32)
        fl = sbt(top, "flt", (128, 4), F32)
        dv = sbt(top, "dv", (128, 104), F32)
        mod = sbt(top, "mod", (128, 48), F32)
        hT = sbt(top, "hT", (128, 8, T), BF16)
        oT_na = sbt(top, "oTna", (128, 4, T), BF16)
        oT_rw = sbt(top, "oTrw", (128, 4, T), BF16)
        cvb = sbt(top, "cvb", (128, 8), BF16)
        NPF = 6
        PF = [top.enter_context(nc.psum_tensor("pf%d" % i, [128, 512], F32)) for i in range(NPF)]
        PBs = [top.enter_context(nc.psum_tensor("pb%d" % i, [128, 1024], BF16)) for i in range(2)]
        PB = PBs[0]
        pfi = [0]

        live = set()

        def pf():
            for _ in range(8):
                t = PF[pfi[0] % NPF]
                pfi[0] += 1
                if t.name not in live:
                    return t
            raise RuntimeError("no free psum bank")

        wsi = [0]

        wq = []

        def _wload(src, shape, track):
            slot = slots[wsi[0] % NSLOT]
            wsi[0] += 1
            n = _prod(shape[1:])
            v = slot[:, 0:n]
            if len(shape) == 3:
                v = v.rearrange("p (a b) -> p a b", a=shape[1], b=shape[2])
            S.dma("pool", v, src, track=track)
            return v

        def prefetch(src, shape):
            wq.append((repr(src), tuple(shape), _wload(src, shape, False)))

        def wload(src, shape):
            if wq:
                k, sh, v = wq.pop(0)
                assert k == repr(src) and sh == tuple(shape), (k, repr(src))
                return v
            return _wload(src, shape, True)

        def dbgdump(name, ap, shape, dt=F32):
            if name not in dbg:
                return
            d = nc.dram_tensor("dbg_" + name, list(shape), dt, kind="ExternalOutput").ap()
            dbg_out[name] = d
            S.dma("sp", d, ap)

        def chk(tag):
            if stop == tag:
                S.barrier()
                S.flush()
                return True
            return False

        ident = cb[:, CB_ID:CB_ID + 128]
        ones_b = cb[:, CB_ONE:CB_ONE + 128]
        bones = cb[:, CB_BO:CB_BO + 128]
        mLT = cb[:, CB_LT:CB_LT + 64]
        mLE = cb[:, CB_LE:CB_LE + 64]
        mGT = cb[:, CB_GT:CB_GT + 64]
        mGE = cb[:, CB_GE:CB_GE + 64]
        eye64 = cb[:, CB_I64:CB_I64 + 64]

        def bc8(m):
            return m.unsqueeze(1).to_broadcast([128, 8, 64])

        def pcol(i):
            return pv[:, i:i + 1]

        DV_C0, DV_NF0, DV_NF1, DV_HW0, DV_HA0, DV_OMKA, DV_RK05, DV_QG8 = 0, 15, 30, 45, 53, 61, 65, 69
        DV_GS1, DV_GS2, DV_G1H = 70, 78, 86
        DV_HKA, DV_OMHKA = 94, 98

        def dcol(i):
            return dv[:, i:i + 1]

        S.dma("pool", cb[:], I["cstb"][:])
        S.dma("sp", pv[:], I["pv"][:])
        S.dma("sp", fl[:], I["fl"][:])
        S.tt("dve", dv[:, DV_C0:DV_C0 + 15], pv[:, PV_MU0:PV_MU0 + 15], pv[:, PV_MU1:PV_MU1 + 15], ALU.add)
        S.ts("dve", dv[:, DV_C0:DV_C0 + 15], dv[:, DV_C0:DV_C0 + 15], -1.0, ALU.mult, 1.0, ALU.add)
        S.ts("dve", dv[:, DV_NF0:DV_NF0 + 15], pv[:, PV_MU0:PV_MU0 + 15], fl[:, 0:1], ALU.mult, -1.0, ALU.mult)
        S.ts("dve", dv[:, DV_NF1:DV_NF1 + 15], pv[:, PV_MU1:PV_MU1 + 15], fl[:, 0:1], ALU.mult, -1.0, ALU.mult)
        S.ts("dve", dv[:, DV_HW0:DV_HW0 + 8], pv[:, PV_W0:PV_W0 + 8], 0.5, ALU.mult)
        S.ts("dve", dv[:, DV_HA0:DV_HA0 + 8], pv[:, PV_A0:PV_A0 + 8], 0.5, ALU.mult)
        S.ts("dve", dv[:, DV_OMKA:DV_OMKA + 4], pv[:, PV_KA:PV_KA + 4], -1.0, ALU.mult, 1.0, ALU.add)
        S.ts("dve", dv[:, DV_HKA:DV_HKA + 4], pv[:, PV_KA:PV_KA + 4], 0.5, ALU.mult)
        S.ts("dve", dv[:, DV_OMHKA:DV_OMHKA + 4], pv[:, PV_KA:PV_KA + 4], -0.5, ALU.mult, 1.0, ALU.add)
        S.ts("dve", dv[:, DV_RK05:DV_RK05 + 4], pv[:, PV_RK:PV_RK + 4], 0.5, ALU.mult)
        S.ts("dve", dv[:, DV_QG8:DV_QG8 + 1], pv[:, PV_QG:PV_QG + 1], 0.125, ALU.mult)

        def mod_group(g):
            pm = pf()
            w = wload(I["wada"][:, :, g * 512:(g + 1) * 512], (128, 8, 512))
            for j4 in range(4):
                for kc in range(8):
                    S.mm(pm[:, j4:j4 + 1], w[:, kc, j4 * 128:(j4 + 1) * 128], cvb[:, kc:kc + 1],
                         start=(kc == 0), stop=(kc == 7))
            S.tt("dve", mod[:, g * 4:g * 4 + 4], pm[:, 0:4], pv[:, PV_BADA + g * 4:PV_BADA + g * 4 + 4], ALU.add)

        def modulation(j0, j1):
            for g in range(j0 // 4, j1 // 4):
                mod_group(g)

        def rmsnorm_mod(xT, gs0, sh0, tmpA, sqb, rstd):
            pa, pb_ = pf(), pf()
            for kc in range(8):
                S.act(sqb[:, kc % 2, :], xT[:, kc, :], AF.Square)
                for th, pp in enumerate((pa, pb_)):
                    S.mm(pp[:], ones_b, sqb[:, kc % 2, th * 512:(th + 1) * 512], start=(kc == 0), stop=(kc == 7))
            for th, pp in enumerate((pa, pb_)):
                S.act(rstd[:, th * 512:(th + 1) * 512], pp[:], AF.Ln, scale=1.0 / D, bias=RMS_EPS)
            S.act(rstd[:], rstd[:], AF.Exp, scale=-0.5)
            for kc in range(8):
                S.tt("dve", tmpA[:, kc % 2, :], xT[:, kc, :], rstd[:], ALU.mult)
                S.act(hT[:, kc, :], tmpA[:, kc % 2, :], AF.Identity, scale=dv[:, gs0 + kc:gs0 + kc + 1],
                      bias=mod[:, sh0 + kc:sh0 + kc + 1])

        with ExitStack() as ph:
            xT = sbt(ph, "xTa", (128, 8, T), F32)
            cvf = sbt(ph, "cvf", (128, 8), F32)
            tmpA = sbt(ph, "tmpA", (128, 2, T), F32)
            sqb = sbt(ph, "sqb", (128, 2, T), BF16)
            rstd = sbt(ph, "rstd", (128, T), F32)
            S.dma("sp", cvf[:], I["cv"][:])
            S.dma("sp", xT[:], I["xT"][:])
            S.act(cvb[:], cvf[:], AF.Silu)
            modulation(0, 16)
            S.stt(dv[:, DV_GS1:DV_GS1 + 8], mod[:, 8:16], 1.0, pv[:, PV_N1:PV_N1 + 8], ALU.add, ALU.mult)
            rmsnorm_mod(xT, DV_GS1, 0, tmpA, sqb, rstd)
            dbgdump("hT", hT[:], (128, 8, T), BF16)
            prefetch(I["win"][:, :, 1920:1920 + 512], (128, 8, 512))
            prefetch(I["win"][:, :, 2432:2432 + 512], (128, 8, 512))
            S.barrier()
            S.flush()
            if stop == "A":
                return nc, dbg_out

        WC_LORA, WC_RD, WC_NQ, WC_NK, WC_NV, WC_GRW, WC_GNA = 0, 384, 1920, 2432, 2944, 3456, 4480

        with ExitStack() as ph:
            qT = sbt(ph, "qT", (128, 4, T), BF16)
            kT = sbt(ph, "kT", (128, 4, T + 256), BF16)
            vaug = sbt(ph, "vaug", (128, 10, 8, 66), BF16)
            ona = sbt(ph, "ona", (128, 8, 512), BF16)
            U = [sbt(ph, "naU%d" % i, (128, T), F32) for i in range(2)]
            sq = [sbt(ph, "nasq%d" % i, (128, T), BF16) for i in range(2)]
            rin = [sbt(ph, "narin%d" % i, (128, T), F32) for i in range(2)]
            kn = [sbt(ph, "nakn%d" % i, (128, T), F32) for i in range(2)]
            vf = [sbt(ph, "navf%d" % i, (128, 512), F32) for i in range(2)]
            tb = [sbt(ph, "natb%d" % i, (128, NBLK, 128), BF16) for i in range(2)]
            PT = [sbt(ph, "naPT%d" % i, (128, NBLK + 16, 128), BF16) for i in range(2)]
            lg = [sbt(ph, "nalg%d" % i, (128, 512), F32) for i in range(3)]
            rden = sbt(ph, "rden", (128, 8), F32)
            if chk("NA00"):
                return nc, dbg_out
            S.dma("pool", kT[:, :, T:T + 256], I["ckT"][:])
            if chk("NA01"):
                return nc, dbg_out
            cvst = sbt(ph, "cvst", (128, 2, 512), BF16)
            S.dma("pool", cvst[:], I["cvv"][:])
            for a_ in range(2):
                S.copy("dve", vaug[:, 8 + a_, :, 0:64], cvst[:, a_, :].rearrange("p (h d) -> p h d", d=64))
            S.memset("dve", vaug[:, :, :, 64:65], 1.0)
            if chk("NA0"):
                return nc, dbg_out
            for which, wc in ((0, WC_NQ), (1, WC_NK)):
                w = wload(I["win"][:, :, wc:wc + 512], (128, 8, 512))
                for rd in range(4):
                    b = rd % 2
                    pa, pb_ = pf(), pf()
                    for th, pp in enumerate((pa, pb_)):
                        for kc in range(8):
                            S.mm(pp[:], w[:, kc, rd * 128:(rd + 1) * 128], hT[:, kc, th * 512:(th + 1) * 512],
                                 start=(kc == 0), stop=(kc == 7))
                    for th, pp in enumerate((pa, pb_)):
                        S.act(sq[b][:, th * 512:(th + 1) * 512], pp[:], AF.Square)
                        S.copy("dve", U[b][:, th * 512:(th + 1) * 512], pp[:])
                    pc, pd = pf(), pf()
                    for th, pp in enumerate((pc, pd)):
                        S.mm(pp[:], bones, sq[b][:, th * 512:(th + 1) * 512])
                    for th, pp in enumerate((pc, pd)):
                        S.act(rin[b][:, th * 512:(th + 1) * 512], pp[:], AF.Ln, scale=1.0 / 64, bias=RMS_EPS)
                    S.act(rin[b][:], rin[b][:], AF.Exp, scale=-0.5)
                    if which == 0:
                        S.stt(qT[:, rd, :], U[b][:], dcol(DV_QG8), rin[b][:], ALU.mult, ALU.mult)
                    else:
                        S.stt(kn[b][:], U[b][:], pcol(PV_KG), rin[b][:], ALU.mult, ALU.mult)
                        S.copy("act", kT[:, rd, 0:T], kn[b][:])
                        S.dma("sp", O["okT"][:, rd, :], kn[b][:])
            if chk("NA1"):
                return nc, dbg_out
            w = wload(I["win"][:, :, WC_NV:WC_NV + 512], (128, 8, 512))
            for tt_ in range(8):
                pp = pf()
                for kc in range(8):
                    S.mm(pp[:], hT[:, kc, tt_ * 128:(tt_ + 1) * 128], w[:, kc, :], start=(kc == 0), stop=(kc == 7))
                S.copy("act", vf[tt_ % 2][:], pp[:])
                S.copy("dve", vaug[:, tt_, :, 0:64], pp[:].rearrange("p (h d) -> p h d", d=64))
                S.dma("sp", O["ov"][:, tt_, :], vf[tt_ % 2][:])
            dbgdump("qT", qT[:], (128, 4, T), BF16)
            dbgdump("kT", kT[:], (128, 4, T + 256), BF16)
            if chk("NA2"):
                return nc, dbg_out
            li = [0]

            def att_qk(h):
                rd, h2 = h // 2, h % 2
                pr = slice(64 * h2, 64 * h2 + 64)
                tbh = tb[h % 2]
                PTh = PT[h % 2]
                S.dma("pool", tbh[:], I["tbl"][h])
                for j in range(8):
                    Rs = QL[j]
                    q0, nq = Rs[0] * 128, len(Rs) * 128
                    b0 = BLK[(j, Rs[0])]
                    for off in range(0, nq, 512):
                        n = min(512, nq - off)
                        pp = pf()
                        S.mm(pp[:, 0:n], kT[pr, rd, j * 128:(j + 1) * 128], qT[pr, rd, q0 + off:q0 + off + n])
                        bb = b0 + off // 128
                        l = lg[li[0] % 3]
                        li[0] += 1
                        S.tt("dve", l[:, 0:n], pp[:, 0:n],
                             tbh[:, bb:bb + n // 128, :].rearrange("p a b -> p (a b)"), ALU.add)
                        S.act(PTh[:, bb:bb + n // 128, :].rearrange("p a b -> p (a b)"), l[:, 0:n], AF.Exp)
                for jc in range(2):
                    for off in range(0, T, 512):
                        pp = pf()
                        S.mm(pp[:], kT[pr, rd, T + jc * 128:T + (jc + 1) * 128], qT[pr, rd, off:off + 512])
                        bb = NBLK + jc * 8 + off // 128
                        S.act(PTh[:, bb:bb + 4, :].rearrange("p a b -> p (a b)"), pp[:], AF.Exp, bias=fl[:, 1:2])

            def att_pv(h):
                PTh = PT[h % 2]
                for half in range(2):
                    po = pf()
                    for r4 in range(4):
                        R = half * 4 + r4
                        w0 = _qwin(R)
                        keys = [(jj, BLK[(jj, R)]) for jj in range(w0, w0 + 5)] + \
                               [(8 + jc, NBLK + jc * 8 + R) for jc in range(2)]
                        for n_, (vt, bb) in enumerate(keys):
                            S.mm(po[:, r4 * 65:(r4 + 1) * 65], PTh[:, bb, :], vaug[:, vt, h, 0:65],
                                 start=(n_ == 0), stop=(n_ == len(keys) - 1))
                    pov = po[:, 0:260].rearrange("p (r c) -> p r c", c=65)
                    S.recip(rden[:, half * 4:(half + 1) * 4], pov[:, :, 64])
                    S.tt("dve", ona[:, half * 4:(half + 1) * 4, h * 64:(h + 1) * 64], pov[:, :, 0:64],
                         rden[:, half * 4:(half + 1) * 4].unsqueeze(2).to_broadcast([128, 4, 64]), ALU.mult)

            att_qk(0)
            for h in range(8):
                if h + 1 < 8:
                    att_qk(h + 1)
                att_pv(h)
            if chk("NA3"):
                return nc, dbg_out
            for ct in range(4):
                for R in range(8):
                    S.tr(PB[:, R * 128:(R + 1) * 128], ona[:, R, ct * 128:(ct + 1) * 128], ident)
                S.copy("act", oT_na[:, ct, :], PB[:])
            dbgdump("oTna", oT_na[:], (128, 4, T), BF16)
            prefetch(I["win"][:, :, WC_LORA:WC_LORA + 384], (128, 8, 384))
            prefetch(I["win"][:, :, WC_RD:WC_RD + 384], (128, 8, 384))
            S.barrier()
            S.flush()
            if stop == "NA":
                return nc, dbg_out

        with ExitStack() as ph:
            twd = sbt(ph, "twd", (128, T), BF16)
            adb = sbt(ph, "adb", (128, T), BF16)
            sgd = sbt(ph, "sgd", (128, T), BF16)
            wupb = sbt(ph, "wupb", (128, 512), BF16)
            aupb = sbt(ph, "aupb", (128, 512), BF16)
            gupb = sbt(ph, "gupb", (128, 512), BF16)
            lnp = [sbt(ph, "lnpt%d" % i, (128, 2, 64), F32) for i in range(2)]
            smask = sbt(ph, "smaskt", (128, 512), F32)
            s0r = [sbt(ph, "s0r%d" % i, (128, 2, 64), F32) for i in range(2)]
            bon = [sbt(ph, "bon%d" % i, (128, 16), F32) for i in range(2)]
            Uraw = sbt(ph, "Uraw", (128, T), F32)
            Rp = sbt(ph, "Rp", (128, T), F32)
            Kp = sbt(ph, "Kp", (128, T), F32)
            VKD = sbt(ph, "VKD", (128, T), F32)
            Vp = VKD
            KD = VKD[:, 0:512]
            Dx = VKD[:, 512:1024]
            Vb = [sbt(ph, "Vb0", (128, T), BF16)] * 2
            Vtm = [sbt(ph, "Vtm%d" % i, (128, 16, 64), BF16) for i in range(2)]
            gfm = [sbt(ph, "gfm%d" % i, (128, T), BF16) for i in range(2)]
            kkn = sbt(ph, "kkn", (128, 512), F32)
            sqk = sbt(ph, "sqk", (128, 512), BF16)
            LW = sbt(ph, "LW", (128, 512), F32)
            C = sbt(ph, "Ccs", (128, 512), F32)
            IC = sbt(ph, "IC", (128, 512), F32)
            E0 = sbt(ph, "E0", (128, 512), F32)
            PBh = [sbt(ph, "PBh%d" % d, (128, 512), BF16) for d in range(2)]
            AR = [[sbt(ph, "AR%d%d" % (h, d), (128, 8, 2, 64), BF16) for d in range(2)] for h in range(2)]
            BK = [[sbt(ph, "BK%d%d" % (h, d), (128, 8, 2, 64), BF16) for d in range(2)] for h in range(2)]
            AA = [[sbt(ph, "AA%d%d" % (h, d), (128, 8, 2, 64), BF16) for d in range(2)] for h in range(2)]
            AB = [[sbt(ph, "AB%d%d" % (h, d), (128, 8, 2, 64), BF16) for d in range(2)] for h in range(2)]
            KTM = [[sbt(ph, "KTM%d%d" % (h, d), (128, 8, 64), BF16) for d in range(2)] for h in range(2)]
            PN = [[sbt(ph, "PN%d%d" % (h, d), (128, 8, 2, 64), BF16) for d in range(2)] for h in range(2)]
            NT = [[sbt(ph, "NT%d%d" % (h, d), (128, 8, 64), BF16) for d in range(2)] for h in range(2)]
            AV = [[sbt(ph, "AV%d%d" % (h, d), (128, 8, 2, 64), BF16) for d in range(2)] for h in range(2)]
            Rh = sbt(ph, "Rh", (128, 2, 16, 64), BF16)
            G0 = sbt(ph, "G0", (128, 2, 16, 64), BF16)
            Avv = sbt(ph, "Avv", (128, 2, 16, 64), BF16)
            H0 = sbt(ph, "H0", (128, 2, 16, 64), BF16)
            Sall = [[sbt(ph, "S%d_%d" % (d, i), (128, 64), BF16) for i in range(17)] for d in range(2)]
            SF = sbt(ph, "SF", (128, 2, 4, 64), F32)
            wend = sbt(ph, "wend", (128, 2, 16), F32)
            wendc = sbt(ph, "wendc", (128, 2, 16), F32)
            wk = sbt(ph, "wk", (128, 2, 16), F32)
            Yf = sbt(ph, "Yf", (128, 16, 64), F32)
            Yq = sbt(ph, "Yq", (128, 16, 64), BF16)
            otm = sbt(ph, "otm", (128, 16, 64), BF16)
            st4 = sbt(ph, "st4", (128, 6, 16), F32)

            S.dma("pool", wupb[:], I["wup"][:])
            S.dma("pool", aupb[:], I["aup"][:])
            S.dma("pool", gupb[:], I["gup"][:])
            S.dma("sp", smask[:], I["smask"][:, 0:512])

            def inproj_shift(w, ti, q, dest):
                pa, pb_ = pf(), pf()
                for th, pp in enumerate((pa, pb_)):
                    for kc in range(8):
                        S.mm(pp[:], w[:, kc, ti * 128:(ti + 1) * 128], hT[:, kc, th * 512:(th + 1) * 512],
                             start=(kc == 0), stop=(kc == 7))
                for th, pp in enumerate((pa, pb_)):
                    S.copy("act", Uraw[:, th * 512:(th + 1) * 512], pp[:])
                    S.ts("dve", dest[:, th * 512:(th + 1) * 512], pp[:], dcol(DV_C0 + q), ALU.mult)
                S.stt(dest[:, 1:T], Uraw[:, 0:T - 1], pcol(PV_MU0 + q), dest[:, 1:T], ALU.mult, ALU.add)
                S.stt(dest[:, 0:T - 1], Uraw[:, 1:T], pcol(PV_MU1 + q), dest[:, 0:T - 1], ALU.mult, ALU.add)
                S.stt(dest[:, 256:T:256], Uraw[:, 255:T - 1:256], dcol(DV_NF0 + q), dest[:, 256:T:256],
                      ALU.mult, ALU.add)
                S.stt(dest[:, 255:T - 1:256], Uraw[:, 256:T:256], dcol(DV_NF1 + q), dest[:, 255:T - 1:256],
                      ALU.mult, ALU.add)

            w = wload(I["win"][:, :, WC_LORA:WC_LORA + 384], (128, 8, 384))
            inproj_shift(w, 0, 0, Rp)
            S.act(twd[:], Rp[:], AF.Tanh)
            inproj_shift(w, 1, 1, Kp)
            S.copy("act", adb[:], Kp[:])
            inproj_shift(w, 2, 2, Vp)
            S.act(Yf[:].rearrange("p c v -> p (c v)"), Vp[:], AF.Tanh, scale=0.5)
            S.ts("dve", sgd[:], Yf[:].rearrange("p c v -> p (c v)"), 0.5, ALU.mult, 0.5, ALU.add)

            def v3(ap):
                return ap.rearrange("p (c t) -> p c t", t=64)

            mstrict = (mLT, mGT)
            mincl = (mLE, mGE)
            mstrictT = (mGT, mLT)

            def bc4(m):
                return m.unsqueeze(1).to_broadcast([128, 4, 64])

            def pv4(p):
                return p[:].rearrange("p (c x) -> p c x", x=128)

            def f128(ap):
                return ap.rearrange("p a b -> p (a b)")

            def pfl():
                t = pf()
                live.add(t.name)
                return t

            def rel(banks):
                for t in banks:
                    live.discard(t.name)

            def run(*gens):
                gens = list(gens)
                while gens:
                    for g_ in list(gens):
                        try:
                            next(g_)
                        except StopIteration:
                            gens.remove(g_)

            def head(rd):
                p = rd % 2
                S.dma("sp", lnp[p][:], I["lnp"][:, :, rd, :])
                S.dma("sp", s0r[p][:], I["s0T"][:, rd, :, :])
                w = wload(I["win"][:, :, WC_RD + rd * 384:WC_RD + (rd + 1) * 384], (128, 8, 384))
                for ti, dest in enumerate((Rp, Kp, Vp)):
                    inproj_shift(w, ti, 3 + rd * 3 + ti, dest)
                    yield
                mod_group(4 + 2 * rd)
                mod_group(5 + 2 * rd)
                yield
                S.copy("act", Vb[p][:], Vp[:])
                for c in range(16):
                    for h2 in range(2):
                        pr = slice(64 * h2, 64 * h2 + 64)
                        S.tr(PB[pr, c * 64:(c + 1) * 64], Vb[p][pr, c * 64:(c + 1) * 64], ident[pr, pr])
                S.copy("act", Vtm[p][:].rearrange("p c v -> p (c v)"), PB[:])
                yield
                for th in range(2):
                    pp = pf()
                    S.mm(pp[:], gupb[:, rd * 128:(rd + 1) * 128], sgd[:, th * 512:(th + 1) * 512])
                    S.copy("act", gfm[p][:, th * 512:(th + 1) * 512], pp[:])
                yield

            pbi = [0]

            def prep(rd, half):
                hs = slice(half * 512, half * 512 + 512)
                S.act(sqk[:], Kp[:, hs], AF.Square, scale=pcol(PV_KK + rd))
                pp = pf()
                S.mm(pp[:], bones, sqk[:])
                S.act(E0[:], pp[:], AF.Ln, bias=L2_EPS)
                S.act(E0[:], E0[:], AF.Exp, scale=-0.5)
                S.stt(kkn[:], Kp[:, hs], pcol(PV_KK + rd), E0[:], ALU.mult, ALU.mult)
                yield
                for d in range(2):
                    ARd, BKd, AAd, ABd, KTMd = AR[half][d], BK[half][d], AA[half][d], AB[half][d], KTM[half][d]
                    pd = slice(64 * d, 64 * d + 64)
                    pp = pf()
                    S.mm(pp[:], wupb[pd, rd * 128:(rd + 1) * 128], twd[pd, hs])
                    S.act(LW[:], pp[:], AF.Tanh, scale=0.5, bias=dcol(DV_HW0 + d * 4 + rd))
                    S.act(LW[:], LW[:], AF.Identity, scale=LWC, bias=LWC)
                    S.scan(C[:], smask[:], LW[:], 0.0, ALU.mult, ALU.add)
                    pp = pf()
                    S.mm(pp[:], aupb[pd, rd * 128:(rd + 1) * 128], adb[pd, hs])
                    S.act(IC[:], pp[:], AF.Tanh, scale=0.5, bias=dcol(DV_HA0 + d * 4 + rd))
                    yield
                    S.act(Dx[:], IC[:], AF.Identity, scale=dcol(DV_HKA + rd), bias=dcol(DV_OMHKA + rd))
                    S.act(IC[:], IC[:], AF.Identity, scale=0.5, bias=0.5)
                    S.tt("dve", KD[:], Kp[:, hs], Dx[:], ALU.mult)
                    S.stt(PBh[d][:], Rp[:, hs], dcol(DV_RK05 + rd), KD[:], ALU.mult, ALU.mult)
                    S.tt("dve", IC[:], kkn[:], IC[:], ALU.mult)
                    yield
                    tot = v3(C[:])[:, :, 63]
                    if d == 0:
                        S.tt("dve", Dx[:], C[:], LW[:], ALU.subtract)
                        logW = C
                    else:
                        S.tt("dve", v3(Dx[:]), tot.unsqueeze(2).to_broadcast([128, 8, 64]), v3(C[:]), ALU.subtract)
                        S.tt("dve", LW[:], Dx[:], LW[:], ALU.add)
                        logW = LW
                    S.act(wend[:, d, half * 8:(half + 1) * 8], tot, AF.Exp)
                    S.act(E0[:], logW[:], AF.Exp)
                    S.tt("dve", ARd[:, :, 1, :], v3(Rp[:, hs]), v3(E0[:]), ALU.mult)
                    S.act(E0[:], Dx[:], AF.Exp)
                    S.stt(ARd[:, :, 0, :], v3(kkn[:]), -1.0, v3(E0[:]), ALU.mult, ALU.mult)
                    yield
                    S.act(E0[:], logW[:], AF.Exp, scale=-1.0)
                    S.tt("dve", BKd[:, :, 0, :], v3(IC[:]), v3(E0[:]), ALU.mult)
                    S.tt("dve", BKd[:, :, 1, :], v3(KD[:]), v3(E0[:]), ALU.mult)
                    yield
                    for src, dst, eng in ((ARd[:, :, 0, :], AAd[:, :, 0, :], "act"),
                                          (BKd[:, :, 0, :], ABd[:, :, 1, :], "act"),
                                          (BKd[:, :, 1, :], KTMd[:], "act")):
                        PBx = PBs[pbi[0] % 2]
                        pbi[0] += 1
                        for c in range(8):
                            for h2 in range(2):
                                pr = slice(64 * h2, 64 * h2 + 64)
                                S.tr(PBx[pr, c * 64:(c + 1) * 64], src[pr, c, :], ident[pr, pr])
                        S.copy(eng, dst, v3(PBx[:, 0:512]))
                        yield
                pbn = pf()
                for c in range(8):
                    for h2 in range(2):
                        pr = slice(64 * h2, 64 * h2 + 64)
                        for d in range(2):
                            S.mm(pbn[pr, c:c + 1], PBh[d][pr, c * 64:(c + 1) * 64], ones_b[pr, 0:1],
                                 start=(d == 0), stop=(d == 1))
                S.copy("act", bon[rd % 2][:, half * 8:(half + 1) * 8], pbn[:, 0:8])
                yield

            def chunk_pipe(d, half):
                ms, mi, mT = mstrict[d], mincl[d], mstrictT[d]
                ARd, BKd, AAd, ABd, KTMd = AR[half][d], BK[half][d], AA[half][d], AB[half][d], KTM[half][d]
                PNs, NTs, AVs = PN[half][d], NT[half][d], AV[half][d]
                g8 = slice(half * 8, half * 8 + 8)

                def stg(lhs, rhs, wide):
                    banks = [pfl(), pfl()] if wide else [pfl()]
                    for c in range(8):
                        for h2 in range(2):
                            pr = slice(64 * h2, 64 * h2 + 64)
                            if wide:
                                o = banks[c // 4][pr, (c % 4) * 128:(c % 4 + 1) * 128]
                            else:
                                o = banks[0][pr, c * 64:(c + 1) * 64]
                            S.mm(o, lhs(pr, c), rhs(pr, c))
                    return banks

                o = stg(lambda pr, c: BKd[pr, c, 0, :], lambda pr, c: f128(ARd[pr, c, :, :]), True)
                yield "mm"
                for b in range(2):
                    bs = slice(b * 4, b * 4 + 4)
                    S.tt("dve", PNs[:, bs, 1, :], pv4(o[b])[:, :, 0:64], bc4(ms), ALU.mult)
                    S.tt("dve", ABd[:, bs, 0, :], pv4(o[b])[:, :, 64:128], bc4(mi), ALU.mult)
                rel(o)
                S.tt("pool", PNs[:, :, 0, :], PNs[:, :, 1, :], bc8(eye64), ALU.add)
                yield "ev"
                o = stg(lambda pr, c: BKd[pr, c, 1, :], lambda pr, c: ARd[pr, c, 1, :], False)
                yield "mm"
                S.tt("dve", Avv[:, d, g8, :], v3(o[0][:]), bc8(mi), ALU.mult)
                rel(o)
                yield "ev"
                o = stg(lambda pr, c: ARd[pr, c, 0, :], lambda pr, c: f128(BKd[pr, c, :, :]), True)
                yield "mm"
                for b in range(2):
                    bs = slice(b * 4, b * 4 + 4)
                    S.tt("dve", NTs[:, bs, :], pv4(o[b])[:, :, 0:64], bc4(mT), ALU.mult)
                    S.tt("dve", AAd[:, bs, 1, :], pv4(o[b])[:, :, 64:128], bc4(mT), ALU.mult)
                rel(o)
                yield "ev"
                oa = stg(lambda pr, c: NTs[pr, c, :], lambda pr, c: PNs[pr, c, 1, :], False)
                ob = stg(lambda pr, c: PNs[pr, c, 1, :], lambda pr, c: NTs[pr, c, :], False)
                yield "mm"
                S.copy("act", PNs[:, :, 1, :], v3(oa[0][:]))
                S.copy("act", NTs[:], v3(ob[0][:]))
                rel(oa + ob)
                yield "ev"
                for step in range(4):
                    oa = stg(lambda pr, c: NTs[pr, c, :], lambda pr, c: f128(PNs[pr, c, :, :]), True)
                    ob = stg(lambda pr, c: PNs[pr, c, 1, :], lambda pr, c: NTs[pr, c, :], False)
                    yield "mm"
                    S.copy("act", NTs[:], v3(ob[0][:]))
                    S.tt("dve", PNs[:, 0:4, 0, :], pv4(oa[0])[:, :, 0:64], PNs[:, 0:4, 0, :], ALU.add)
                    S.copy("act", PNs[:, 0:4, 1, :], pv4(oa[0])[:, :, 64:128])
                    S.tt("dve", PNs[:, 4:8, 0, :], pv4(oa[1])[:, :, 0:64], PNs[:, 4:8, 0, :], ALU.add)
                    S.copy("act", PNs[:, 4:8, 1, :], pv4(oa[1])[:, :, 64:128])
                    rel(oa + ob)
                    yield "ev"
                oa = stg(lambda pr, c: NTs[pr, c, :], lambda pr, c: PNs[pr, c, 0, :], False)
                yield "mm"
                S.tt("dve", PNs[:, :, 0, :], v3(oa[0][:]), PNs[:, :, 0, :], ALU.add)
                rel(oa)
                yield "ev"
                o = stg(lambda pr, c: PNs[pr, c, 0, :], lambda pr, c: f128(AAd[pr, c, :, :]), True)
                yield "mm"
                for b in range(2):
                    bs = slice(b * 4, b * 4 + 4)
                    S.copy("act", AVs[:, bs, :, :].rearrange("p c a b -> p c (a b)"), pv4(o[b]))
                rel(o)
                yield "ev"
                o = stg(lambda pr, c: AVs[pr, c, 0, :], lambda pr, c: f128(ABd[pr, c, :, :]), True)
                yield "mm"
                for b in range(2):
                    bs = slice(b * 4, b * 4 + 4)
                    gs_ = slice(half * 8 + b * 4, half * 8 + b * 4 + 4)
                    S.tt("dve", Rh[:, d, gs_, :], pv4(o[b])[:, :, 0:64], ARd[:, bs, 1, :], ALU.add)
                    S.tt("dve", G0[:, d, gs_, :], pv4(o[b])[:, :, 64:128], bc4(eye64), ALU.add)
                rel(o)
                yield "ev"
                o = stg(lambda pr, c: AVs[pr, c, 1, :], lambda pr, c: f128(ABd[pr, c, :, :]), True)
                yield "mm"
                for b in range(2):
                    bs = slice(b * 4, b * 4 + 4)
                    gs_ = slice(half * 8 + b * 4, half * 8 + b * 4 + 4)
                    S.tt("dve", Avv[:, d, gs_, :], pv4(o[b])[:, :, 0:64], Avv[:, d, gs_, :], ALU.add)
                    S.tt("dve", H0[:, d, gs_, :], pv4(o[b])[:, :, 64:128], KTMd[:, bs, :], ALU.add)
                rel(o)

            def run_pipes(gens):
                idle = list(range(len(gens)))
                holding = []
                while idle or holding:
                    while len(holding) < 2 and idle:
                        i = idle.pop(0)
                        tag = next(gens[i])
                        assert tag == "mm"
                        holding.append(i)
                    i = holding.pop(0)
                    try:
                        tag = next(gens[i])
                        assert tag == "ev"
                        idle.append(i)
                    except StopIteration:
                        pass

            def tail(rd):
                p = rd % 2
                Vt = Vtm[p]
                S.copy("dve", wendc[:], wend[:])
                S.copy("dve", wk[:], wend[:])
                S.ts("dve", wk[:, 0, 3:12:4], wendc[:, 0, 3:12:4], fl[:, 2:3], ALU.mult)
                S.ts("dve", wk[:, 1, 4:13:4], wendc[:, 1, 4:13:4], fl[:, 2:3], ALU.mult)
                S.copy("dve", Sall[0][0][:], s0r[p][:, 0, :])
                S.copy("dve", Sall[1][16][:], s0r[p][:, 1, :])
                yield
                pys = (pfl(), pfl())

                def ymm(c):
                    for h2 in range(2):
                        pr = slice(64 * h2, 64 * h2 + 64)
                        o = pys[c // 8][pr, (c % 8) * 64:(c % 8 + 1) * 64]
                        S.mm(o, Avv[pr, 0, c, :], Vt[pr, c, :], start=True, stop=False)
                        S.mm(o, Avv[pr, 1, c, :], Vt[pr, c, :], start=False, stop=False)
                        S.mm(o, Rh[pr, 0, c, :], Sall[0][c][pr, :], start=False, stop=False)
                        S.mm(o, Rh[pr, 1, c, :], Sall[1][c + 1][pr, :], start=False, stop=True)

                for step in range(16):
                    cf, cbk = step, 15 - step
                    pcs = (pfl(), pfl())
                    for d, c, sin in ((0, cf, cf), (1, cbk, cbk + 1)):
                        for h2 in range(2):
                            pr = slice(64 * h2, 64 * h2 + 64)
                            S.mm(pcs[d][pr, 0:64], H0[pr, d, c, :], Vt[pr, c, :], start=True, stop=False)
                            S.mm(pcs[d][pr, 0:64], G0[pr, d, c, :], Sall[d][sin][pr, :], start=False, stop=True)
                    if step >= 8:
                        ymm(15 - step)
                        ymm(step)
                    yield
                    S.act(Sall[0][cf + 1][:], pcs[0][:, 0:64], AF.Identity, scale=wk[:, 0, cf:cf + 1])
                    S.ts("dve", Sall[1][cbk][:], pcs[1][:, 0:64], wk[:, 1, cbk:cbk + 1], ALU.mult)
                    if cf % 4 == 3:
                        S.act(SF[:, 0, cf // 4, :], pcs[0][:, 0:64], AF.Identity, scale=wendc[:, 0, cf:cf + 1])
                        S.ts("dve", SF[:, 1, cbk // 4, :], pcs[1][:, 0:64], wendc[:, 1, cbk:cbk + 1], ALU.mult)
                    rel(list(pcs))
                S.dma("sp", O["oS"][:, rd, :, :, :], SF[:])
                for b in range(2):
                    S.copy("act", Yf[:, b * 8:(b + 1) * 8, :], v3(pys[b][:]))
                rel(list(pys))
                yield
                S.reduce(st4[:, 0, :], Yf[:], ALU.add)
                S.tt("dve", Yq[:], Yf[:], Yf[:], ALU.mult)
                S.reduce(st4[:, 1, :], Yq[:], ALU.add)
                yield
                S.ts("dve", st4[:, 0, :], st4[:, 0, :], 1.0 / 64, ALU.mult)
                S.tt("dve", st4[:, 2, :], st4[:, 0, :], st4[:, 0, :], ALU.mult)
                S.stt(st4[:, 3, :], st4[:, 1, :], 1.0 / 64, st4[:, 2, :], ALU.mult, ALU.subtract)
                S.act(st4[:, 3, :], st4[:, 3, :], AF.Ln, bias=GN_EPS)
                S.act(st4[:, 3, :], st4[:, 3, :], AF.Exp, scale=-0.5)
                b16 = lambda ap: ap.unsqueeze(2).to_broadcast([128, 16, 64])
                g16 = lambda ap: ap.unsqueeze(1).to_broadcast([128, 16, 64])
                S.tt("dve", Yq[:], Yf[:], b16(st4[:, 0, :]), ALU.subtract)
                yield
                S.tt("dve", Yq[:], Yq[:], b16(st4[:, 3, :]), ALU.mult)
                S.tt("dve", Yq[:], Yq[:], g16(lnp[p][:, 0, :]), ALU.mult)
                yield
                S.tt("dve", Yq[:], Yq[:], g16(lnp[p][:, 1, :]), ALU.add)
                S.tt("dve", Yf[:], Vt[:], b16(bon[p][:]), ALU.mult)
                yield
                S.tt("dve", otm[:], Yq[:], Yf[:], ALU.add)
                for c in range(16):
                    for h2 in range(2):
                        pr = slice(64 * h2, 64 * h2 + 64)
                        S.tr(PBs[1][pr, c * 64:(c + 1) * 64], otm[pr, c, :], ident[pr, pr])
                S.tt("dve", oT_rw[:, rd, :], PBs[1][:], gfm[p][:], ALU.mult)
                yield

            def nxt(r):
                yield from head(r)
                yield from prep(r, 0)
                yield from prep(r, 1)

            run(nxt(0))
            for rd in range(4):
                run_pipes([chunk_pipe(0, 0), chunk_pipe(1, 0), chunk_pipe(0, 1), chunk_pipe(1, 1)])
                if rd < 3:
                    t_, n_ = tail(rd), nxt(rd + 1)
                    alive = True
                    while alive:
                        alive = False
                        for g_, k_ in ((t_, 1), (n_, 2)):
                            for _ in range(k_):
                                try:
                                    next(g_)
                                    alive = True
                                except StopIteration:
                                    break
                else:
                    run(tail(rd))
            dbgdump("oTrw", oT_rw[:], (128, 4, T), BF16)
            prefetch(I["wor"][:], (128, 4, D))
            prefetch(I["won"][:], (128, 4, D))
            prefetch(I["win"][:, :, WC_GRW:WC_GRW + 512], (128, 8, 512))
            prefetch(I["win"][:, :, WC_GNA:WC_GNA + 512], (128, 8, 512))
            S.barrier()
            S.flush()
            if stop == "RW":
                return nc, dbg_out

        with ExitStack() as phx, ExitStack() as ph:
            xT = sbt(phx, "xTm", (128, 8, T), F32)
            mrg = sbt(ph, "mrg", (128, 8, T), BF16)
            tg = [sbt(ph, "tg%d" % i, (128, 512), F32) for i in range(4)]
            t1 = [sbt(ph, "t1%d" % i, (128, 512), F32) for i in range(2)]
            tmpA = sbt(ph, "tmpB", (128, 2, T), F32)
            sqb = sbt(ph, "sqb2", (128, 2, T), BF16)
            rstd = sbt(ph, "rstd2", (128, T), F32)
            S.dma("sp", xT[:], I["xT"][:])
            S.ts("dve", dv[:, DV_G1H:DV_G1H + 8], mod[:, 16:24], 0.5, ALU.mult)
            S.stt(dv[:, DV_GS2:DV_GS2 + 8], mod[:, 32:40], 1.0, pv[:, PV_N2:PV_N2 + 8], ALU.add, ALU.mult)
            ti = 0
            for g in range(2):
                wor = wload(I["wor"][:], (128, 4, D))
                won = wload(I["won"][:], (128, 4, D))
                wgr = wload(I["win"][:, :, WC_GRW + g * 512:WC_GRW + (g + 1) * 512], (128, 8, 512))
                wgn = wload(I["win"][:, :, WC_GNA + g * 512:WC_GNA + (g + 1) * 512], (128, 8, 512))
                for j4 in range(4):
                    j = g * 4 + j4
                    for th in range(2):
                        ts_ = slice(th * 512, th * 512 + 512)
                        pgr, pgn, pa, pb_ = pf(), pf(), pf(), pf()
                        for kc in range(8):
                            S.mm(pgr[:], wgr[:, kc, j4 * 128:(j4 + 1) * 128], hT[:, kc, ts_], start=(kc == 0), stop=(kc == 7))
                        for kc in range(8):
                            S.mm(pgn[:], wgn[:, kc, j4 * 128:(j4 + 1) * 128], hT[:, kc, ts_], start=(kc == 0), stop=(kc == 7))
                        for kc in range(4):
                            S.mm(pa[:], wor[:, kc, j * 128:(j + 1) * 128], oT_rw[:, kc, ts_], start=(kc == 0), stop=(kc == 3))
                        for kc in range(4):
                            S.mm(pb_[:], won[:, kc, j * 128:(j + 1) * 128], oT_na[:, kc, ts_], start=(kc == 0), stop=(kc == 3))
                        ta, tb_ = tg[ti % 4], tg[(ti + 1) % 4]
                        tt1 = t1[(ti // 2) % 2]
                        ti += 2
                        S.act(ta[:], pgr[:], AF.Tanh, scale=0.5)
                        S.act(tb_[:], pgn[:], AF.Tanh, scale=0.5)
                        S.stt(tt1[:], ta[:], 1.0, pa[:], ALU.add, ALU.mult)
                        S.stt(ta[:], tb_[:], 1.0, pb_[:], ALU.add, ALU.mult)
                        S.tt("dve", mrg[:, j, ts_], tt1[:], ta[:], ALU.add)
            dbgdump("mrg", mrg[:], (128, 8, T), BF16)
            for g in range(2):
                wo = wload(I["wout"][:, :, g * 512:(g + 1) * 512], (128, 8, 512))
                for j4 in range(4):
                    j = g * 4 + j4
                    for th in range(2):
                        ts_ = slice(th * 512, th * 512 + 512)
                        pp = pf()
                        for kc in range(8):
                            S.mm(pp[:], wo[:, kc, j4 * 128:(j4 + 1) * 128], mrg[:, kc, ts_], start=(kc == 0), stop=(kc == 7))
                        S.stt(xT[:, j, ts_], pp[:], dcol(DV_G1H + j), xT[:, j, ts_], ALU.mult, ALU.add)
            dbgdump("x1", xT[:], (128, 8, T))
            prefetch(I["w1"][:, :, 0:512], (128, 8, 512))
            prefetch(I["w3"][:, :, 0:512], (128, 8, 512))
            rmsnorm_mod(xT, DV_GS2, 24, tmpA, sqb, rstd)
            S.barrier()
            S.flush()
            ph.close()

            with ExitStack() as ph2:
                actb = sbt(ph2, "actb", (128, NFT, T), BF16)
                sl = [sbt(ph2, "sl%d" % i, (128, 512), F32) for i in range(3)]
                yo = [sbt(ph2, "yo%d" % i, (128, 512), F32) for i in range(2)]
                ti = 0
                for g in range(6):
                    nt = min(4, NFT - g * 4)
                    wa = wload(I["w1"][:, :, g * 512:g * 512 + nt * 128], (128, 8, nt * 128))
                    wb = wload(I["w3"][:, :, g * 512:g * 512 + nt * 128], (128, 8, nt * 128))
                    for f4 in range(nt):
                        f = g * 4 + f4
                        for th in range(2):
                            ts_ = slice(th * 512, th * 512 + 512)
                            pa, pb_ = pf(), pf()
                            for kc in range(8):
                                S.mm(pa[:], wa[:, kc, f4 * 128:(f4 + 1) * 128], hT[:, kc, ts_], start=(kc == 0), stop=(kc == 7))
                            for kc in range(8):
                                S.mm(pb_[:], wb[:, kc, f4 * 128:(f4 + 1) * 128], hT[:, kc, ts_], start=(kc == 0), stop=(kc == 7))
                            s_ = sl[ti % 3]
                            ti += 1
                            S.act(s_[:], pa[:], AF.Silu)
                            S.tt("dve", actb[:, f, ts_], s_[:], pb_[:], ALU.mult)
                oi = 0
                for j in range(8):
                    w2 = wload(I["w2"][:, :, j * 128:(j + 1) * 128], (128, NFT, 128))
                    for th in range(2):
                        ts_ = slice(th * 512, th * 512 + 512)
                        pp = pf()
                        for f in range(NFT):
                            S.mm(pp[:], w2[:, f, :], actb[:, f, ts_], start=(f == 0), stop=(f == NFT - 1))
                        y_ = yo[oi % 2]
                        oi += 1
                        S.stt(y_[:], pp[:], mod[:, 40 + j:41 + j], xT[:, j, ts_], ALU.mult, ALU.add)
                        S.dma("sp", O["yT"][:, j, ts_], y_[:])
                S.barrier()
                S.flush()
    return nc, dbg_out


def _fm(w):
    K, N = w.shape
    return np.ascontiguousarray(w.reshape(K // 128, 128, N).transpose(1, 0, 2))


def _consts():
    cst = np.zeros((128, NCB), np.float32)
    p = np.arange(128)
    cst[:, CB_ID:CB_ID + 128] = np.eye(128, dtype=np.float32)
    cst[:, CB_ONE:CB_ONE + 128] = 1.0
    cst[:, CB_BO:CB_BO + 128] = (p[:, None] // 64 == p[None, :] // 64)
    i = (p % 64)[:, None]
    f = np.arange(64)[None, :]
    cst[:, CB_LT:CB_LT + 64] = i < f
    cst[:, CB_LE:CB_LE + 64] = i <= f
    cst[:, CB_GT:CB_GT + 64] = i > f
    cst[:, CB_GE:CB_GE + 64] = i >= f
    cst[:, CB_I64:CB_I64 + 64] = i == f
    sm = np.ones((128, T), np.float32)
    sm[:, ::64] = 0.0
    return cst, sm


def _bias_tables(na_rpb):
    GW, ROWS, NR, NCOL = 64, 16, 8, 16
    t = np.arange(T)
    row, col = t // GW, t % GW
    row_start = np.clip(np.arange(ROWS) - NR // 2, 0, ROWS - NR)
    win_start = np.clip(np.arange(GW) - NCOL // 2, 0, GW - NCOL)
    qr, qc = row[:, None], col[:, None]
    kr, kc = row[None, :], col[None, :]
    valid = (kr >= row_start[qr]) & (kr < row_start[qr] + NR) & (kc >= win_start[qc]) & (kc < win_start[qc] + NCOL)
    dr = np.clip(kr - qr + NR - 1, 0, 2 * NR - 2)
    dc = np.clip(kc - qc, -(NCOL - 1), NCOL - 1) + NCOL - 1
    Bs = np.where(valid[None], na_rpb[:, dr, dc], np.float32(NEGM)).astype(np.float32)
    same = (t[:, None] // 256) == (t[None, :] // 256)
    Bp = np.where(same, np.float32(0.0), np.float32(NEGM)).astype(np.float32)[None]

    def tile(B):
        out = np.empty((B.shape[0], 128, NBLK, 128), np.float32)
        for (j, R), b in BLK.items():
            out[:, :, b, :] = B[:, R * 128:(R + 1) * 128, j * 128:(j + 1) * 128].transpose(0, 2, 1)
        return out
    ts_ = tile(Bs)
    tp = np.ascontiguousarray(np.broadcast_to(tile(Bp), (8, 128, NBLK, 128)))
    return tp, ts_


_CACHE = {}


def _get_nc():
    if "nc" not in _CACHE:
        _CACHE["nc"] = build_nc()[0]
    return _CACHE["nc"]


def _prep(inp):
    f = lambda a: np.ascontiguousarray(np.asarray(a, dtype=np.float32))
    w_in = f(inp["w_in"])[0]
    perm = list(range(1536, 1920))
    for rd in range(4):
        for ti in range(3):
            perm += list(range(ti * 512 + rd * 128, ti * 512 + rd * 128 + 128))
    perm += list(range(1920, 5504))
    perm = np.array(perm)
    otile = [12, 13, 14] + [ti * 4 + rd for rd in range(4) for ti in range(3)]
    mu = f(inp["shift_mu"])[0]
    pv = np.zeros((128, NPV), np.float32)
    col = lambda v: np.ascontiguousarray(v.reshape(-1, 128).T)
    pv[:, PV_N1:PV_N1 + 8] = col(f(inp["norm1_g"])[0])
    pv[:, PV_N2:PV_N2 + 8] = col(f(inp["norm2_g"])[0])
    pv[:, PV_BADA:PV_BADA + 48] = col(f(inp["b_ada"])[0])
    pv[:, PV_MU0:PV_MU0 + 15] = col(mu[0])[:, otile]
    pv[:, PV_MU1:PV_MU1 + 15] = col(mu[1])[:, otile]
    pv[:, PV_KK:PV_KK + 4] = col(f(inp["rw_k_k"])[0])
    pv[:, PV_KA:PV_KA + 4] = col(f(inp["rw_k_a"])[0])
    pv[:, PV_RK:PV_RK + 4] = col(f(inp["rw_r_k"])[0].reshape(-1))
    pv[:, PV_W0:PV_W0 + 4] = col(f(inp["rw_w0"])[0, 0])
    pv[:, PV_W0 + 4:PV_W0 + 8] = col(f(inp["rw_w0"])[0, 1])
    pv[:, PV_A0:PV_A0 + 4] = col(f(inp["rw_a0"])[0, 0])
    pv[:, PV_A0 + 4:PV_A0 + 8] = col(f(inp["rw_a0"])[0, 1])
    pv[:, PV_QG] = np.tile(f(inp["na_q_g"])[0], 2)
    pv[:, PV_KG] = np.tile(f(inp["na_k_g"])[0], 2)
    lnp = np.stack([f(inp["rw_ln_g"])[0].reshape(4, 2, 64), f(inp["rw_ln_b"])[0].reshape(4, 2, 64)])
    lnp = np.ascontiguousarray(np.repeat(lnp.transpose(2, 0, 1, 3), 64, axis=0))
    cst, sm = _consts()
    tp, ts_ = _bias_tables(f(inp["na_rpb"])[0])
    common = {
        "cstb": cst, "smask": sm, "pv": pv, "lnp": lnp,
        "wada": _fm(f(inp["w_ada"])[0]), "win": _fm(w_in[:, perm]),
        "wup": f(inp["rw_w_up"])[0].reshape(128, 512), "aup": f(inp["rw_a_up"])[0].reshape(128, 512),
        "gup": f(inp["rw_g_up"])[0], "wor": _fm(f(inp["w_o_rwkv"])[0]), "won": _fm(f(inp["w_o_na"])[0]),
        "wout": _fm(f(inp["w_out"])[0]), "w1": _fm(f(inp["ffn_w1"])[0]), "w3": _fm(f(inp["ffn_w3"])[0]),
        "w2": _fm(f(inp["ffn_w2"])[0]),
    }
    xp, xs = f(inp["x_prompt"]), f(inp["x_sample"])
    st, ck, cv_ = f(inp["state_rwkv"]), f(inp["cache_na_k"]), f(inp["cache_na_v"])
    c, cctx = f(inp["c"]), f(inp["c_ctx"])
    maps = []
    for u in range(8):
        m = dict(common)
        if u < 4:
            X = xp[4 * u:4 * u + 4].reshape(T, D)
            cvec = cctx
            m["fl"] = np.tile(np.array([1.0, NEGM, 0.0, 0.0], np.float32), (128, 1))
            m["s0T"] = np.zeros((128, 4, 2, 64), np.float32)
            m["ckT"] = np.zeros((128, 4, 256), np.float32)
            m["cvv"] = np.zeros((128, 2, 512), np.float32)
            m["tbl"] = tp
        else:
            b = (u - 4) % 2
            X = xs[b]
            cvec = c[b]
            m["fl"] = np.tile(np.array([0.0, 0.0, 1.0, 0.0], np.float32), (128, 1))
            s = st[b, 0].reshape(2, 4, 2, 64, 64)
            m["s0T"] = np.ascontiguousarray(s.transpose(2, 4, 1, 0, 3).reshape(128, 4, 2, 64))
            kk_ = ck[b, 0].reshape(256, 4, 2, 64)
            m["ckT"] = np.ascontiguousarray(kk_.transpose(2, 3, 1, 0).reshape(128, 4, 256))
            m["cvv"] = np.ascontiguousarray(cv_[b, 0].reshape(2, 128, 512).transpose(1, 0, 2))
            m["tbl"] = ts_
        m["xT"] = _fm(np.ascontiguousarray(X.T))
        m["cv"] = np.ascontiguousarray(cvec.reshape(8, 128).T)
        maps.append(m)
    return maps


def _unfm(a):
    return a.transpose(1, 0, 2).reshape(-1, a.shape[2])


def kernel(**inp):
    maps = _prep(inp)
    nc = _get_nc()
    res = run_bass_kernel_spmd(nc, maps, core_ids=list(range(8)))
    R = res.results
    y_p = np.empty((16, 256, D), np.float32)
    y_s = np.empty((2, 1024, D), np.float32)
    s_new = np.empty((16, 1, 2, 8, 64, 64), np.float32)
    k_new = np.empty((16, 1, 256, 8, 64), np.float32)
    v_new = np.empty((16, 1, 256, 8, 64), np.float32)
    for u in range(6):
        Y = _unfm(np.asarray(R[u]["yT"])).T
        if u < 4:
            y_p[4 * u:4 * u + 4] = Y.reshape(4, 256, D)
            oS = np.asarray(R[u]["oS"]).reshape(2, 64, 4, 2, 4, 64)
            s_new[4 * u:4 * u + 4, 0] = oS.transpose(4, 3, 2, 0, 5, 1).reshape(4, 2, 8, 64, 64)
            okT = np.asarray(R[u]["okT"]).reshape(2, 64, 4, T)
            k_new[4 * u:4 * u + 4, 0] = okT.transpose(3, 2, 0, 1).reshape(4, 256, 8, 64)
            ov = np.asarray(R[u]["ov"]).transpose(1, 0, 2).reshape(T, 512)
            v_new[4 * u:4 * u + 4, 0] = ov.reshape(4, 256, 8, 64)
        else:
            y_s[u - 4] = Y
    return y_p, y_s, s_new, k_new, v_new
```

```python
import numpy as np
from contextlib import ExitStack
import concourse.bass as bass
import concourse.mybir as mybir
from concourse.bass_utils import run_bass_kernel_spmd

F32 = mybir.dt.float32
BF16 = mybir.dt.bfloat16
ALU = mybir.AluOpType
AF = mybir.ActivationFunctionType
AX = mybir.AxisListType

D = 1024
T = 1024
NCH = 16
FF = 2816
NFT = 22
RMS_EPS = 1e-6
GN_EPS = 64e-5
L2_EPS = 1e-12
NEGM = -30000.0
LWC = -0.30326533
NDMA = 24
NSLOT = 4

PV_N1, PV_N2, PV_BADA, PV_MU0, PV_MU1 = 0, 8, 16, 64, 79
PV_KK, PV_KA, PV_RK, PV_W0, PV_A0, PV_QG, PV_KG, NPV = 94, 98, 102, 106, 114, 122, 123, 124
CB_ID, CB_ONE, CB_BO, CB_LT, CB_LE, CB_GT, CB_GE, CB_I64, NCB = 0, 128, 256, 384, 448, 512, 576, 640, 704

def _qwin(R):
    return min(max(R - 2, 0), 3)
QL = [[R for R in range(8) if _qwin(R) <= j < _qwin(R) + 5] for j in range(8)]
BLK = {}
_b = 0
for _j in range(8):
    for _R in QL[_j]:
        BLK[(_j, _R)] = _b
        _b += 1
NBLK = _b


class Sched:
    def __init__(self, nc, stack):
        self.nc = nc
        self.ce = ["pe", "act", "dve", "pool"]
        self.qe = ["sp"]
        self.sem = {e: stack.enter_context(nc.semaphore("s_" + e)) for e in self.ce + self.qe}
        self.cnt = {e: 0 for e in self.ce + self.qe}
        self.dsem = [stack.enter_context(nc.semaphore("d%d" % i)) for i in range(NDMA)]
        self.dcnt = [0] * NDMA
        self.drr = {"sp": 0, "pool": 0}
        self.ops = {e: [] for e in self.ce + self.qe}
        self.res = {}
        self.waited = {e: {} for e in self.ce + self.qe}
        self.nops = 0
        self.pending_dma = []
        self.same = {e: {} for e in self.ce}

    def _semobj(self, sk):
        return self.sem[sk[1]] if sk[0] == "eng" else self.dsem[sk[1]]

    def _key(self, x):
        if isinstance(x, (str, tuple)):
            return x
        return x.tensor.name if hasattr(x, "tensor") else x.name

    def _deps(self, reads, writes):
        raw, other = [], []
        for r in reads:
            st = self.res.get(self._key(r))
            if st and st["w"]:
                raw.append(st["w"])
        for w in writes:
            st = self.res.get(self._key(w))
            if st:
                if st["w"]:
                    other.append(st["w"])
                other.extend(st["r"].values())
        return raw, other

    def _mark(self, reads, writes, ev, rk):
        for r in reads:
            st = self.res.setdefault(self._key(r), {"w": None, "r": {}})
            st["r"][rk] = ev
        for w in writes:
            self.res[self._key(w)] = {"w": ev, "r": {}}

    def _waits(self, eng, deps):
        if isinstance(deps, tuple):
            raw, other = deps
            deps = list(raw) + [d_ for d_ in other if d_[0] != ("eng", eng)]
        best = {}
        for sk, v in deps:
            if sk == ("eng", eng) and eng == "pe":
                continue
            if self.waited[eng].get(sk, 0) >= v:
                continue
            if best.get(sk, 0) < v:
                best[sk] = v
        for sk, v in best.items():
            self.waited[eng][sk] = v
        return list(best.items())

    @staticmethod
    def _box(x):
        if isinstance(x, (str, tuple)) or not hasattr(x, "ap"):
            return None
        try:
            ap = x.ap
            p0 = x.base_partition()
            off = int(x.offset) - p0 * ap[0][0]
            lo = hi = off
            for st_, n_ in ap[1:]:
                ext = st_ * (n_ - 1)
                if ext < 0:
                    lo += ext
                else:
                    hi += ext
            return (p0, p0 + ap[0][1], lo, hi + 1)
        except Exception:
            return None

    @staticmethod
    def _ovl(a, b):
        if a is None or b is None:
            return True
        return a[0] < b[1] and b[0] < a[1] and a[2] < b[3] and b[2] < a[3]

    def _same_deps(self, eng, reads, writes):
        deps = []
        hist = self.same[eng]
        for lst, isw in ((reads, False), (writes, True)):
            for x in lst:
                h = hist.get(self._key(x))
                if not h:
                    continue
                b = self._box(x)
                for b2, cnt2, w2 in h:
                    if (isw or w2) and self._ovl(b, b2):
                        deps.append((("eng", eng), cnt2))
        return deps

    def _same_record(self, eng, reads, writes, cnt):
        hist = self.same[eng]
        for lst, isw in ((reads, False), (writes, True)):
            for x in lst:
                h = hist.setdefault(self._key(x), [])
                h.append((self._box(x), cnt, isw))
                if len(h) > 64:
                    del h[0]

    def _psum_excl(self, r, w):
        r2, w2 = [], list(w)
        for x in r:
            k = self._key(x)
            if isinstance(k, str) and (k.startswith("pf") or k.startswith("pb")):
                w2.append(x)
            else:
                r2.append(x)
        return r2, w2

    def op(self, eng, fn, r=(), w=()):
        r0, w0 = list(r), list(w)
        r, w = self._psum_excl(r, w)
        raw_, oth_ = self._deps(r, w)
        own = ("eng", eng)
        deps = [d_ for d_ in list(raw_) + list(oth_) if d_[0] != own]
        if eng != "pe":
            deps += self._same_deps(eng, r0, w0)
        waits = self._waits(eng, deps)
        self.cnt[eng] += 1
        if eng != "pe":
            self._same_record(eng, r0, w0, self.cnt[eng])
        ev = (("eng", eng), self.cnt[eng])
        self.ops[eng].append((fn, waits, (ev[0], 1)))
        self._mark(r, w, ev, eng)
        self.nops += 1

    def dma(self, q, out, in_, r=None, w=None, track=True):
        r = [in_] if r is None else r
        w = [out] if w is None else w
        half_ = NDMA // 2
        base = 0 if q == "sp" else half_
        i = base + self.drr[q] % half_
        self.drr[q] += 1
        raw_, oth_ = self._deps(r, w)
        deps = list(raw_) + list(oth_)
        if self.dcnt[i] > 0:
            deps.append((("dma", i), self.dcnt[i]))
        waits = self._waits(q, deps)
        self.dcnt[i] += 16
        ev = (("dma", i), self.dcnt[i])

        def fn(e, out=out, in_=in_):
            return e.dma_start(out=out, in_=in_)

        self.ops[q].append((fn, waits, (ev[0], 16)))
        self._mark(r, w, ev, ("dma", i))
        if track:
            self.pending_dma.append(ev)
        self.nops += 1
        return ev

    def barrier(self):
        waits = self._waits("sp", list(self.pending_dma))
        self.pending_dma = []
        self.cnt["sp"] += 1
        spv = self.cnt["sp"]
        self.ops["sp"].append(("seminc", waits, None))
        tgt = [(("eng", e), self.cnt[e]) for e in self.ce if self.cnt[e] > 0] + [(("eng", "sp"), spv)]
        for e in self.ce + self.qe:
            ws = self._waits(e, [t for t in tgt if t[0] != ("eng", e)])
            self.ops[e].append((None, ws, None))

    def flush(self):
        nc = self.nc
        ops = self.ops
        self.ops = {e: [] for e in self.ce + self.qe}
        with nc.Block() as block:
            def emit(engname):
                def body(e):
                    for fn, waits, inc in ops[engname]:
                        for sk, v in waits:
                            e.wait_ge(self._semobj(sk), v)
                        if fn is None:
                            continue
                        if fn == "seminc":
                            e.sem_inc(self.sem["sp"], 1)
                            continue
                        ins = fn(e)
                        if inc is not None:
                            ins.then_inc(self._semobj(inc[0]), inc[1])
                return body
            block.tensor(emit("pe"))
            block.scalar(emit("act"))
            block.vector(emit("dve"))
            block.gpsimd(emit("pool"))
            block.sync(emit("sp"))

    def mm(self, out, lhsT, rhs, start=True, stop=True):
        self.op("pe", lambda e: e.matmul(out, lhsT, rhs, start=start, stop=stop), r=[lhsT, rhs], w=[out])

    def tr(self, out, in_, ident):
        self.op("pe", lambda e: e.transpose(out, in_, ident), r=[in_, ident], w=[out])

    def act(self, out, in_, func, bias=None, scale=None):
        kw = {}
        rr = [in_]
        if bias is not None:
            kw["bias"] = bias
            if not isinstance(bias, (int, float)):
                rr.append(bias)
        if scale is not None:
            kw["scale"] = scale
            if not isinstance(scale, (int, float)):
                rr.append(scale)
        self.op("act", lambda e: e.activation(out, in_, func, **kw), r=rr, w=[out])

    def tt(self, eng, out, in0, in1, op):
        self.op(eng, lambda e: e.tensor_tensor(out, in0, in1, op), r=[in0, in1], w=[out])

    def ts(self, eng, out, in0, s1, op0, s2=None, op1=None):
        rr = [in0] + [s for s in (s1, s2) if s is not None and not isinstance(s, (int, float))]
        kw = {}
        if op1 is not None:
            kw["op1"] = op1
        self.op(eng, lambda e: e.tensor_scalar(out, in0, s1, s2, op0, **kw), r=rr, w=[out])

    def stt(self, out, in0, scalar, in1, op0, op1):
        rr = [in0, in1] + ([scalar] if not isinstance(scalar, (int, float)) else [])
        self.op("dve", lambda e: e.scalar_tensor_tensor(out, in0, scalar, in1, op0, op1), r=rr, w=[out])

    def copy(self, eng, out, in_):
        if eng == "act":
            self.op("act", lambda e: e.activation(out, in_, AF.Identity), r=[in_], w=[out])
        else:
            self.op(eng, lambda e: e.tensor_copy(out, in_), r=[in_], w=[out])

    def memset(self, eng, out, val):
        self.op(eng, lambda e: e.memset(out, val), r=[], w=[out])

    def reduce(self, out, in_, op, axis=AX.X):
        self.op("dve", lambda e: e.tensor_reduce(out, in_, axis, op), r=[in_], w=[out])

    def recip(self, out, in_):
        self.op("dve", lambda e: e.reciprocal(out, in_), r=[in_], w=[out])

    def scan(self, out, d0, d1, init, op0, op1):
        self.op("dve", lambda e: e.tensor_tensor_scan(out, d0, d1, init, op0, op1), r=[d0, d1], w=[out])


def _prod(xs):
    p = 1
    for x in xs:
        p *= x
    return p


def build_nc(dbg=(), stop=None):
    nc = bass.Bass("TRN2", target_bir_lowering=False)

    def din(name, shape):
        return nc.dram_tensor(name, list(shape), F32, kind="ExternalInput").ap()

    def dout(name, shape):
        return nc.dram_tensor(name, list(shape), F32, kind="ExternalOutput").ap()

    I = {}
    for name, shape in [("xT", (128, 8, T)), ("cv", (128, 8)), ("fl", (128, 4)), ("s0T", (128, 4, 2, 64)),
                        ("ckT", (128, 4, 256)), ("cvv", (128, 2, 512)), ("tbl", (8, 128, NBLK, 128)),
                        ("cstb", (128, NCB)), ("smask", (128, T)), ("pv", (128, NPV)), ("lnp", (128, 2, 4, 64)),
                        ("wada", (128, 8, 6 * D)), ("win", (128, 8, 5504)), ("wup", (128, 512)),
                        ("aup", (128, 512)), ("gup", (128, 512)), ("wor", (128, 4, D)), ("won", (128, 4, D)),
                        ("wout", (128, 8, D)), ("w1", (128, 8, FF)), ("w3", (128, 8, FF)), ("w2", (128, NFT, D))]:
        I[name] = din(name, shape)
    O = {}
    for name, shape in [("yT", (128, 8, T)), ("oS", (128, 4, 2, 4, 64)), ("okT", (128, 4, T)), ("ov", (128, 8, 512))]:
        O[name] = dout(name, shape)
    dbg_out = {}

    with ExitStack() as top:
        S = Sched(nc, top)

        def sbt(stack, name, shape, dt):
            return stack.enter_context(nc.sbuf_tensor("sb_" + name, list(shape), dt))

        slots = [sbt(top, "ws%d" % i, (128, 4096), BF16) for i in range(NSLOT)]
        cb = sbt(top, "cb", (128, NCB), BF16)
        pv = sbt(top, "pvt", (128, NPV), F32)
        fl = sbt(top, "flt", (128, 4), F32)
        dv = sbt(top, "dv", (128, 104), F32)
        mod = sbt(top, "mod", (128, 48), F32)
        hT = sbt(top, "hT", (128, 8, T), BF16)
        oT_na = sbt(top, "oTna", (128, 4, T), BF16)
        oT_rw = sbt(top, "oTrw", (128, 4, T), BF16)
        cvb = sbt(top, "cvb", (128, 8), BF16)
        NPF = 6
        PF = [top.enter_context(nc.psum_tensor("pf%d" % i, [128, 512], F32)) for i in range(NPF)]
        PBs = [top.enter_context(nc.psum_tensor("pb%d" % i, [128, 1024], BF16)) for i in range(2)]
        PB = PBs[0]
        pfi = [0]

        live = set()

        def pf():
            for _ in range(8):
                t = PF[pfi[0] % NPF]
                pfi[0] += 1
                if t.name not in live:
                    return t
            raise RuntimeError("no free psum bank")

        wsi = [0]

        wq = []

        def _wload(src, shape, track):
            slot = slots[wsi[0] % NSLOT]
            wsi[0] += 1
            n = _prod(shape[1:])
            v = slot[:, 0:n]
            if len(shape) == 3:
                v = v.rearrange("p (a b) -> p a b", a=shape[1], b=shape[2])
            S.dma("pool", v, src, track=track)
            return v

        def prefetch(src, shape):
            wq.append((repr(src), tuple(shape), _wload(src, shape, False)))

        def wload(src, shape):
            if wq:
                k, sh, v = wq.pop(0)
                assert k == repr(src) and sh == tuple(shape), (k, repr(src))
                return v
            return _wload(src, shape, True)

        def dbgdump(name, ap, shape, dt=F32):
            if name not in dbg:
                return
            d = nc.dram_tensor("dbg_" + name, list(shape), dt, kind="ExternalOutput").ap()
            dbg_out[name] = d
            S.dma("sp", d, ap)

        def chk(tag):
            if stop == tag:
                S.barrier()
                S.flush()
                return True
            return False

        ident = cb[:, CB_ID:CB_ID + 128]
        ones_b = cb[:, CB_ONE:CB_ONE + 128]
        bones = cb[:, CB_BO:CB_BO + 128]
        mLT = cb[:, CB_LT:CB_LT + 64]
        mLE = cb[:, CB_LE:CB_LE + 64]
        mGT = cb[:, CB_GT:CB_GT + 64]
        mGE = cb[:, CB_GE:CB_GE + 64]
        eye64 = cb[:, CB_I64:CB_I64 + 64]

        def bc8(m):
            return m.unsqueeze(1).to_broadcast([128, 8, 64])

        def pcol(i):
            return pv[:, i:i + 1]

        DV_C0, DV_NF0, DV_NF1, DV_HW0, DV_HA0, DV_OMKA, DV_RK05, DV_QG8 = 0, 15, 30, 45, 53, 61, 65, 69
        DV_GS1, DV_GS2, DV_G1H = 70, 78, 86
        DV_HKA, DV_OMHKA = 94, 98

        def dcol(i):
            return dv[:, i:i + 1]

        S.dma("pool", cb[:], I["cstb"][:])
        S.dma("sp", pv[:], I["pv"][:])
        S.dma("sp", fl[:], I["fl"][:])
        S.tt("dve", dv[:, DV_C0:DV_C0 + 15], pv[:, PV_MU0:PV_MU0 + 15], pv[:, PV_MU1:PV_MU1 + 15], ALU.add)
        S.ts("dve", dv[:, DV_C0:DV_C0 + 15], dv[:, DV_C0:DV_C0 + 15], -1.0, ALU.mult, 1.0, ALU.add)
        S.ts("dve", dv[:, DV_NF0:DV_NF0 + 15], pv[:, PV_MU0:PV_MU0 + 15], fl[:, 0:1], ALU.mult, -1.0, ALU.mult)
        S.ts("dve", dv[:, DV_NF1:DV_NF1 + 15], pv[:, PV_MU1:PV_MU1 + 15], fl[:, 0:1], ALU.mult, -1.0, ALU.mult)
        S.ts("dve", dv[:, DV_HW0:DV_HW0 + 8], pv[:, PV_W0:PV_W0 + 8], 0.5, ALU.mult)
        S.ts("dve", dv[:, DV_HA0:DV_HA0 + 8], pv[:, PV_A0:PV_A0 + 8], 0.5, ALU.mult)
        S.ts("dve", dv[:, DV_OMKA:DV_OMKA + 4], pv[:, PV_KA:PV_KA + 4], -1.0, ALU.mult, 1.0, ALU.add)
        S.ts("dve", dv[:, DV_HKA:DV_HKA + 4], pv[:, PV_KA:PV_KA + 4], 0.5, ALU.mult)
        S.ts("dve", dv[:, DV_OMHKA:DV_OMHKA + 4], pv[:, PV_KA:PV_KA + 4], -0.5, ALU.mult, 1.0, ALU.add)
        S.ts("dve", dv[:, DV_RK05:DV_RK05 + 4], pv[:, PV_RK:PV_RK + 4], 0.5, ALU.mult)
        S.ts("dve", dv[:, DV_QG8:DV_QG8 + 1], pv[:, PV_QG:PV_QG + 1], 0.125, ALU.mult)

        def mod_group(g):
            pm = pf()
            w = wload(I["wada"][:, :, g * 512:(g + 1) * 512], (128, 8, 512))
            for j4 in range(4):
                for kc in range(8):
                    S.mm(pm[:, j4:j4 + 1], w[:, kc, j4 * 128:(j4 + 1) * 128], cvb[:, kc:kc + 1],
                         start=(kc == 0), stop=(kc == 7))
            S.tt("dve", mod[:, g * 4:g * 4 + 4], pm[:, 0:4], pv[:, PV_BADA + g * 4:PV_BADA + g * 4 + 4], ALU.add)

        def modulation(j0, j1):
            for g in range(j0 // 4, j1 // 4):
                mod_group(g)

        def rmsnorm_mod(xT, gs0, sh0, tmpA, sqb, rstd):
            pa, pb_ = pf(), pf()
            for kc in range(8):
                S.act(sqb[:, kc % 2, :], xT[:, kc, :], AF.Square)
                for th, pp in enumerate((pa, pb_)):
                    S.mm(pp[:], ones_b, sqb[:, kc % 2, th * 512:(th + 1) * 512], start=(kc == 0), stop=(kc == 7))
            for th, pp in enumerate((pa, pb_)):
                S.act(rstd[:, th * 512:(th + 1) * 512], pp[:], AF.Ln, scale=1.0 / D, bias=RMS_EPS)
            S.act(rstd[:], rstd[:], AF.Exp, scale=-0.5)
            for kc in range(8):
                S.tt("dve", tmpA[:, kc % 2, :], xT[:, kc, :], rstd[:], ALU.mult)
                S.act(hT[:, kc, :], tmpA[:, kc % 2, :], AF.Identity, scale=dv[:, gs0 + kc:gs0 + kc + 1],
                      bias=mod[:, sh0 + kc:sh0 + kc + 1])

        with ExitStack() as ph:
            xT = sbt(ph, "xTa", (128, 8, T), F32)
            cvf = sbt(ph, "cvf", (128, 8), F32)
            tmpA = sbt(ph, "tmpA", (128, 2, T), F32)
            sqb = sbt(ph, "sqb", (128, 2, T), BF16)
            rstd = sbt(ph, "rstd", (128, T), F32)
            S.dma("sp", cvf[:], I["cv"][:])
            S.dma("sp", xT[:], I["xT"][:])
            S.act(cvb[:], cvf[:], AF.Silu)
            modulation(0, 16)
            S.stt(dv[:, DV_GS1:DV_GS1 + 8], mod[:, 8:16], 1.0, pv[:, PV_N1:PV_N1 + 8], ALU.add, ALU.mult)
            rmsnorm_mod(xT, DV_GS1, 0, tmpA, sqb, rstd)
            dbgdump("hT", hT[:], (128, 8, T), BF16)
            prefetch(I["win"][:, :, 1920:1920 + 512], (128, 8, 512))
            prefetch(I["win"][:, :, 2432:2432 + 512], (128, 8, 512))
            S.barrier()
            S.flush()
            if stop == "A":
                return nc, dbg_out

        WC_LORA, WC_RD, WC_NQ, WC_NK, WC_NV, WC_GRW, WC_GNA = 0, 384, 1920, 2432, 2944, 3456, 4480

        with ExitStack() as ph:
            qT = sbt(ph, "qT", (128, 4, T), BF16)
            kT = sbt(ph, "kT", (128, 4, T + 256), BF16)
            vaug = sbt(ph, "vaug", (128, 10, 8, 66), BF16)
            ona = sbt(ph, "ona", (128, 8, 512), BF16)
            U = [sbt(ph, "naU%d" % i, (128, T), F32) for i in range(2)]
            sq = [sbt(ph, "nasq%d" % i, (128, T), BF16) for i in range(2)]
            rin = [sbt(ph, "narin%d" % i, (128, T), F32) for i in range(2)]
            kn = [sbt(ph, "nakn%d" % i, (128, T), F32) for i in range(2)]
            vf = [sbt(ph, "navf%d" % i, (128, 512), F32) for i in range(2)]
            tb = [sbt(ph, "natb%d" % i, (128, NBLK, 128), BF16) for i in range(2)]
            PT = [sbt(ph, "naPT%d" % i, (128, NBLK + 16, 128), BF16) for i in range(2)]
            lg = [sbt(ph, "nalg%d" % i, (128, 512), F32) for i in range(3)]
            rden = sbt(ph, "rden", (128, 8), F32)
            if chk("NA00"):
                return nc, dbg_out
            S.dma("pool", kT[:, :, T:T + 256], I["ckT"][:])
            if chk("NA01"):
                return nc, dbg_out
            cvst = sbt(ph, "cvst", (128, 2, 512), BF16)
            S.dma("pool", cvst[:], I["cvv"][:])
            for a_ in range(2):
                S.copy("dve", vaug[:, 8 + a_, :, 0:64], cvst[:, a_, :].rearrange("p (h d) -> p h d", d=64))
            S.memset("dve", vaug[:, :, :, 64:65], 1.0)
            if chk("NA0"):
                return nc, dbg_out
            qkw = {}

            def qk_stage1(i):
                which, rd = i // 4, i % 4
                if rd == 0:
                    wc = WC_NQ if which == 0 else WC_NK
                    qkw[which] = wload(I["win"][:, :, wc:wc + 512], (128, 8, 512))
                w = qkw[which]
                b = i % 2
                pa, pb_ = pf(), pf()
                for th, pp in enumerate((pa, pb_)):
                    for kc in range(8):
                        S.mm(pp[:], w[:, kc, rd * 128:(rd + 1) * 128], hT[:, kc, th * 512:(th + 1) * 512],
                             start=(kc == 0), stop=(kc == 7))
                for th, pp in enumerate((pa, pb_)):
                    S.act(sq[b][:, th * 512:(th + 1) * 512], pp[:], AF.Square)
                    S.copy("dve", U[b][:, th * 512:(th + 1) * 512], pp[:])

            def qk_stage2(i):
                which, rd = i // 4, i % 4
                b = i % 2
                pc, pd = pf(), pf()
                for th, pp in enumerate((pc, pd)):
                    S.mm(pp[:], bones, sq[b][:, th * 512:(th + 1) * 512])
                for th, pp in enumerate((pc, pd)):
                    S.act(rin[b][:, th * 512:(th + 1) * 512], pp[:], AF.Ln, scale=1.0 / 64, bias=RMS_EPS)
                S.act(rin[b][:], rin[b][:], AF.Exp, scale=-0.5)
                if which == 0:
                    S.stt(qT[:, rd, :], U[b][:], dcol(DV_QG8), rin[b][:], ALU.mult, ALU.mult)
                else:
                    S.stt(kn[b][:], U[b][:], pcol(PV_KG), rin[b][:], ALU.mult, ALU.mult)
                    S.copy("act", kT[:, rd, 0:T], kn[b][:])
                    S.dma("sp", O["okT"][:, rd, :], kn[b][:])

            qk_stage1(0)
            for i in range(8):
                if i + 1 < 8:
                    qk_stage1(i + 1)
                qk_stage2(i)
            if chk("NA1"):
                return nc, dbg_out
            w = wload(I["win"][:, :, WC_NV:WC_NV + 512], (128, 8, 512))
            for tt_ in range(8):
                pp = pf()
                for kc in range(8):
                    S.mm(pp[:], hT[:, kc, tt_ * 128:(tt_ + 1) * 128], w[:, kc, :], start=(kc == 0), stop=(kc == 7))
                S.copy("act", vf[tt_ % 2][:], pp[:])
                S.copy("dve", vaug[:, tt_, :, 0:64], pp[:].rearrange("p (h d) -> p h d", d=64))
                S.dma("sp", O["ov"][:, tt_, :], vf[tt_ % 2][:])
            dbgdump("qT", qT[:], (128, 4, T), BF16)
            dbgdump("kT", kT[:], (128, 4, T + 256), BF16)
            if chk("NA2"):
                return nc, dbg_out
            li = [0]

            def att_qk(h):
                rd, h2 = h // 2, h % 2
                pr = slice(64 * h2, 64 * h2 + 64)
                tbh = tb[h % 2]
                PTh = PT[h % 2]
                S.dma("pool", tbh[:], I["tbl"][h])
                for j in range(8):
                    Rs = QL[j]
                    q0, nq = Rs[0] * 128, len(Rs) * 128
                    b0 = BLK[(j, Rs[0])]
                    for off in range(0, nq, 512):
                        n = min(512, nq - off)
                        pp = pf()
                        S.mm(pp[:, 0:n], kT[pr, rd, j * 128:(j + 1) * 128], qT[pr, rd, q0 + off:q0 + off + n])
                        bb = b0 + off // 128
                        l = lg[li[0] % 3]
                        li[0] += 1
                        S.tt("dve", l[:, 0:n], pp[:, 0:n],
                             tbh[:, bb:bb + n // 128, :].rearrange("p a b -> p (a b)"), ALU.add)
                        S.act(PTh[:, bb:bb + n // 128, :].rearrange("p a b -> p (a b)"), l[:, 0:n], AF.Exp)
                for jc in range(2):
                    for off in range(0, T, 512):
                        pp = pf()
                        S.mm(pp[:], kT[pr, rd, T + jc * 128:T + (jc + 1) * 128], qT[pr, rd, off:off + 512])
                        bb = NBLK + jc * 8 + off // 128
                        S.act(PTh[:, bb:bb + 4, :].rearrange("p a b -> p (a b)"), pp[:], AF.Exp, bias=fl[:, 1:2])

            def att_pv(h):
                PTh = PT[h % 2]
                for half in range(2):
                    po = pf()
                    for r4 in range(4):
                        R = half * 4 + r4
                        w0 = _qwin(R)
                        keys = [(jj, BLK[(jj, R)]) for jj in range(w0, w0 + 5)] + \
                               [(8 + jc, NBLK + jc * 8 + R) for jc in range(2)]
                        for n_, (vt, bb) in enumerate(keys):
                            S.mm(po[:, r4 * 65:(r4 + 1) * 65], PTh[:, bb, :], vaug[:, vt, h, 0:65],
                                 start=(n_ == 0), stop=(n_ == len(keys) - 1))
                    pov = po[:, 0:260].rearrange("p (r c) -> p r c", c=65)
                    S.recip(rden[:, half * 4:(half + 1) * 4], pov[:, :, 64])
                    S.tt("dve", ona[:, half * 4:(half + 1) * 4, h * 64:(h + 1) * 64], pov[:, :, 0:64],
                         rden[:, half * 4:(half + 1) * 4].unsqueeze(2).to_broadcast([128, 4, 64]), ALU.mult)

            att_qk(0)
            for h in range(8):
                if h + 1 < 8:
                    att_qk(h + 1)
                att_pv(h)
            if chk("NA3"):
                return nc, dbg_out
            for ct in range(4):
                for R in range(8):
                    S.tr(PB[:, R * 128:(R + 1) * 128], ona[:, R, ct * 128:(ct + 1) * 128], ident)
                S.copy("act", oT_na[:, ct, :], PB[:])
            dbgdump("oTna", oT_na[:], (128, 4, T), BF16)
            prefetch(I["win"][:, :, WC_LORA:WC_LORA + 384], (128, 8, 384))
            prefetch(I["win"][:, :, WC_RD:WC_RD + 384], (128, 8, 384))
            S.barrier()
            S.flush()
            if stop == "NA":
                return nc, dbg_out

        with ExitStack() as ph:
            twd = sbt(ph, "twd", (128, T), BF16)
            adb = sbt(ph, "adb", (128, T), BF16)
            sgd = sbt(ph, "sgd", (128, T), BF16)
            wupb = sbt(ph, "wupb", (128, 512), BF16)
            aupb = sbt(ph, "aupb", (128, 512), BF16)
            gupb = sbt(ph, "gupb", (128, 512), BF16)
            lnp = [sbt(ph, "lnpt%d" % i, (128, 2, 64), F32) for i in range(2)]
            smask = sbt(ph, "smaskt", (128, 512), F32)
            s0r = [sbt(ph, "s0r%d" % i, (128, 2, 64), F32) for i in range(2)]
            bon = [sbt(ph, "bon%d" % i, (128, 16), F32) for i in range(2)]
            Uraw = sbt(ph, "Uraw", (128, T), F32)
            Rp = sbt(ph, "Rp", (128, T), F32)
            Kp = sbt(ph, "Kp", (128, T), F32)
            VKD = sbt(ph, "VKD", (128, T), F32)
            Vp = VKD
            KD = VKD[:, 0:512]
            Dx = VKD[:, 512:1024]
            Vb = [sbt(ph, "Vb0", (128, T), BF16)] * 2
            Vtm = [sbt(ph, "Vtm%d" % i, (128, 16, 64), BF16) for i in range(2)]
            gfm = [sbt(ph, "gfm%d" % i, (128, T), BF16) for i in range(2)]
            kkn = sbt(ph, "kkn", (128, 512), F32)
            sqk = sbt(ph, "sqk", (128, 512), BF16)
            LW = sbt(ph, "LW", (128, 512), F32)
            C = sbt(ph, "Ccs", (128, 512), F32)
            IC = sbt(ph, "IC", (128, 512), F32)
            E0 = sbt(ph, "E0", (128, 512), F32)
            PBh = [sbt(ph, "PBh%d" % d, (128, 512), BF16) for d in range(2)]
            AR = [[sbt(ph, "AR%d%d" % (h, d), (128, 8, 2, 64), BF16) for d in range(2)] for h in range(2)]
            BK = [[sbt(ph, "BK%d%d" % (h, d), (128, 8, 2, 64), BF16) for d in range(2)] for h in range(2)]
            AA = [[sbt(ph, "AA%d%d" % (h, d), (128, 8, 2, 64), BF16) for d in range(2)] for h in range(2)]
            AB = [[sbt(ph, "AB%d%d" % (h, d), (128, 8, 2, 64), BF16) for d in range(2)] for h in range(2)]
            KTM = [[sbt(ph, "KTM%d%d" % (h, d), (128, 8, 64), BF16) for d in range(2)] for h in range(2)]
            PN = [[sbt(ph, "PN%d%d" % (h, d), (128, 8, 2, 64), BF16) for d in range(2)] for h in range(2)]
            NT = [[sbt(ph, "NT%d%d" % (h, d), (128, 8, 64), BF16) for d in range(2)] for h in range(2)]
            AV = [[sbt(ph, "AV%d%d" % (h, d), (128, 8, 2, 64), BF16) for d in range(2)] for h in range(2)]
            Rh = sbt(ph, "Rh", (128, 2, 16, 64), BF16)
            G0 = sbt(ph, "G0", (128, 2, 16, 64), BF16)
            Avv = sbt(ph, "Avv", (128, 2, 16, 64), BF16)
            H0 = sbt(ph, "H0", (128, 2, 16, 64), BF16)
            Sall = [[sbt(ph, "S%d_%d" % (d, i), (128, 64), BF16) for i in range(17)] for d in range(2)]
            SF = sbt(ph, "SF", (128, 2, 4, 64), F32)
            wend = sbt(ph, "wend", (128, 2, 16), F32)
            wendc = sbt(ph, "wendc", (128, 2, 16), F32)
            wk = sbt(ph, "wk", (128, 2, 16), F32)
            Yf = sbt(ph, "Yf", (128, 16, 64), F32)
            Yq = sbt(ph, "Yq", (128, 16, 64), BF16)
            otm = sbt(ph, "otm", (128, 16, 64), BF16)
            st4 = sbt(ph, "st4", (128, 6, 16), F32)

            S.dma("pool", wupb[:], I["wup"][:])
            S.dma("pool", aupb[:], I["aup"][:])
            S.dma("pool", gupb[:], I["gup"][:])
            S.dma("sp", smask[:], I["smask"][:, 0:512])

            def inproj_shift(w, ti, q, dest):
                pa, pb_ = pf(), pf()
                for th, pp in enumerate((pa, pb_)):
                    for kc in range(8):
                        S.mm(pp[:], w[:, kc, ti * 128:(ti + 1) * 128], hT[:, kc, th * 512:(th + 1) * 512],
                             start=(kc == 0), stop=(kc == 7))
                for th, pp in enumerate((pa, pb_)):
                    S.copy("act", Uraw[:, th * 512:(th + 1) * 512], pp[:])
                    S.ts("dve", dest[:, th * 512:(th + 1) * 512], pp[:], dcol(DV_C0 + q), ALU.mult)
                S.stt(dest[:, 1:T], Uraw[:, 0:T - 1], pcol(PV_MU0 + q), dest[:, 1:T], ALU.mult, ALU.add)
                S.stt(dest[:, 0:T - 1], Uraw[:, 1:T], pcol(PV_MU1 + q), dest[:, 0:T - 1], ALU.mult, ALU.add)
                S.stt(dest[:, 256:T:256], Uraw[:, 255:T - 1:256], dcol(DV_NF0 + q), dest[:, 256:T:256],
                      ALU.mult, ALU.add)
                S.stt(dest[:, 255:T - 1:256], Uraw[:, 256:T:256], dcol(DV_NF1 + q), dest[:, 255:T - 1:256],
                      ALU.mult, ALU.add)

            w = wload(I["win"][:, :, WC_LORA:WC_LORA + 384], (128, 8, 384))
            inproj_shift(w, 0, 0, Rp)
            S.act(twd[:], Rp[:], AF.Tanh)
            inproj_shift(w, 1, 1, Kp)
            S.copy("act", adb[:], Kp[:])
            inproj_shift(w, 2, 2, Vp)
            S.act(Yf[:].rearrange("p c v -> p (c v)"), Vp[:], AF.Tanh, scale=0.5)
            S.ts("dve", sgd[:], Yf[:].rearrange("p c v -> p (c v)"), 0.5, ALU.mult, 0.5, ALU.add)

            def v3(ap):
                return ap.rearrange("p (c t) -> p c t", t=64)

            mstrict = (mLT, mGT)
            mincl = (mLE, mGE)
            mstrictT = (mGT, mLT)

            def bc4(m):
                return m.unsqueeze(1).to_broadcast([128, 4, 64])

            def pv4(p):
                return p[:].rearrange("p (c x) -> p c x", x=128)

            def f128(ap):
                return ap.rearrange("p a b -> p (a b)")

            def pfl():
                t = pf()
                live.add(t.name)
                return t

            def rel(banks):
                for t in banks:
                    live.discard(t.name)

            def run(*gens):
                gens = list(gens)
                while gens:
                    for g_ in list(gens):
                        try:
                            next(g_)
                        except StopIteration:
                            gens.remove(g_)

            def head(rd):
                p = rd % 2
                S.dma("sp", lnp[p][:], I["lnp"][:, :, rd, :])
                S.dma("sp", s0r[p][:], I["s0T"][:, rd, :, :])
                w = wload(I["win"][:, :, WC_RD + rd * 384:WC_RD + (rd + 1) * 384], (128, 8, 384))
                for ti, dest in enumerate((Rp, Kp, Vp)):
                    inproj_shift(w, ti, 3 + rd * 3 + ti, dest)
                    yield
                mod_group(4 + 2 * rd)
                mod_group(5 + 2 * rd)
                yield
                S.copy("act", Vb[p][:], Vp[:])
                for c in range(16):
                    for h2 in range(2):
                        pr = slice(64 * h2, 64 * h2 + 64)
                        S.tr(PB[pr, c * 64:(c + 1) * 64], Vb[p][pr, c * 64:(c + 1) * 64], ident[pr, pr])
                S.copy("act", Vtm[p][:].rearrange("p c v -> p (c v)"), PB[:])
                yield
                for th in range(2):
                    pp = pf()
                    S.mm(pp[:], gupb[:, rd * 128:(rd + 1) * 128], sgd[:, th * 512:(th + 1) * 512])
                    S.copy("act", gfm[p][:, th * 512:(th + 1) * 512], pp[:])
                yield

            pbi = [0]

            def prep(rd, half):
                hs = slice(half * 512, half * 512 + 512)
                S.act(sqk[:], Kp[:, hs], AF.Square, scale=pcol(PV_KK + rd))
                pp = pf()
                S.mm(pp[:], bones, sqk[:])
                S.act(E0[:], pp[:], AF.Ln, bias=L2_EPS)
                S.act(E0[:], E0[:], AF.Exp, scale=-0.5)
                S.stt(kkn[:], Kp[:, hs], pcol(PV_KK + rd), E0[:], ALU.mult, ALU.mult)
                yield
                for d in range(2):
                    ARd, BKd, AAd, ABd, KTMd = AR[half][d], BK[half][d], AA[half][d], AB[half][d], KTM[half][d]
                    pd = slice(64 * d, 64 * d + 64)
                    pp = pf()
                    S.mm(pp[:], wupb[pd, rd * 128:(rd + 1) * 128], twd[pd, hs])
                    S.act(LW[:], pp[:], AF.Tanh, scale=0.5, bias=dcol(DV_HW0 + d * 4 + rd))
                    S.act(LW[:], LW[:], AF.Identity, scale=LWC, bias=LWC)
                    S.scan(C[:], smask[:], LW[:], 0.0, ALU.mult, ALU.add)
                    pp = pf()
                    S.mm(pp[:], aupb[pd, rd * 128:(rd + 1) * 128], adb[pd, hs])
                    S.act(IC[:], pp[:], AF.Tanh, scale=0.5, bias=dcol(DV_HA0 + d * 4 + rd))
                    yield
                    S.act(Dx[:], IC[:], AF.Identity, scale=dcol(DV_HKA + rd), bias=dcol(DV_OMHKA + rd))
                    S.act(IC[:], IC[:], AF.Identity, scale=0.5, bias=0.5)
                    S.tt("dve", KD[:], Kp[:, hs], Dx[:], ALU.mult)
                    S.stt(PBh[d][:], Rp[:, hs], dcol(DV_RK05 + rd), KD[:], ALU.mult, ALU.mult)
                    S.tt("dve", IC[:], kkn[:], IC[:], ALU.mult)
                    yield
                    tot = v3(C[:])[:, :, 63]
                    if d == 0:
                        S.tt("dve", Dx[:], C[:], LW[:], ALU.subtract)
                        logW = C
                    else:
                        S.tt("dve", v3(Dx[:]), tot.unsqueeze(2).to_broadcast([128, 8, 64]), v3(C[:]), ALU.subtract)
                        S.tt("dve", LW[:], Dx[:], LW[:], ALU.add)
                        logW = LW
                    S.act(wend[:, d, half * 8:(half + 1) * 8], tot, AF.Exp)
                    S.act(E0[:], logW[:], AF.Exp)
                    S.tt("dve", ARd[:, :, 1, :], v3(Rp[:, hs]), v3(E0[:]), ALU.mult)
                    S.act(E0[:], Dx[:], AF.Exp)
                    S.stt(ARd[:, :, 0, :], v3(kkn[:]), -1.0, v3(E0[:]), ALU.mult, ALU.mult)
                    yield
                    S.act(E0[:], logW[:], AF.Exp, scale=-1.0)
                    S.tt("dve", BKd[:, :, 0, :], v3(IC[:]), v3(E0[:]), ALU.mult)
                    S.tt("dve", BKd[:, :, 1, :], v3(KD[:]), v3(E0[:]), ALU.mult)
                    yield
                    for src, dst, eng in ((ARd[:, :, 0, :], AAd[:, :, 0, :], "act"),
                                          (BKd[:, :, 0, :], ABd[:, :, 1, :], "act"),
                                          (BKd[:, :, 1, :], KTMd[:], "act")):
                        PBx = PBs[pbi[0] % 2]
                        pbi[0] += 1
                        for c in range(8):
                            for h2 in range(2):
                                pr = slice(64 * h2, 64 * h2 + 64)
                                S.tr(PBx[pr, c * 64:(c + 1) * 64], src[pr, c, :], ident[pr, pr])
                        S.copy(eng, dst, v3(PBx[:, 0:512]))
                        yield
                pbn = pf()
                for c in range(8):
                    for h2 in range(2):
                        pr = slice(64 * h2, 64 * h2 + 64)
                        for d in range(2):
                            S.mm(pbn[pr, c:c + 1], PBh[d][pr, c * 64:(c + 1) * 64], ones_b[pr, 0:1],
                                 start=(d == 0), stop=(d == 1))
                S.copy("act", bon[rd % 2][:, half * 8:(half + 1) * 8], pbn[:, 0:8])
                yield

            def chunk_pipe(d, half):
                ms, mi, mT = mstrict[d], mincl[d], mstrictT[d]
                ARd, BKd, AAd, ABd, KTMd = AR[half][d], BK[half][d], AA[half][d], AB[half][d], KTM[half][d]
                PNs, NTs, AVs = PN[half][d], NT[half][d], AV[half][d]
                g8 = slice(half * 8, half * 8 + 8)

                def stg(lhs, rhs, wide):
                    banks = [pfl(), pfl()] if wide else [pfl()]
                    for c in range(8):
                        for h2 in range(2):
                            pr = slice(64 * h2, 64 * h2 + 64)
                            if wide:
                                o = banks[c // 4][pr, (c % 4) * 128:(c % 4 + 1) * 128]
                            else:
                                o = banks[0][pr, c * 64:(c + 1) * 64]
                            S.mm(o, lhs(pr, c), rhs(pr, c))
                    return banks

                o = stg(lambda pr, c: BKd[pr, c, 0, :], lambda pr, c: f128(ARd[pr, c, :, :]), True)
                yield "mm"
                for b in range(2):
                    bs = slice(b * 4, b * 4 + 4)
                    S.tt("dve", PNs[:, bs, 1, :], pv4(o[b])[:, :, 0:64], bc4(ms), ALU.mult)
                    S.tt("dve", ABd[:, bs, 0, :], pv4(o[b])[:, :, 64:128], bc4(mi), ALU.mult)
                rel(o)
                S.tt("pool", PNs[:, :, 0, :], PNs[:, :, 1, :], bc8(eye64), ALU.add)
                yield "ev"
                o = stg(lambda pr, c: BKd[pr, c, 1, :], lambda pr, c: ARd[pr, c, 1, :], False)
                yield "mm"
                S.tt("dve", Avv[:, d, g8, :], v3(o[0][:]), bc8(mi), ALU.mult)
                rel(o)
                yield "ev"
                o = stg(lambda pr, c: ARd[pr, c, 0, :], lambda pr, c: f128(BKd[pr, c, :, :]), True)
                yield "mm"
                for b in range(2):
                    bs = slice(b * 4, b * 4 + 4)
                    S.tt("dve", NTs[:, bs, :], pv4(o[b])[:, :, 0:64], bc4(mT), ALU.mult)
                    S.tt("dve", AAd[:, bs, 1, :], pv4(o[b])[:, :, 64:128], bc4(mT), ALU.mult)
                rel(o)
                yield "ev"
                oa = stg(lambda pr, c: NTs[pr, c, :], lambda pr, c: PNs[pr, c, 1, :], False)
                ob = stg(lambda pr, c: PNs[pr, c, 1, :], lambda pr, c: NTs[pr, c, :], False)
                yield "mm"
                S.copy("act", PNs[:, :, 1, :], v3(oa[0][:]))
                S.copy("act", NTs[:], v3(ob[0][:]))
                rel(oa + ob)
                yield "ev"
                for step in range(4):
                    oa = stg(lambda pr, c: NTs[pr, c, :], lambda pr, c: f128(PNs[pr, c, :, :]), True)
                    ob = stg(lambda pr, c: PNs[pr, c, 1, :], lambda pr, c: NTs[pr, c, :], False)
                    yield "mm"
                    S.copy("act", NTs[:], v3(ob[0][:]))
                    S.tt("dve", PNs[:, 0:4, 0, :], pv4(oa[0])[:, :, 0:64], PNs[:, 0:4, 0, :], ALU.add)
                    S.copy("act", PNs[:, 0:4, 1, :], pv4(oa[0])[:, :, 64:128])
                    S.tt("dve", PNs[:, 4:8, 0, :], pv4(oa[1])[:, :, 0:64], PNs[:, 4:8, 0, :], ALU.add)
                    S.copy("act", PNs[:, 4:8, 1, :], pv4(oa[1])[:, :, 64:128])
                    rel(oa + ob)
                    yield "ev"
                oa = stg(lambda pr, c: NTs[pr, c, :], lambda pr, c: PNs[pr, c, 0, :], False)
                yield "mm"
                S.tt("dve", PNs[:, :, 0, :], v3(oa[0][:]), PNs[:, :, 0, :], ALU.add)
                rel(oa)
                yield "ev"
                o = stg(lambda pr, c: PNs[pr, c, 0, :], lambda pr, c: f128(AAd[pr, c, :, :]), True)
                yield "mm"
                for b in range(2):
                    bs = slice(b * 4, b * 4 + 4)
                    S.copy("act", AVs[:, bs, :, :].rearrange("p c a b -> p c (a b)"), pv4(o[b]))
                rel(o)
                yield "ev"
                o = stg(lambda pr, c: AVs[pr, c, 0, :], lambda pr, c: f128(ABd[pr, c, :, :]), True)
                yield "mm"
                for b in range(2):
                    bs = slice(b * 4, b * 4 + 4)
                    gs_ = slice(half * 8 + b * 4, half * 8 + b * 4 + 4)
                    S.tt("dve", Rh[:, d, gs_, :], pv4(o[b])[:, :, 0:64], ARd[:, bs, 1, :], ALU.add)
                    S.tt("dve", G0[:, d, gs_, :], pv4(o[b])[:, :, 64:128], bc4(eye64), ALU.add)
                rel(o)
                yield "ev"
                o = stg(lambda pr, c: AVs[pr, c, 1, :], lambda pr, c: f128(ABd[pr, c, :, :]), True)
                yield "mm"
                for b in range(2):
                    bs = slice(b * 4, b * 4 + 4)
                    gs_ = slice(half * 8 + b * 4, half * 8 + b * 4 + 4)
                    S.tt("dve", Avv[:, d, gs_, :], pv4(o[b])[:, :, 0:64], Avv[:, d, gs_, :], ALU.add)
                    S.tt("dve", H0[:, d, gs_, :], pv4(o[b])[:, :, 64:128], KTMd[:, bs, :], ALU.add)
                rel(o)

            def run_pipes(gens):
                idle = list(range(len(gens)))
                holding = []
                while idle or holding:
                    while len(holding) < 2 and idle:
                        i = idle.pop(0)
                        tag = next(gens[i])
                        assert tag == "mm"
                        holding.append(i)
                    i = holding.pop(0)
                    try:
                        tag = next(gens[i])
                        assert tag == "ev"
                        idle.append(i)
                    except StopIteration:
                        pass

            def tail(rd):
                p = rd % 2
                Vt = Vtm[p]
                S.copy("dve", wendc[:], wend[:])
                S.copy("dve", wk[:], wend[:])
                S.ts("dve", wk[:, 0, 3:12:4], wendc[:, 0, 3:12:4], fl[:, 2:3], ALU.mult)
                S.ts("dve", wk[:, 1, 4:13:4], wendc[:, 1, 4:13:4], fl[:, 2:3], ALU.mult)
                S.copy("dve", Sall[0][0][:], s0r[p][:, 0, :])
                S.copy("dve", Sall[1][16][:], s0r[p][:, 1, :])
                yield
                pys = (pfl(), pfl())

                def ymm(c):
                    for h2 in range(2):
                        pr = slice(64 * h2, 64 * h2 + 64)
                        o = pys[c // 8][pr, (c % 8) * 64:(c % 8 + 1) * 64]
                        S.mm(o, Avv[pr, 0, c, :], Vt[pr, c, :], start=True, stop=False)
                        S.mm(o, Avv[pr, 1, c, :], Vt[pr, c, :], start=False, stop=False)
                        S.mm(o, Rh[pr, 0, c, :], Sall[0][c][pr, :], start=False, stop=False)
                        S.mm(o, Rh[pr, 1, c, :], Sall[1][c + 1][pr, :], start=False, stop=True)

                for step in range(16):
                    cf, cbk = step, 15 - step
                    pcs = (pfl(), pfl())
                    for d, c, sin in ((0, cf, cf), (1, cbk, cbk + 1)):
                        for h2 in range(2):
                            pr = slice(64 * h2, 64 * h2 + 64)
                            S.mm(pcs[d][pr, 0:64], H0[pr, d, c, :], Vt[pr, c, :], start=True, stop=False)
                            S.mm(pcs[d][pr, 0:64], G0[pr, d, c, :], Sall[d][sin][pr, :], start=False, stop=True)
                    if step >= 8:
                        ymm(15 - step)
                        ymm(step)
                    yield
                    S.act(Sall[0][cf + 1][:], pcs[0][:, 0:64], AF.Identity, scale=wk[:, 0, cf:cf + 1])
                    S.ts("dve", Sall[1][cbk][:], pcs[1][:, 0:64], wk[:, 1, cbk:cbk + 1], ALU.mult)
                    if cf % 4 == 3:
                        S.act(SF[:, 0, cf // 4, :], pcs[0][:, 0:64], AF.Identity, scale=wendc[:, 0, cf:cf + 1])
                        S.ts("dve", SF[:, 1, cbk // 4, :], pcs[1][:, 0:64], wendc[:, 1, cbk:cbk + 1], ALU.mult)
                    rel(list(pcs))
                S.dma("sp", O["oS"][:, rd, :, :, :], SF[:])
                for b in range(2):
                    S.copy("act", Yf[:, b * 8:(b + 1) * 8, :], v3(pys[b][:]))
                rel(list(pys))
                yield
                S.reduce(st4[:, 0, :], Yf[:], ALU.add)
                S.tt("dve", Yq[:], Yf[:], Yf[:], ALU.mult)
                S.reduce(st4[:, 1, :], Yq[:], ALU.add)
                yield
                S.ts("dve", st4[:, 0, :], st4[:, 0, :], 1.0 / 64, ALU.mult)
                S.tt("dve", st4[:, 2, :], st4[:, 0, :], st4[:, 0, :], ALU.mult)
                S.stt(st4[:, 3, :], st4[:, 1, :], 1.0 / 64, st4[:, 2, :], ALU.mult, ALU.subtract)
                S.act(st4[:, 3, :], st4[:, 3, :], AF.Ln, bias=GN_EPS)
                S.act(st4[:, 3, :], st4[:, 3, :], AF.Exp, scale=-0.5)
                b16 = lambda ap: ap.unsqueeze(2).to_broadcast([128, 16, 64])
                g16 = lambda ap: ap.unsqueeze(1).to_broadcast([128, 16, 64])
                S.tt("dve", Yq[:], Yf[:], b16(st4[:, 0, :]), ALU.subtract)
                yield
                S.tt("dve", Yq[:], Yq[:], b16(st4[:, 3, :]), ALU.mult)
                S.tt("dve", Yq[:], Yq[:], g16(lnp[p][:, 0, :]), ALU.mult)
                yield
                S.tt("dve", Yq[:], Yq[:], g16(lnp[p][:, 1, :]), ALU.add)
                S.tt("dve", Yf[:], Vt[:], b16(bon[p][:]), ALU.mult)
                yield
                S.tt("dve", otm[:], Yq[:], Yf[:], ALU.add)
                for c in range(16):
                    for h2 in range(2):
                        pr = slice(64 * h2, 64 * h2 + 64)
                        S.tr(PBs[1][pr, c * 64:(c + 1) * 64], otm[pr, c, :], ident[pr, pr])
                S.tt("dve", oT_rw[:, rd, :], PBs[1][:], gfm[p][:], ALU.mult)
                yield

            def nxt(r):
                yield from head(r)
                yield from prep(r, 0)
                yield from prep(r, 1)

            run(nxt(0))
            for rd in range(4):
                run_pipes([chunk_pipe(0, 0), chunk_pipe(1, 0), chunk_pipe(0, 1), chunk_pipe(1, 1)])
                if rd < 3:
                    t_, n_ = tail(rd), nxt(rd + 1)
                    alive = True
                    while alive:
                        alive = False
                        for g_, k_ in ((t_, 1), (n_, 2)):
                            for _ in range(k_):
                                try:
                                    next(g_)
                                    alive = True
                                except StopIteration:
                                    break
                else:
                    run(tail(rd))
            dbgdump("oTrw", oT_rw[:], (128, 4, T), BF16)
            prefetch(I["wor"][:], (128, 4, D))
            prefetch(I["won"][:], (128, 4, D))
            prefetch(I["win"][:, :, WC_GRW:WC_GRW + 512], (128, 8, 512))
            prefetch(I["win"][:, :, WC_GNA:WC_GNA + 512], (128, 8, 512))
            S.barrier()
            S.flush()
            if stop == "RW":
                return nc, dbg_out

        with ExitStack() as phx, ExitStack() as ph:
            xT = sbt(phx, "xTm", (128, 8, T), F32)
            mrg = sbt(ph, "mrg", (128, 8, T), BF16)
            tg = [sbt(ph, "tg%d" % i, (128, 512), F32) for i in range(4)]
            t1 = [sbt(ph, "t1%d" % i, (128, 512), F32) for i in range(2)]
            tmpA = sbt(ph, "tmpB", (128, 2, T), F32)
            sqb = sbt(ph, "sqb2", (128, 2, T), BF16)
            rstd = sbt(ph, "rstd2", (128, T), F32)
            S.dma("sp", xT[:], I["xT"][:])
            S.ts("dve", dv[:, DV_G1H:DV_G1H + 8], mod[:, 16:24], 0.5, ALU.mult)
            S.stt(dv[:, DV_GS2:DV_GS2 + 8], mod[:, 32:40], 1.0, pv[:, PV_N2:PV_N2 + 8], ALU.add, ALU.mult)
            ti = 0
            for g in range(2):
                wor = wload(I["wor"][:], (128, 4, D))
                won = wload(I["won"][:], (128, 4, D))
                wgr = wload(I["win"][:, :, WC_GRW + g * 512:WC_GRW + (g + 1) * 512], (128, 8, 512))
                wgn = wload(I["win"][:, :, WC_GNA + g * 512:WC_GNA + (g + 1) * 512], (128, 8, 512))
                for j4 in range(4):
                    j = g * 4 + j4
                    for th in range(2):
                        ts_ = slice(th * 512, th * 512 + 512)
                        pgr, pgn, pa, pb_ = pf(), pf(), pf(), pf()
                        for kc in range(8):
                            S.mm(pgr[:], wgr[:, kc, j4 * 128:(j4 + 1) * 128], hT[:, kc, ts_], start=(kc == 0), stop=(kc == 7))
                        for kc in range(8):
                            S.mm(pgn[:], wgn[:, kc, j4 * 128:(j4 + 1) * 128], hT[:, kc, ts_], start=(kc == 0), stop=(kc == 7))
                        for kc in range(4):
                            S.mm(pa[:], wor[:, kc, j * 128:(j + 1) * 128], oT_rw[:, kc, ts_], start=(kc == 0), stop=(kc == 3))
                        for kc in range(4):
                            S.mm(pb_[:], won[:, kc, j * 128:(j + 1) * 128], oT_na[:, kc, ts_], start=(kc == 0), stop=(kc == 3))
                        ta, tb_ = tg[ti % 4], tg[(ti + 1) % 4]
                        tt1 = t1[(ti // 2) % 2]
                        ti += 2
                        S.act(ta[:], pgr[:], AF.Tanh, scale=0.5)
                        S.act(tb_[:], pgn[:], AF.Tanh, scale=0.5)
                        S.stt(tt1[:], ta[:], 1.0, pa[:], ALU.add, ALU.mult)
                        S.stt(ta[:], tb_[:], 1.0, pb_[:], ALU.add, ALU.mult)
                        S.tt("dve", mrg[:, j, ts_], tt1[:], ta[:], ALU.add)
            dbgdump("mrg", mrg[:], (128, 8, T), BF16)
            for g in range(2):
                wo = wload(I["wout"][:, :, g * 512:(g + 1) * 512], (128, 8, 512))
                for j4 in range(4):
                    j = g * 4 + j4
                    for th in range(2):
                        ts_ = slice(th * 512, th * 512 + 512)
                        pp = pf()
                        for kc in range(8):
                            S.mm(pp[:], wo[:, kc, j4 * 128:(j4 + 1) * 128], mrg[:, kc, ts_], start=(kc == 0), stop=(kc == 7))
                        S.stt(xT[:, j, ts_], pp[:], dcol(DV_G1H + j), xT[:, j, ts_], ALU.mult, ALU.add)
            dbgdump("x1", xT[:], (128, 8, T))
            prefetch(I["w1"][:, :, 0:512], (128, 8, 512))
            prefetch(I["w3"][:, :, 0:512], (128, 8, 512))
            rmsnorm_mod(xT, DV_GS2, 24, tmpA, sqb, rstd)
            S.barrier()
            S.flush()
            ph.close()

            with ExitStack() as ph2:
                actb = sbt(ph2, "actb", (128, NFT, T), BF16)
                sl = [sbt(ph2, "sl%d" % i, (128, 512), F32) for i in range(3)]
                yo = [sbt(ph2, "yo%d" % i, (128, 512), F32) for i in range(2)]
                ti = 0
                for g in range(6):
                    nt = min(4, NFT - g * 4)
                    wa = wload(I["w1"][:, :, g * 512:g * 512 + nt * 128], (128, 8, nt * 128))
                    wb = wload(I["w3"][:, :, g * 512:g * 512 + nt * 128], (128, 8, nt * 128))
                    for f4 in range(nt):
                        f = g * 4 + f4
                        for th in range(2):
                            ts_ = slice(th * 512, th * 512 + 512)
                            pa, pb_ = pf(), pf()
                            for kc in range(8):
                                S.mm(pa[:], wa[:, kc, f4 * 128:(f4 + 1) * 128], hT[:, kc, ts_], start=(kc == 0), stop=(kc == 7))
                            for kc in range(8):
                                S.mm(pb_[:], wb[:, kc, f4 * 128:(f4 + 1) * 128], hT[:, kc, ts_], start=(kc == 0), stop=(kc == 7))
                            s_ = sl[ti % 3]
                            ti += 1
                            S.act(s_[:], pa[:], AF.Silu)
                            S.tt("dve", actb[:, f, ts_], s_[:], pb_[:], ALU.mult)
                oi = 0
                for j in range(8):
                    w2 = wload(I["w2"][:, :, j * 128:(j + 1) * 128], (128, NFT, 128))
                    for th in range(2):
                        ts_ = slice(th * 512, th * 512 + 512)
                        pp = pf()
                        for f in range(NFT):
                            S.mm(pp[:], w2[:, f, :], actb[:, f, ts_], start=(f == 0), stop=(f == NFT - 1))
                        y_ = yo[oi % 2]
                        oi += 1
                        S.stt(y_[:], pp[:], mod[:, 40 + j:41 + j], xT[:, j, ts_], ALU.mult, ALU.add)
                        S.dma("sp", O["yT"][:, j, ts_], y_[:])
                S.barrier()
                S.flush()
    return nc, dbg_out


def _fm(w):
    K, N = w.shape
    return np.ascontiguousarray(w.reshape(K // 128, 128, N).transpose(1, 0, 2))


def _consts():
    cst = np.zeros((128, NCB), np.float32)
    p = np.arange(128)
    cst[:, CB_ID:CB_ID + 128] = np.eye(128, dtype=np.float32)
    cst[:, CB_ONE:CB_ONE + 128] = 1.0
    cst[:, CB_BO:CB_BO + 128] = (p[:, None] // 64 == p[None, :] // 64)
    i = (p % 64)[:, None]
    f = np.arange(64)[None, :]
    cst[:, CB_LT:CB_LT + 64] = i < f
    cst[:, CB_LE:CB_LE + 64] = i <= f
    cst[:, CB_GT:CB_GT + 64] = i > f
    cst[:, CB_GE:CB_GE + 64] = i >= f
    cst[:, CB_I64:CB_I64 + 64] = i == f
    sm = np.ones((128, T), np.float32)
    sm[:, ::64] = 0.0
    return cst, sm


def _bias_tables(na_rpb):
    GW, ROWS, NR, NCOL = 64, 16, 8, 16
    t = np.arange(T)
    row, col = t // GW, t % GW
    row_start = np.clip(np.arange(ROWS) - NR // 2, 0, ROWS - NR)
    win_start = np.clip(np.arange(GW) - NCOL // 2, 0, GW - NCOL)
    qr, qc = row[:, None], col[:, None]
    kr, kc = row[None, :], col[None, :]
    valid = (kr >= row_start[qr]) & (kr < row_start[qr] + NR) & (kc >= win_start[qc]) & (kc < win_start[qc] + NCOL)
    dr = np.clip(kr - qr + NR - 1, 0, 2 * NR - 2)
    dc = np.clip(kc - qc, -(NCOL - 1), NCOL - 1) + NCOL - 1
    Bs = np.where(valid[None], na_rpb[:, dr, dc], np.float32(NEGM)).astype(np.float32)
    same = (t[:, None] // 256) == (t[None, :] // 256)
    Bp = np.where(same, np.float32(0.0), np.float32(NEGM)).astype(np.float32)[None]

    def tile(B):
        out = np.empty((B.shape[0], 128, NBLK, 128), np.float32)
        for (j, R), b in BLK.items():
            out[:, :, b, :] = B[:, R * 128:(R + 1) * 128, j * 128:(j + 1) * 128].transpose(0, 2, 1)
        return out
    ts_ = tile(Bs)
    tp = np.ascontiguousarray(np.broadcast_to(tile(Bp), (8, 128, NBLK, 128)))
    return tp, ts_


_CACHE = {}


def _get_nc():
    if "nc" not in _CACHE:
        _CACHE["nc"] = build_nc()[0]
    return _CACHE["nc"]


def _prep(inp):
    f = lambda a: np.ascontiguousarray(np.asarray(a, dtype=np.float32))
    w_in = f(inp["w_in"])[0]
    perm = list(range(1536, 1920))
    for rd in range(4):
        for ti in range(3):
            perm += list(range(ti * 512 + rd * 128, ti * 512 + rd * 128 + 128))
    perm += list(range(1920, 5504))
    perm = np.array(perm)
    otile = [12, 13, 14] + [ti * 4 + rd for rd in range(4) for ti in range(3)]
    mu = f(inp["shift_mu"])[0]
    pv = np.zeros((128, NPV), np.float32)
    col = lambda v: np.ascontiguousarray(v.reshape(-1, 128).T)
    pv[:, PV_N1:PV_N1 + 8] = col(f(inp["norm1_g"])[0])
    pv[:, PV_N2:PV_N2 + 8] = col(f(inp["norm2_g"])[0])
    pv[:, PV_BADA:PV_BADA + 48] = col(f(inp["b_ada"])[0])
    pv[:, PV_MU0:PV_MU0 + 15] = col(mu[0])[:, otile]
    pv[:, PV_MU1:PV_MU1 + 15] = col(mu[1])[:, otile]
    pv[:, PV_KK:PV_KK + 4] = col(f(inp["rw_k_k"])[0])
    pv[:, PV_KA:PV_KA + 4] = col(f(inp["rw_k_a"])[0])
    pv[:, PV_RK:PV_RK + 4] = col(f(inp["rw_r_k"])[0].reshape(-1))
    pv[:, PV_W0:PV_W0 + 4] = col(f(inp["rw_w0"])[0, 0])
    pv[:, PV_W0 + 4:PV_W0 + 8] = col(f(inp["rw_w0"])[0, 1])
    pv[:, PV_A0:PV_A0 + 4] = col(f(inp["rw_a0"])[0, 0])
    pv[:, PV_A0 + 4:PV_A0 + 8] = col(f(inp["rw_a0"])[0, 1])
    pv[:, PV_QG] = np.tile(f(inp["na_q_g"])[0], 2)
    pv[:, PV_KG] = np.tile(f(inp["na_k_g"])[0], 2)
    lnp = np.stack([f(inp["rw_ln_g"])[0].reshape(4, 2, 64), f(inp["rw_ln_b"])[0].reshape(4, 2, 64)])
    lnp = np.ascontiguousarray(np.repeat(lnp.transpose(2, 0, 1, 3), 64, axis=0))
    cst, sm = _consts()
    tp, ts_ = _bias_tables(f(inp["na_rpb"])[0])
    common = {
        "cstb": cst, "smask": sm, "pv": pv, "lnp": lnp,
        "wada": _fm(f(inp["w_ada"])[0]), "win": _fm(w_in[:, perm]),
        "wup": f(inp["rw_w_up"])[0].reshape(128, 512), "aup": f(inp["rw_a_up"])[0].reshape(128, 512),
        "gup": f(inp["rw_g_up"])[0], "wor": _fm(f(inp["w_o_rwkv"])[0]), "won": _fm(f(inp["w_o_na"])[0]),
        "wout": _fm(f(inp["w_out"])[0]), "w1": _fm(f(inp["ffn_w1"])[0]), "w3": _fm(f(inp["ffn_w3"])[0]),
        "w2": _fm(f(inp["ffn_w2"])[0]),
    }
    xp, xs = f(inp["x_prompt"]), f(inp["x_sample"])
    st, ck, cv_ = f(inp["state_rwkv"]), f(inp["cache_na_k"]), f(inp["cache_na_v"])
    c, cctx = f(inp["c"]), f(inp["c_ctx"])
    maps = []
    for u in range(8):
        m = dict(common)
        if u < 4:
            X = xp[4 * u:4 * u + 4].reshape(T, D)
            cvec = cctx
            m["fl"] = np.tile(np.array([1.0, NEGM, 0.0, 0.0], np.float32), (128, 1))
            m["s0T"] = np.zeros((128, 4, 2, 64), np.float32)
            m["ckT"] = np.zeros((128, 4, 256), np.float32)
            m["cvv"] = np.zeros((128, 2, 512), np.float32)
            m["tbl"] = tp
        else:
            b = (u - 4) % 2
            X = xs[b]
            cvec = c[b]
            m["fl"] = np.tile(np.array([0.0, 0.0, 1.0, 0.0], np.float32), (128, 1))
            s = st[b, 0].reshape(2, 4, 2, 64, 64)
            m["s0T"] = np.ascontiguousarray(s.transpose(2, 4, 1, 0, 3).reshape(128, 4, 2, 64))
            kk_ = ck[b, 0].reshape(256, 4, 2, 64)
            m["ckT"] = np.ascontiguousarray(kk_.transpose(2, 3, 1, 0).reshape(128, 4, 256))
            m["cvv"] = np.ascontiguousarray(cv_[b, 0].reshape(2, 128, 512).transpose(1, 0, 2))
            m["tbl"] = ts_
        m["xT"] = _fm(np.ascontiguousarray(X.T))
        m["cv"] = np.ascontiguousarray(cvec.reshape(8, 128).T)
        maps.append(m)
    return maps


def _unfm(a):
    return a.transpose(1, 0, 2).reshape(-1, a.shape[2])


def kernel(**inp):
    maps = _prep(inp)
    nc = _get_nc()
    res = run_bass_kernel_spmd(nc, maps, core_ids=list(range(8)))
    R = res.results
    y_p = np.empty((16, 256, D), np.float32)
    y_s = np.empty((2, 1024, D), np.float32)
    s_new = np.empty((16, 1, 2, 8, 64, 64), np.float32)
    k_new = np.empty((16, 1, 256, 8, 64), np.float32)
    v_new = np.empty((16, 1, 256, 8, 64), np.float32)
    for u in range(6):
        Y = _unfm(np.asarray(R[u]["yT"])).T
        if u < 4:
            y_p[4 * u:4 * u + 4] = Y.reshape(4, 256, D)
            oS = np.asarray(R[u]["oS"]).reshape(2, 64, 4, 2, 4, 64)
            s_new[4 * u:4 * u + 4, 0] = oS.transpose(4, 3, 2, 0, 5, 1).reshape(4, 2, 8, 64, 64)
            okT = np.asarray(R[u]["okT"]).reshape(2, 64, 4, T)
            k_new[4 * u:4 * u + 4, 0] = okT.transpose(3, 2, 0, 1).reshape(4, 256, 8, 64)
            ov = np.asarray(R[u]["ov"]).transpose(1, 0, 2).reshape(T, 512)
            v_new[4 * u:4 * u + 4, 0] = ov.reshape(4, 256, 8, 64)
        else:
            y_s[u - 4] = Y
    return y_p, y_s, s_new, k_new, v_new
```
